# Optimizing a Trainium2 kernel written in Bass

```python
import math
import jax, jax.numpy as jnp
from jax import lax
import numpy as np

D_MODEL = 2048
BATCH = 4
SEQ = 2048
DEPTH = 2
DEC_BATCH = 128
DEC_SEQ = 1
PAST_LEN = 16384
PAGE_SIZE = 128

RET_HEADS = 4
RET_DK = 128
RET_DV = 128
RET_QK = RET_HEADS * RET_DK
RET_W = RET_HEADS * RET_DV
RET_CHUNK = 128
ROPE_BASE = 10000.0
GDN_HEADS = 8
GDN_DK = 128
GDN_DV = 128
GDN_QK = GDN_HEADS * GDN_DK
GDN_W = GDN_HEADS * GDN_DV
GDN_CHUNK = 64
CONV_W = 4
GDN_CONV_C = 2 * GDN_QK + GDN_W
SSM_W = D_MODEL - RET_W - GDN_W
SSM_GROUP = 16
SSM_GROUPS = SSM_W // SSM_GROUP
SSM_N = 64
MIX_W = RET_W + GDN_W + SSM_W
IN_SIZES = (RET_QK, RET_QK, RET_W, RET_W, GDN_QK, GDN_QK, GDN_W, GDN_W, GDN_HEADS, GDN_HEADS, SSM_W)
IN_COLS = sum(IN_SIZES)
D_FF = -(-8 * D_MODEL // (3 * 256)) * 256
ALPHA = (2 * DEPTH) ** 0.25
BETA = (8 * DEPTH) ** -0.25
LN_EPS = 1e-5
RMS_EPS = 1e-6
L2_EPS = 1e-6

kernel_name = 'hybrid_ret_gdn_s5_adaln_deepnorm_step'


def _layernorm(x, w, b):
    xf = x.astype(jnp.float32)
    mu = xf.mean(-1, keepdims=True)
    var = jnp.square(xf - mu).mean(-1, keepdims=True)
    return ((xf - mu) * lax.rsqrt(var + LN_EPS) * w + b).astype(x.dtype)


def _l2norm(x):
    return x * lax.rsqrt(jnp.sum(x * x, -1, keepdims=True) + L2_EPS)


def _rope(x, pos):
    half = x.shape[-1] // 2
    freq = ROPE_BASE ** (-jnp.arange(half, dtype=jnp.float32) / half)
    ang = pos.astype(jnp.float32)[:, None] * freq[None, :]
    cos = jnp.cos(ang)[None, :, None, :]
    sin = jnp.sin(ang)[None, :, None, :]
    x1, x2 = x[..., :half], x[..., half:]
    return jnp.concatenate([x1 * cos - x2 * sin, x1 * sin + x2 * cos], axis=-1)


def _blocks(t, n):
    B, L, H = t.shape[:3]
    t = t.reshape((B, n, L // n, H) + t.shape[3:])
    return jnp.moveaxis(t, (1, 3), (0, 2))


def _unblocks(t):
    n, B, H, C = t.shape[:4]
    return jnp.moveaxis(t, (0, 2), (1, 3)).reshape((B, n * C, H) + t.shape[4:])


def _retention(q, k, v, s0):
    B, L, H, dk = q.shape
    C = math.gcd(L, RET_CHUNK)
    n = L // C
    log_g = jnp.log1p(-jnp.power(2.0, -5.0 - jnp.arange(H, dtype=jnp.float32)))
    idx = jnp.arange(C, dtype=jnp.float32)
    diff = idx[:, None] - idx[None, :]
    dmask = jnp.exp(jnp.where(diff >= 0, log_g[:, None, None] * diff, -jnp.inf))
    qc, kc, vc = _blocks(q, n), _blocks(k, n), _blocks(v, n)
    scores = jnp.einsum('nbhid,nbhjd->nbhij', qc, kc) * dmask
    intra = jnp.einsum('nbhij,nbhje->nbhie', scores, vc)
    k_w = kc * jnp.exp(log_g[:, None] * (C - 1 - idx))[..., None]
    kv = jnp.einsum('nbhjd,nbhje->nbhde', k_w, vc)
    g_chunk = jnp.exp(log_g * C)[:, None, None]

    def step(s, kv_i):
        return s * g_chunk + kv_i, s

    s_fin, s_prev = lax.scan(step, s0, kv)
    q_w = qc * jnp.exp(log_g[:, None] * (idx + 1))[..., None]
    cross = jnp.einsum('nbhid,nbhde->nbhie', q_w, s_prev)
    return _unblocks(intra + cross), s_fin


def _gated_delta(q, k, v, g, beta, s0):
    B, L, H, dk = q.shape
    dv = v.shape[-1]
    C = math.gcd(L, GDN_CHUNK)
    n = L // C
    qc, kc, vc = _blocks(q, n), _blocks(k, n), _blocks(v, n)
    gc = jnp.cumsum(_blocks(g, n), axis=-1)
    bc = _blocks(beta, n)[..., None]
    tri = jnp.tril(jnp.ones((C, C), dtype=bool))
    strict = jnp.tril(jnp.ones((C, C), dtype=bool), -1)
    decay = jnp.exp(jnp.where(tri, gc[..., :, None] - gc[..., None, :], -jnp.inf))
    kb = kc * bc
    a = jnp.where(strict, jnp.einsum('nbhid,nbhjd->nbhij', kb, kc) * decay, 0.0)
    m = a + jnp.eye(C, dtype=a.dtype)
    rhs = jnp.concatenate([vc * bc, kb * jnp.exp(gc)[..., None]], axis=-1)
    sol = lax.linalg.triangular_solve(m, rhs, left_side=True, lower=True, unit_diagonal=True)
    u, w = sol[..., :dv], sol[..., dv:]
    attn = jnp.where(tri, jnp.einsum('nbhid,nbhjd->nbhij', qc, kc) * decay, 0.0)
    q_dec = qc * jnp.exp(gc)[..., None]
    k_dec = kc * jnp.exp(gc[..., -1:] - gc)[..., None]
    g_last = jnp.exp(gc[..., -1])[..., None, None]

    def step(s, xs):
        u_i, w_i, a_i, qd_i, kd_i, gl_i = xs
        v_new = u_i - jnp.einsum('bhcd,bhde->bhce', w_i, s)
        o_i = jnp.einsum('bhcd,bhde->bhce', qd_i, s) + jnp.einsum('bhij,bhje->bhie', a_i, v_new)
        s = s * gl_i + jnp.einsum('bhcd,bhce->bhde', kd_i, v_new)
        return s, o_i

    s_fin, o = lax.scan(step, s0, (u, w, attn, q_dec, k_dec, g_last))
    return _unblocks(o), s_fin


def _short_conv(xin, buf, w):
    L = xin.shape[1]
    xp = jnp.concatenate([buf.astype(xin.dtype), xin], axis=1)
    y = xp[:, 0:L] * w[0]
    for i in range(1, CONV_W):
        y = y + xp[:, i:i + L] * w[i]
    return jax.nn.silu(y), xp[:, L:].astype(buf.dtype)


def _s5(u, h0_re, h0_im, a_re, a_im, log_dt, b_re, b_im, c_re, c_im, d_skip, w_glu):
    f32 = jnp.float32
    B, L, _ = u.shape
    ug = u.astype(f32).reshape(B, L, SSM_GROUPS, SSM_GROUP)
    lam = lax.complex(a_re.astype(f32), a_im.astype(f32))
    dt = jnp.exp(log_dt.astype(f32))[:, None]
    lam_bar = jnp.exp(lam * dt)
    b = lax.complex(b_re.astype(f32), b_im.astype(f32))
    b_bar = ((lam_bar - 1.0) / lam)[..., None] * b
    c = lax.complex(c_re.astype(f32), c_im.astype(f32))
    h0 = lax.complex(h0_re.astype(f32), h0_im.astype(f32))
    bu = jnp.einsum('blgp,gnp->blgn', ug, b_bar)
    bu = bu.at[:, 0].add(lam_bar[None] * h0)
    a_seq = jnp.broadcast_to(lam_bar, bu.shape)

    def comb(e1, e2):
        a1, x1 = e1
        a2, x2 = e2
        return a1 * a2, a2 * x1 + x2

    _, h = lax.associative_scan(comb, (a_seq, bu), axis=1)
    y = jnp.einsum('blgn,gpn->blgp', h, c).real + d_skip.astype(f32).reshape(SSM_GROUPS, SSM_GROUP) * ug
    y = jax.nn.gelu(y.reshape(B, L, SSM_W))
    y = y * jax.nn.sigmoid(y @ w_glu.astype(f32))
    h_last = h[:, -1]
    return y, h_last.real.astype(h0_re.dtype), h_last.imag.astype(h0_im.dtype)


def _layer(x, c, pos, state, p):
    s_ret, s_gdn, s_conv, s_re, s_im = state
    f32 = jnp.float32
    B, L, _ = x.shape
    mod = jax.nn.silu(c) @ p['w_mod'] + p['b_mod']
    sh1, sc1, g1, sh2, sc2, g2 = [m[:, None, :] for m in jnp.split(mod, 6, axis=-1)]
    h = x * (1.0 + sc1) + sh1
    proj = h @ p['w_in']
    splits = np.cumsum(IN_SIZES)[:-1].tolist()
    rq, rk, rv, rg, dq, dk_, dv_, dz, da, db, su = jnp.split(proj, splits, axis=-1)
    rq = _rope(rq.reshape(B, L, RET_HEADS, RET_DK).astype(f32), pos)
    rk = _rope(rk.reshape(B, L, RET_HEADS, RET_DK).astype(f32), pos) * RET_DK ** -0.5
    rv = rv.reshape(B, L, RET_HEADS, RET_DV).astype(f32)
    ro, s_ret_new = _retention(rq, rk, rv, s_ret.astype(f32))
    mu = ro.mean(-1, keepdims=True)
    var = jnp.square(ro - mu).mean(-1, keepdims=True)
    ro = ((ro - mu) * lax.rsqrt(var + LN_EPS)).reshape(B, L, RET_W) * p['ret_gn_w'] * jax.nn.silu(rg.astype(f32))
    conv_out, conv_new = _short_conv(jnp.concatenate([dq, dk_, dv_], axis=-1), s_conv, p['conv_w'])
    cq, ck, cv = jnp.split(conv_out.astype(f32), [GDN_QK, 2 * GDN_QK], axis=-1)
    cq = _l2norm(cq.reshape(B, L, GDN_HEADS, GDN_DK)) * GDN_DK ** -0.5
    ck = _l2norm(ck.reshape(B, L, GDN_HEADS, GDN_DK))
    cv = cv.reshape(B, L, GDN_HEADS, GDN_DV)
    g = -jnp.exp(p['gdn_a_log'].astype(f32)) * jax.nn.softplus(da.astype(f32) + p['gdn_dt_bias'].astype(f32))
    beta = jax.nn.sigmoid(db.astype(f32))
    go, s_gdn_new = _gated_delta(cq, ck, cv, g, beta, s_gdn.astype(f32))
    go = go * lax.rsqrt(jnp.mean(go * go, -1, keepdims=True) + RMS_EPS) * p['gdn_norm_w']
    go = go.reshape(B, L, GDN_W) * jax.nn.silu(dz.astype(f32))
    so, s_re_new, s_im_new = _s5(su, s_re, s_im, p['ssm_a_re'], p['ssm_a_im'], p['ssm_log_dt'],
                                 p['ssm_b_re'], p['ssm_b_im'], p['ssm_c_re'], p['ssm_c_im'],
                                 p['ssm_d'], p['ssm_w_glu'])
    mix = jnp.concatenate([ro, go, so], axis=-1).astype(x.dtype) @ p['w_out']
    x = _layernorm(ALPHA * x + g1 * mix, p['ln1_w'], p['ln1_b'])
    h = x * (1.0 + sc2) + sh2
    gate, up = jnp.split(h @ p['w_ffn_in'], 2, axis=-1)
    ff = (jax.nn.silu(gate) * up) @ p['w_ffn_out']
    x = _layernorm(ALPHA * x + g2 * ff, p['ln2_w'], p['ln2_b'])
    new_state = (s_ret_new.astype(s_ret.dtype), s_gdn_new.astype(s_gdn.dtype), conv_new, s_re_new, s_im_new)
    return x, new_state


def setup_inputs(seed: int = 0) -> dict:
    key = jax.random.key(seed)
    ks = iter(jax.random.split(key, 48))
    f32 = jnp.float32

    def nrm(shape, scale):
        return scale * jax.random.normal(next(ks), shape, f32)

    x_prompt = nrm((BATCH, SEQ, D_MODEL), 1.0)
    x_sample = nrm((DEC_BATCH, DEC_SEQ, D_MODEL), 1.0)
    c_prompt = nrm((BATCH, D_MODEL), 1.0)
    c_sample = nrm((DEC_BATCH, D_MODEL), 1.0)
    state_ret = nrm((DEPTH, DEC_BATCH, RET_HEADS, RET_DK, RET_DV), 0.1)
    state_gdn = nrm((DEPTH, DEC_BATCH, GDN_HEADS, GDN_DK, GDN_DV), 0.1)
    state_conv = nrm((DEPTH, DEC_BATCH, CONV_W - 1, GDN_CONV_C), 1.0)
    state_ssm_re = nrm((DEPTH, DEC_BATCH, SSM_GROUPS, SSM_N), 0.3)
    state_ssm_im = nrm((DEPTH, DEC_BATCH, SSM_GROUPS, SSM_N), 0.3)
    w_mod = nrm((DEPTH, D_MODEL, 6 * D_MODEL), D_MODEL ** -0.5)
    b_mod = nrm((DEPTH, 6 * D_MODEL), 0.02)
    w_in = nrm((DEPTH, D_MODEL, IN_COLS), D_MODEL ** -0.5)
    conv_w = nrm((DEPTH, CONV_W, GDN_CONV_C), CONV_W ** -0.5)
    ret_gn_w = 1.0 + nrm((DEPTH, RET_W), 0.02)
    gdn_a_log = jnp.log(jax.random.uniform(next(ks), (DEPTH, GDN_HEADS), f32, 1.0, 16.0))
    dt = jnp.exp(jax.random.uniform(next(ks), (DEPTH, GDN_HEADS), f32, math.log(1e-3), math.log(1e-1)))
    gdn_dt_bias = dt + jnp.log(-jnp.expm1(-dt))
    gdn_norm_w = 1.0 + nrm((DEPTH, GDN_DV), 0.02)
    ssm_a_re = -0.5 + nrm((DEPTH, SSM_GROUPS, SSM_N), 0.01)
    ssm_a_im = math.pi * jnp.arange(SSM_N, dtype=f32) + nrm((DEPTH, SSM_GROUPS, SSM_N), 0.01)
    ssm_log_dt = jax.random.uniform(next(ks), (DEPTH, SSM_GROUPS), f32, math.log(1e-3), math.log(1e-1))
    ssm_b_re = nrm((DEPTH, SSM_GROUPS, SSM_N, SSM_GROUP), (2 * SSM_GROUP) ** -0.5)
    ssm_b_im = nrm((DEPTH, SSM_GROUPS, SSM_N, SSM_GROUP), (2 * SSM_GROUP) ** -0.5)
    ssm_c_re = nrm((DEPTH, SSM_GROUPS, SSM_GROUP, SSM_N), 0.5 ** 0.5)
    ssm_c_im = nrm((DEPTH, SSM_GROUPS, SSM_GROUP, SSM_N), 0.5 ** 0.5)
    ssm_d = nrm((DEPTH, SSM_W), 1.0)
    ssm_w_glu = nrm((DEPTH, SSM_W, SSM_W), SSM_W ** -0.5)
    w_out = nrm((DEPTH, MIX_W, D_MODEL), BETA * MIX_W ** -0.5)
    ln1_w = 1.0 + nrm((DEPTH, D_MODEL), 0.02)
    ln1_b = nrm((DEPTH, D_MODEL), 0.02)
    w_ffn_in = nrm((DEPTH, D_MODEL, 2 * D_FF), D_MODEL ** -0.5)
    w_ffn_out = nrm((DEPTH, D_FF, D_MODEL), BETA * D_FF ** -0.5)
    ln2_w = 1.0 + nrm((DEPTH, D_MODEL), 0.02)
    ln2_b = nrm((DEPTH, D_MODEL), 0.02)
    return {'x_prompt': x_prompt, 'x_sample': x_sample, 'c_prompt': c_prompt, 'c_sample': c_sample,
            'state_ret': state_ret, 'state_gdn': state_gdn, 'state_conv': state_conv,
            'state_ssm_re': state_ssm_re, 'state_ssm_im': state_ssm_im,
            'w_mod': w_mod, 'b_mod': b_mod, 'w_in': w_in, 'conv_w': conv_w, 'ret_gn_w': ret_gn_w,
            'gdn_a_log': gdn_a_log, 'gdn_dt_bias': gdn_dt_bias, 'gdn_norm_w': gdn_norm_w,
            'ssm_a_re': ssm_a_re, 'ssm_a_im': ssm_a_im, 'ssm_log_dt': ssm_log_dt,
            'ssm_b_re': ssm_b_re, 'ssm_b_im': ssm_b_im, 'ssm_c_re': ssm_c_re, 'ssm_c_im': ssm_c_im,
            'ssm_d': ssm_d, 'ssm_w_glu': ssm_w_glu, 'w_out': w_out, 'ln1_w': ln1_w, 'ln1_b': ln1_b,
            'w_ffn_in': w_ffn_in, 'w_ffn_out': w_ffn_out, 'ln2_w': ln2_w, 'ln2_b': ln2_b}


def reference(x_prompt, x_sample, c_prompt, c_sample, state_ret, state_gdn, state_conv,
              state_ssm_re, state_ssm_im, w_mod, b_mod, w_in, conv_w, ret_gn_w, gdn_a_log,
              gdn_dt_bias, gdn_norm_w, ssm_a_re, ssm_a_im, ssm_log_dt, ssm_b_re, ssm_b_im,
              ssm_c_re, ssm_c_im, ssm_d, ssm_w_glu, w_out, ln1_w, ln1_b, w_ffn_in, w_ffn_out,
              ln2_w, ln2_b):
    Bp = x_prompt.shape[0]
    pos_p = jnp.arange(x_prompt.shape[1])
    pos_s = PAST_LEN + jnp.arange(x_sample.shape[1])
    zero_state = (jnp.zeros((Bp, RET_HEADS, RET_DK, RET_DV), state_ret.dtype),
                  jnp.zeros((Bp, GDN_HEADS, GDN_DK, GDN_DV), state_gdn.dtype),
                  jnp.zeros((Bp, CONV_W - 1, GDN_CONV_C), state_conv.dtype),
                  jnp.zeros((Bp, SSM_GROUPS, SSM_N), state_ssm_re.dtype),
                  jnp.zeros((Bp, SSM_GROUPS, SSM_N), state_ssm_im.dtype))
    yp, ys = x_prompt, x_sample
    new_p, new_s = [], []
    for l in range(DEPTH):
        p = dict(w_mod=w_mod[l], b_mod=b_mod[l], w_in=w_in[l], conv_w=conv_w[l], ret_gn_w=ret_gn_w[l],
                 gdn_a_log=gdn_a_log[l], gdn_dt_bias=gdn_dt_bias[l], gdn_norm_w=gdn_norm_w[l],
                 ssm_a_re=ssm_a_re[l], ssm_a_im=ssm_a_im[l], ssm_log_dt=ssm_log_dt[l],
                 ssm_b_re=ssm_b_re[l], ssm_b_im=ssm_b_im[l], ssm_c_re=ssm_c_re[l], ssm_c_im=ssm_c_im[l],
                 ssm_d=ssm_d[l], ssm_w_glu=ssm_w_glu[l], w_out=w_out[l], ln1_w=ln1_w[l], ln1_b=ln1_b[l],
                 w_ffn_in=w_ffn_in[l], w_ffn_out=w_ffn_out[l], ln2_w=ln2_w[l], ln2_b=ln2_b[l])
        yp, st_p = _layer(yp, c_prompt, pos_p, zero_state, p)
        past = (state_ret[l], state_gdn[l], state_conv[l], state_ssm_re[l], state_ssm_im[l])
        ys, st_s = _layer(ys, c_sample, pos_s, past, p)
        new_p.append(st_p)
        new_s.append(st_s)
    ret_p, gdn_p, conv_p, ssm_re_p, ssm_im_p = [jnp.stack(t) for t in zip(*new_p)]
    ret_s, gdn_s, conv_s, ssm_re_s, ssm_im_s = [jnp.stack(t) for t in zip(*new_s)]
    return (yp, ys, ret_p, gdn_p, conv_p, ssm_re_p, ssm_im_p, ret_s, gdn_s, conv_s, ssm_re_s, ssm_im_s)
```

```python
from contextlib import ExitStack
import math
import numpy as np
import concourse.bass as bass
import concourse.mybir as mybir
from concourse.bass_utils import run_bass_kernel_spmd

F32 = mybir.dt.float32
BF16 = mybir.dt.bfloat16
AF = mybir.ActivationFunctionType
ALU = mybir.AluOpType

ENGS = ("pe", "act", "dve", "pool", "sp")
N_DMA_SEMS = 28
N_SP_SEMS = 16


class Region:
    __slots__ = ("name", "w", "r", "arena", "excl")

    def __init__(self, name, arena=False, excl=False):
        self.name = name
        self.w = None
        self.r = []
        self.arena = arena
        self.excl = excl


class Op:
    __slots__ = ("eng", "idx", "fn", "deps", "is_dma", "dsem", "dval", "marked", "semval")

    def __init__(self, eng, idx, fn, is_dma=False):
        self.eng = eng
        self.idx = idx
        self.fn = fn
        self.deps = []
        self.is_dma = is_dma
        self.dsem = None
        self.dval = None
        self.marked = False
        self.semval = None


class Prog:
    def __init__(self):
        self.nc = bass.Bass("TRN2", target_bir_lowering=False)
        self.stack = ExitStack()
        self.ops = {e: [] for e in ENGS}
        self.known = {e: {} for e in ENGS}
        self.known_dma = {e: {} for e in ENGS}
        self.dma_cnt = [0] * N_DMA_SEMS
        self.dma_last = [None] * N_DMA_SEMS
        self.dma_rr = {"sp": 0, "pool": 0}
        self.nreg = 0
        self.out_dmas = []
        self.arena_last = {}
        self.arena_dmas = []
        self.arena_dmas_prev = []
        self.fence = []

    def sb(self, name, shape, dtype=F32):
        return self.stack.enter_context(self.nc.sbuf_tensor(name, list(shape), dtype))

    def ps(self, name, shape, dtype=F32):
        return self.stack.enter_context(self.nc.psum_tensor(name, list(shape), dtype))

    def dram(self, name, shape, dtype=F32, kind="Internal"):
        return self.nc.dram_tensor(name, list(shape), dtype, kind=kind)

    def region(self, name=None):
        self.nreg += 1
        return Region(name or f"r{self.nreg}")

    def regions(self, n, name="r", excl=False):
        rs = [self.region(f"{name}{i}") for i in range(n)]
        for r in rs:
            r.excl = excl
        return rs

    def aregion(self, name=None):
        self.nreg += 1
        r = Region(name or f"a{self.nreg}", arena=True)
        r.r = list(self.fence)
        return r

    def arena_phase(self):
        f = list(self.arena_last.values()) + list(self.arena_dmas) + list(self.arena_dmas_prev)
        self.fence = f + [o for o in self.fence if o.is_dma is False and o.eng not in self.arena_last]
        self.arena_last = {}
        self.arena_dmas_prev = self.arena_dmas
        self.arena_dmas = []

    def _add_deps(self, op, reads, writes):
        deps = []
        arena = False
        for r in reads:
            arena |= r.arena
            if r.w is not None:
                deps.append(r.w)
            if r.excl:
                deps.extend(o for o in r.r if o.eng != op.eng)
        for w in writes:
            arena |= w.arena
            if w.w is not None:
                deps.append(w.w)
            deps.extend(w.r)
        eng = op.eng
        need = {}
        for d in deps:
            if d is op:
                continue
            if d.is_dma:
                key = ("d", d.dsem)
                if need.get(key, (0, None))[0] < d.dval:
                    need[key] = (d.dval, d)
            else:
                if d.eng == eng and eng == "pe":
                    continue
                key = ("e", d.eng)
                if need.get(key, (-1, None))[0] < d.idx:
                    need[key] = (d.idx, d)
        for key, (v, d) in need.items():
            if key[0] == "d":
                if self.known_dma[eng].get(key[1], 0) >= v:
                    continue
                self.known_dma[eng][key[1]] = v
            else:
                if self.known[eng].get(key[1], -1) >= v:
                    continue
                self.known[eng][key[1]] = v
                d.marked = True
            op.deps.append(d)
        for r in reads:
            r.r.append(op)
        for w in writes:
            w.w = op
            w.r = []
        if arena:
            if op.is_dma:
                self.arena_dmas.append(op)
            else:
                self.arena_last[eng] = op

    def op(self, eng, fn, reads=(), writes=()):
        o = Op(eng, len(self.ops[eng]), fn)
        self._add_deps(o, reads, writes)
        self.ops[eng].append(o)
        return o

    def dma(self, out_ap, in_ap, reads=(), writes=(), queue="sp", is_output=False, **kw):
        if queue == "pool":
            s = N_SP_SEMS + self.dma_rr["pool"]
            self.dma_rr["pool"] = (self.dma_rr["pool"] + 1) % (N_DMA_SEMS - N_SP_SEMS)
        else:
            s = self.dma_rr["sp"]
            self.dma_rr["sp"] = (self.dma_rr["sp"] + 1) % N_SP_SEMS
        o = Op(queue, len(self.ops[queue]), None, is_dma=True)
        o.fn = lambda e, _o=out_ap, _i=in_ap, _kw=kw: e.dma_start(out=_o, in_=_i, **_kw)
        prev = self.dma_last[s]
        if prev is not None:
            if self.known_dma[queue].get(s, 0) < prev.dval:
                self.known_dma[queue][s] = prev.dval
                o.deps.append(prev)
        self.dma_cnt[s] += 1
        o.dsem = s
        o.dval = 16 * self.dma_cnt[s]
        self.dma_last[s] = o
        self._add_deps(o, reads, writes)
        self.ops[queue].append(o)
        if is_output:
            self.out_dmas.append(o)
        return o

    def finish(self):
        nc = self.nc
        fin = Op("sp", len(self.ops["sp"]), None)
        for d in self.out_dmas:
            if self.known_dma["sp"].get(d.dsem, 0) < d.dval:
                self.known_dma["sp"][d.dsem] = d.dval
                fin.deps.append(d)
        for e in ENGS:
            if e != "sp" and self.ops[e]:
                last = None
                for o in reversed(self.ops[e]):
                    if not o.is_dma and o.fn is not None:
                        last = o
                        break
                if last is not None:
                    last.marked = True
                    fin.deps.append(last)
        self.ops["sp"].append(fin)
        for e in ENGS:
            c = 0
            for o in self.ops[e]:
                if o.is_dma:
                    continue
                if o.marked:
                    c += 1
                    o.semval = c
        self.esem = {e: self.stack.enter_context(nc.semaphore(f"s_{e}")) for e in ENGS}
        self.dsems = [self.stack.enter_context(nc.semaphore(f"s_dma{i}")) for i in range(N_DMA_SEMS)]
        with nc.Block() as block:
            def emit(ename, e):
                for o in self.ops[ename]:
                    for d in o.deps:
                        if d.is_dma:
                            e.wait_ge(self.dsems[d.dsem], d.dval)
                        else:
                            e.wait_ge(self.esem[d.eng], d.semval)
                    if o.fn is None:
                        continue
                    ins = o.fn(e)
                    if o.is_dma:
                        ins.then_inc(self.dsems[o.dsem], 16)
                    elif o.marked:
                        ins.then_inc(self.esem[ename], 1)

            @block.tensor
            def _(e):
                emit("pe", e)

            @block.scalar
            def _(e):
                emit("act", e)

            @block.vector
            def _(e):
                emit("dve", e)

            @block.gpsimd
            def _(e):
                emit("pool", e)

            @block.sync
            def _(e):
                emit("sp", e)
        self.stack.close()
        return nc

    def stats(self):
        return {e: len(self.ops[e]) for e in ENGS}


def bc_last(ap, n):
    return bass.AP(ap.tensor, ap.offset, [list(x) for x in ap.ap] + [[0, n]])


def bc_mid(ap, n):
    a = [list(x) for x in ap.ap]
    return bass.AP(ap.tensor, ap.offset, [a[0], [0, n]] + a[1:])


D = 2048
KC = 16
SEQ = 2048
T = 512
NT = SEQ // T
NS = 16
DEPTH = 2
PAST_LEN = 16384
DFF = 5632
FK = DFF // 128
ALPHA = (2 * DEPTH) ** 0.25
LN_EPS = 1e-5
RMS_EPS = 1e-6
L2_EPS = 1e-6
NEG = -1.0e30


def _blk(w, cols):
    K, N = w.shape
    return np.ascontiguousarray(w.reshape(K // 128, 128, N // cols, cols).transpose(2, 1, 0, 3))


def _fm(v):
    F = v.shape[-1]
    lead = v.shape[:-1]
    a = v.reshape(lead + (F // 128, 128))
    nd = a.ndim
    return np.ascontiguousarray(np.moveaxis(np.moveaxis(a, nd - 1, 0), nd - 1, 1))


def host_constants():
    c = {}
    f32 = np.float32
    h = np.arange(4, dtype=np.float64)
    log_g = np.log1p(-np.power(2.0, -5.0 - h))
    i = np.arange(128)
    diff = i[None, :] - i[:, None]
    m = np.zeros((128, 4, 128))
    for hh in range(4):
        m[:, hh, :] = np.where(diff >= 0, np.exp(log_g[hh] * diff) * 128 ** -0.5, 0.0)
    c["ret_maskT"] = m.astype(f32)
    gr = np.exp(log_g[:, None] * (i[None, :] + 1))
    c["ret_grow"] = np.broadcast_to(gr[None], (128, 4, 128)).astype(f32).copy()
    c["ret_kwcol"] = (np.exp(log_g[None, :] * (127 - i[:, None])) * 128 ** -0.5).astype(f32)
    c["ret_gC"] = [float(np.exp(log_g[hh] * 128)) for hh in range(4)]
    c["ret_g1"] = [float(np.exp(log_g[hh])) for hh in range(4)]
    half = 64
    freq = (np.float32(10000.0) ** (-np.arange(half, dtype=f32) / np.float32(half))).astype(f32)
    pos = np.arange(SEQ, dtype=f32)
    ang = (pos[:, None] * freq[None, :]).astype(f32)
    cs, sn = np.cos(ang).astype(f32), np.sin(ang).astype(f32)
    c["rope_cos"] = np.ascontiguousarray(np.concatenate([cs, cs], 1).T)
    c["rope_sin"] = np.ascontiguousarray(np.concatenate([-sn, sn], 1).T)
    angs = (np.float32(PAST_LEN) * freq).astype(f32)
    c["rope_cos_s"] = np.concatenate([np.cos(angs), np.cos(angs)]).astype(f32)[:, None]
    c["rope_sin_s"] = np.concatenate([-np.sin(angs), np.sin(angs)]).astype(f32)[:, None]
    k = np.arange(128)
    c["pswap"] = (k[:, None] == ((k[None, :] + 64) % 128)).astype(f32)
    same = (k[:, None] // 64) == (k[None, :] // 64)
    c["maskU"] = np.where(same & (k[None, :] >= k[:, None]), 0.0, NEG).astype(f32)
    c["maskL"] = np.where(same & (k[:, None] > k[None, :]), 0.0, -NEG).astype(f32)
    c["tri"] = (same & (k[:, None] <= k[None, :])).astype(f32)
    c["blk1"] = same.astype(f32)
    sel = np.zeros((8, 8, 128), f32)
    for hh in range(8):
        sel[hh, hh, :] = 1.0
    c["sel"] = sel
    c["ident"] = np.eye(128, dtype=f32)
    return c


def host_shared(inp):
    f32 = np.float32
    s = {}
    w_mod = inp["w_mod"]
    s["w_mod_b"] = np.stack([_blk(w_mod[l], 256) for l in range(2)])
    s["b_mod_t"] = np.ascontiguousarray(inp["b_mod"].reshape(2, 96, 128).transpose(2, 0, 1))
    w_in = inp["w_in"]
    s["w_in_main"] = np.stack([_blk(w_in[l][:, :6144], 256) for l in range(2)])
    s["w_in_ab"] = np.ascontiguousarray(np.stack([w_in[l][:, 6144:6160].reshape(16, 128, 16).transpose(1, 0, 2) for l in range(2)]))
    s["w_in_su"] = np.stack([_blk(w_in[l][:, 6160:6672], 256) for l in range(2)])
    s["conv_w_t"] = np.ascontiguousarray(inp["conv_w"].reshape(2, 4, 24, 128).transpose(3, 0, 2, 1))
    s["conv_w"] = np.ascontiguousarray(inp["conv_w"])
    s["ret_gn_w_t"] = np.ascontiguousarray(inp["ret_gn_w"].reshape(2, 4, 128).transpose(2, 0, 1))
    s["gdn_a_log"] = np.ascontiguousarray(inp["gdn_a_log"])
    s["gdn_dt_bias"] = np.ascontiguousarray(inp["gdn_dt_bias"])
    s["gdn_norm_w_t"] = np.ascontiguousarray(inp["gdn_norm_w"].T)
    def st(a):
        return np.ascontiguousarray(a.reshape(2, 16, 128).transpose(2, 0, 1))
    s["ssm_a_re_t"] = st(inp["ssm_a_re"])
    s["ssm_a_im_t"] = st(inp["ssm_a_im"])
    s["ssm_log_dt_t"] = st(np.repeat(inp["ssm_log_dt"][:, :, None], 64, axis=2))
    def bpad(b):
        out = np.zeros((2, 128, 16, 128), f32)
        for g in range(32):
            k_, half_ = g // 2, g % 2
            r0 = (k_ % 4) * 32 + half_ * 16
            out[:, r0:r0 + 16, k_, half_ * 64:(half_ + 1) * 64] = b[:, g].transpose(0, 2, 1)
        return out
    s["ssm_bpad_re"] = bpad(inp["ssm_b_re"])
    s["ssm_bpad_im"] = bpad(inp["ssm_b_im"])
    def cpad(cc):
        out = np.zeros((2, 128, 16, 128), f32)
        for g in range(32):
            k_, half_ = g // 2, g % 2
            c0 = (k_ % 4) * 32 + half_ * 16
            out[:, half_ * 64:(half_ + 1) * 64, k_, c0:c0 + 16] = cc[:, g].transpose(0, 2, 1)
        return out
    s["ssm_cpad_re"] = cpad(inp["ssm_c_re"])
    s["ssm_cpad_im"] = cpad(inp["ssm_c_im"])
    s["ssm_d_t"] = np.ascontiguousarray(inp["ssm_d"].reshape(2, 4, 128).transpose(2, 0, 1))
    s["w_glu_b"] = np.ascontiguousarray(np.stack([inp["ssm_w_glu"][l].reshape(4, 128, 512).transpose(1, 0, 2) for l in range(2)]))
    s["w_out_b"] = np.stack([_blk(inp["w_out"][l], 128) for l in range(2)])
    for nm in ("ln1_w", "ln1_b", "ln2_w", "ln2_b"):
        s[nm + "_t"] = np.ascontiguousarray(inp[nm].reshape(2, 16, 128).transpose(2, 0, 1))
    wfi = inp["w_ffn_in"]
    gu = []
    for l in range(2):
        g_ = _blk(wfi[l][:, :DFF], 128)
        u_ = _blk(wfi[l][:, DFF:], 128)
        gu.append(np.concatenate([g_, u_], axis=3))
    s["w_ffn_in_b"] = np.stack(gu)
    s["w_ffn_out_b"] = np.stack([_blk(inp["w_ffn_out"][l], 128) for l in range(2)])
    return s


def host_core(inp, core):
    b = core % 4
    s0 = core * NS
    m = {}
    m["xT"] = _fm(inp["x_prompt"][b])
    m["xsT"] = _fm(inp["x_sample"][s0:s0 + NS, 0])
    call = np.concatenate([inp["c_prompt"][b:b + 1], inp["c_sample"][s0:s0 + NS]], 0)
    m["cT"] = _fm(call)
    m["st_ret"] = np.ascontiguousarray(inp["state_ret"][:, s0:s0 + NS])
    m["st_gdn"] = np.ascontiguousarray(inp["state_gdn"][:, s0:s0 + NS])
    m["st_conv"] = np.ascontiguousarray(inp["state_conv"][:, s0:s0 + NS])
    m["st_sre"] = np.ascontiguousarray(inp["state_ssm_re"][:, s0:s0 + NS].reshape(2, NS, 2048))
    m["st_sim"] = np.ascontiguousarray(inp["state_ssm_im"][:, s0:s0 + NS].reshape(2, NS, 2048))
    return m


AR_BYTES = 84 * 1024 + 512
WB_ELEMS = 4096


def build(prompt=True, sample=True, dbg=None, stop_at=None):
    P = Prog()
    nc = P.nc
    C = host_constants()
    D_ = {}

    def din(name, shape, dt=F32):
        D_[name] = P.dram(name, shape, dt, kind="ExternalInput")
        return D_[name]

    def dout(name, shape, dt=F32):
        D_[name] = P.dram(name, shape, dt, kind="ExternalOutput")
        return D_[name]

    xT_d = din("xT", [128, 16, SEQ]); xsT_d = din("xsT", [128, 16, NS]); cT_d = din("cT", [128, 16, 17])
    w_mod_d = din("w_mod_b", [2, 48, 128, 16, 256]); b_mod_d = din("b_mod_t", [128, 2, 96])
    w_in_d = din("w_in_main", [2, 24, 128, 16, 256]); w_ab_d = din("w_in_ab", [2, 128, 16, 16]); w_su_d = din("w_in_su", [2, 2, 128, 16, 256])
    convw_d = din("conv_w_t", [128, 2, 24, 4]); convw_raw_d = din("conv_w", [2, 4, 3072])
    gnw_d = din("ret_gn_w_t", [128, 2, 4]); alog_d = din("gdn_a_log", [2, 8]); dtb_d = din("gdn_dt_bias", [2, 8]); gdnw_d = din("gdn_norm_w_t", [128, 2])
    are_d = din("ssm_a_re_t", [128, 2, 16]); aim_d = din("ssm_a_im_t", [128, 2, 16]); ldt_d = din("ssm_log_dt_t", [128, 2, 16])
    bpr_d = din("ssm_bpad_re", [2, 128, 16, 128]); bpi_d = din("ssm_bpad_im", [2, 128, 16, 128])
    cpr_d = din("ssm_cpad_re", [2, 128, 16, 128]); cpi_d = din("ssm_cpad_im", [2, 128, 16, 128])
    ssmd_d = din("ssm_d_t", [128, 2, 4]); wglu_d = din("w_glu_b", [2, 128, 4, 512])
    w_out_d = din("w_out_b", [2, 16, 128, 16, 128])
    ln_d = {nm: din(nm + "_t", [128, 2, 16]) for nm in ("ln1_w", "ln1_b", "ln2_w", "ln2_b")}
    wfi_d = din("w_ffn_in_b", [2, 44, 128, 16, 256]); wfo_d = din("w_ffn_out_b", [2, 16, 128, 44, 128])
    st_ret_d = din("st_ret", [2, NS, 4, 128, 128]); st_gdn_d = din("st_gdn", [2, NS, 8, 128, 128])
    st_conv_d = din("st_conv", [2, NS, 3, 3072]); st_sre_d = din("st_sre", [2, NS, 2048]); st_sim_d = din("st_sim", [2, NS, 2048])
    cd = {}
    for nm in ("ret_maskT", "ret_grow", "ret_kwcol", "rope_cos", "rope_sin", "rope_cos_s", "rope_sin_s", "pswap",
               "maskU", "maskL", "tri", "blk1", "sel", "ident"):
        cd[nm] = din("c_" + nm, list(C[nm].shape))
    yT_d = dout("yT", [128, 16, SEQ]); ysT_d = dout("ysT", [128, 16, NS])
    retp_d = dout("ret_p", [2, 4, 128, 128]); gdnp_d = dout("gdn_p", [2, 8, 128, 128])
    convp_d = dout("conv_p", [2, 128, 24, 3]); ssmp_d = dout("ssm_p", [2, 128, 16, 2])
    rets_d = dout("ret_s", [2, NS, 4, 128, 128]); gdns_d = dout("gdn_s", [2, NS, 8, 128, 128])
    convs_d = dout("conv_s", [2, NS, 3, 3072]); ssms_re_d = dout("ssm_s_re", [2, NS, 2048]); ssms_im_d = dout("ssm_s_im", [2, NS, 2048])
    bt_scr = P.dram("bt_scr", [2, 2, 128, 16, 128], BF16)
    scr64 = P.dram("scr64", [64, 128], F32)
    tab_scr = P.dram("tab_scr", [2, 4, 128, 16, T], F32)
    scr_s = P.dram("scr_s", [NS, 4096], F32)
    dbg_d = {}
    if dbg:
        for nm, shp in dbg.items():
            dbg_d[nm] = dout("dbg_" + nm, shp)

    def mm(out, lhsT, rhs, start=True, stop=True, reads=(), writes=()):
        P.op("pe", lambda e: e.matmul(out, lhsT, rhs, start=start, stop=stop), reads, writes)

    def tr(out, in_, ident, reads=(), writes=()):
        P.op("pe", lambda e: e.transpose(out, in_, ident), reads, writes)

    def tt(out, a, b, op, reads=(), writes=(), eng="dve"):
        P.op(eng, lambda e: e.tensor_tensor(out, a, b, op), reads, writes)

    def ts(out, a, s1, op0, s2=None, op1=None, reads=(), writes=(), eng="dve"):
        if s2 is None:
            P.op(eng, lambda e: e.tensor_scalar(out, a, s1, None, op0), reads, writes)
        else:
            P.op(eng, lambda e: e.tensor_scalar(out, a, s1, s2, op0, op1), reads, writes)

    def stt(out, a, s, b, op0, op1, reads=(), writes=(), eng="dve"):
        P.op(eng, lambda e: e.scalar_tensor_tensor(out, a, s, b, op0, op1), reads, writes)

    def act(out, in_, func, bias=0.0, scale=1.0, reads=(), writes=()):
        P.op("act", lambda e: e.activation(out, in_, func, bias=bias, scale=scale), reads, writes)

    def cp(out, in_, reads=(), writes=(), eng="act"):
        if eng == "act":
            P.op("act", lambda e: e.copy(out, in_), reads, writes)
        else:
            P.op(eng, lambda e: e.tensor_copy(out, in_), reads, writes)

    def amul(out, in_, c, reads=(), writes=()):
        P.op("act", lambda e: e.mul(out, in_, c), reads, writes)

    def memset(ap, v, writes=(), eng="dve"):
        P.op(eng, lambda e: e.memset(ap, v), (), writes)

    def recip(out, in_, reads=(), writes=()):
        P.op("dve", lambda e: e.reciprocal(out, in_), reads, writes)

    def col_bc1(a, n):
        return bass.AP(a.tensor, a.offset, [list(a.ap[0]), [0, n]])

    _alt = {"i": 0}

    def alt():
        _alt["i"] ^= 1
        return "act" if _alt["i"] else "dve"

    banks = [P.ps(f"bank{i}", [128, 512], F32) for i in range(8)]
    banks_bf = [b.bitcast(BF16) for b in banks]
    bank_r = P.regions(8, "bank", excl=True)
    _bk = {"i": 0}

    def pb():
        i = _bk["i"]
        _bk["i"] = (i + 1) % 5
        return banks[i], banks_bf[i], bank_r[i]

    def pb_res(i):
        return banks[i], banks_bf[i], bank_r[i]

    NWB = 3
    wbufs = [P.sb(f"wbuf{i}", [128, WB_ELEMS], BF16) for i in range(NWB)]
    wregs = P.regions(NWB, "wbuf")
    _wb = {"i": 0}

    def wload(src, kc, cols):
        i = _wb["i"]
        _wb["i"] = (i + 1) % NWB
        view = wbufs[i][:, 0:kc * cols].rearrange("p (k c) -> p k c", c=cols)
        P.dma(view, src, writes=[wregs[i]], queue="pool")
        return view, wregs[i]

    WSH = {}
    for _nm, _t, _shape in (("w_in_main", w_in_d, [2, 24, 128, 16, 256]), ("w_in_su", w_su_d, [2, 2, 128, 16, 256]),
                            ("w_out_b", w_out_d, [2, 16, 128, 16, 128]), ("w_ffn_in_b", wfi_d, [2, 44, 128, 16, 256]),
                            ("w_ffn_out_b", wfo_d, [2, 16, 128, 44, 128]), ("w_glu_b", wglu_d, [2, 128, 4, 512])):
        WSH[_nm] = (_t, P.dram("sh_" + _nm, _shape, BF16))
    w_written = {}

    def wload2(name, idx, kc, cols):
        f32_t, sh_t = WSH[name]
        key = (name, repr(idx))
        i = _wb["i"]
        _wb["i"] = (i + 1) % NWB
        view = wbufs[i][:, 0:kc * cols].rearrange("p (k c) -> p k c", c=cols)
        if key in w_written:
            P.dma(view, sh_t[idx], reads=[w_written[key]], writes=[wregs[i]], queue="pool")
        else:
            P.dma(view, f32_t[idx], writes=[wregs[i]], queue="pool")
            rk = P.region("wsh")
            P.dma(sh_t[idx], view, reads=[wregs[i]], writes=[rk], queue="sp")
            w_written[key] = rk
        return view, wregs[i]

    arena_bf = P.sb("arena", [128, AR_BYTES // 2], BF16)
    arena_f = arena_bf.bitcast(F32)
    _ar = {"off": 0}

    def phase(keep=None):
        P.arena_phase()
        _ar["off"] = _ar.get("base", 0) if keep is None else keep

    def aal(n, dt=F32, nreg=1):
        sz = 4 if dt == F32 else 2
        off = (_ar["off"] + 31) // 32 * 32
        assert off + n * sz <= AR_BYTES, ("arena overflow", off, n, sz)
        _ar["off"] = off + n * sz
        _ar["hw"] = max(_ar.get("hw", 0), _ar["off"])
        if dt == F32:
            v = arena_f[:, off // 4: off // 4 + n]
        else:
            v = arena_bf[:, off // 2: off // 2 + n]
        if nreg == 1:
            return v, P.aregion()
        return v, [P.aregion() for _ in range(nreg)]

    xT = P.sb("xT_sb", [128, 16, T], F32); r_x = P.regions(16, "x")
    hT = P.sb("hT", [128, 16, T], BF16); r_h = P.regions(16, "h")
    mixT = P.sb("mixT", [128, 16, T], BF16); r_mix = P.regions(16, "mix")
    mod_scr = P.dram("mod_scr", [128, 2 * 96 * 17], F32); r_modscr = P.region("modscr")
    MODP = P.sb("MODP", [128, 2, 6, 16], F32); r_modp = P.region("modp")
    lnw = {nm: P.sb("s_" + nm, [128, 2, 16], F32) for nm in ln_d}
    r_const = P.region("const")
    S_ret = P.sb("S_ret", [128, 2, 4, 128], F32); r_Sret = P.regions(2, "Sret")
    Sb_ret = P.sb("Sb_ret", [128, 4, 128], BF16); r_Sbret = P.region("Sbret")
    S_gdn = P.sb("S_gdn", [128, 2, 8, 128], F32); r_Sgdn = P.regions(2, "Sgdn")
    Sb_gdn = P.sb("Sb_gdn", [128, 8, 128], BF16); r_Sbgdn = P.region("Sbgdn")
    convc = P.sb("convc", [128, 2, 24, 3], F32); r_convc = P.regions(2, "convc")
    ssmc = P.sb("ssmc", [128, 2, 16, 2], F32); r_ssmc = [P.regions(16, f"ssmc{l_}") for l_ in range(2)]
    s5pw = P.sb("s5pw", [128, 2, 9, 3, 16], F32); r_s5pw = P.region("s5pw")
    s5pn = P.sb("s5pn", [128, 2, 9, 2, 16], F32); r_s5pn = P.region("s5pn")
    c_maskT = P.sb("s_maskT", [128, 4, 128], F32); c_grow = P.sb("s_grow", [128, 4, 128], F32); c_kwcol = P.sb("s_kwcol", [128, 4], F32)
    c_pswap = P.sb("s_pswap", [128, 128], F32); c_maskU = P.sb("s_maskU", [128, 128], F32); c_maskL = P.sb("s_maskL", [128, 128], F32)
    c_tri = P.sb("s_tri", [128, 128], F32); c_blk1 = P.sb("s_blk1", [128, 128], F32)
    c_ident = P.sb("s_ident", [128, 128], F32); c_identb = P.sb("s_identb", [128, 128], BF16)
    c_o128 = P.sb("s_o128", [128, 128], F32); c_oln = P.sb("s_oln", [128, 128], F32); c_o1 = P.sb("s_o1", [128, 128], F32); c_odk = P.sb("s_odk", [128, 128], F32)
    c_convw = P.sb("s_convw", [128, 2, 24, 4], F32); c_gnw = P.sb("s_gnw", [128, 2, 4], F32); c_gdnw = P.sb("s_gdnw", [128, 2], F32)
    c_nega = P.sb("s_nega", [128, 2, 8], F32); c_dtb = P.sb("s_dtb", [128, 2, 8], F32); c_ssmd = P.sb("s_ssmd", [128, 2, 4], F32)
    c_bmod = P.sb("s_bmod", [128, 2, 96], F32)
    c_olnb = P.sb("s_olnb", [128, 128], BF16)
    c_ropes = P.sb("s_ropes", [128, 2], F32)

    def ld(dst, src):
        P.dma(dst, src, writes=[r_const])

    ld(c_maskT[:, :, :], cd["ret_maskT"].ap()); ld(c_grow[:, :, :], cd["ret_grow"].ap()); ld(c_kwcol[:, :], cd["ret_kwcol"].ap())
    ld(c_pswap[:, :], cd["pswap"].ap()); ld(c_maskU[:, :], cd["maskU"].ap()); ld(c_maskL[:, :], cd["maskL"].ap())
    ld(c_tri[:, :], cd["tri"].ap()); ld(c_blk1[:, :], cd["blk1"].ap()); ld(c_ident[:, :], cd["ident"].ap())
    ld(c_convw[:, :, :, :], convw_d.ap()); ld(c_gnw[:, :, :], gnw_d.ap()); ld(c_gdnw[:, :], gdnw_d.ap()); ld(c_ssmd[:, :, :], ssmd_d.ap())
    ld(c_bmod[:, :, :], b_mod_d.ap())
    ld(c_ropes[:, 0:1], cd["rope_cos_s"].ap()); ld(c_ropes[:, 1:2], cd["rope_sin_s"].ap())
    for nm in ln_d:
        ld(lnw[nm][:, :, :], ln_d[nm].ap())
    ld(c_nega[:, :, :].rearrange("p l h -> p (l h)"), alog_d.ap().rearrange("l h -> (l h)").partition_broadcast(128))
    ld(c_dtb[:, :, :].rearrange("p l h -> p (l h)"), dtb_d.ap().rearrange("l h -> (l h)").partition_broadcast(128))
    cp(c_identb[:, :], c_ident[:, :], [r_const], [r_const])
    memset(c_o128[:, :], 1.0 / 128, [r_const]); memset(c_oln[:, :], 1.0 / D, [r_const]); memset(c_o1[:, :], 1.0, [r_const]); memset(c_odk[:, :], 128.0, [r_const])
    cp(c_olnb[:, :], c_oln[:, :], [r_const], [r_const])
    act(c_nega[:, :, :], c_nega[:, :, :], AF.Exp, reads=[r_const], writes=[r_const])
    ts(c_nega[:, :, :], c_nega[:, :, :], -1.0, ALU.mult, reads=[r_const], writes=[r_const])
    memset(S_ret[:, :, :, :], 0.0, r_Sret); memset(S_gdn[:, :, :, :], 0.0, r_Sgdn)
    memset(convc[:, :, :, :], 0.0, r_convc); memset(ssmc[:, :, :, :], 0.0, r_ssmc[0] + r_ssmc[1])

    if stop_at == "const":
        return P.finish()
    phase()
    modT, r_mod = aal(2 * 96 * 17); modT = modT.rearrange("p (l j n) -> p l j n", l=2, j=96)
    c32, r_c32 = aal(16 * 17); c32 = c32.rearrange("p (k n) -> p k n", n=17)
    csb, r_csb = aal(16 * 17, BF16); csb = csb.rearrange("p (k n) -> p k n", n=17)
    P.dma(c32, cT_d.ap(), writes=[r_c32])
    act(csb, c32, AF.Silu, reads=[r_c32], writes=[r_csb])
    for l in range(2):
        for jb in range(48):
            wv, wr = wload(w_mod_d[l, jb], 16, 256)
            for sub in range(2):
                j = jb * 2 + sub
                bk, _, br = pb()
                for kc in range(16):
                    mm(bk[:, 0:17], wv[:, kc, sub * 128:(sub + 1) * 128], csb[:, kc, :], start=(kc == 0), stop=(kc == 15),
                       reads=[wr, r_csb], writes=[br])
                act(modT[:, l, j, :], bk[:, 0:17], AF.Identity, bias=c_bmod[:, l, j:j + 1], reads=[br, r_const], writes=[r_mod])
    for l in range(2):
        for q, (j0, addone, sc) in enumerate([(0, 0.0, 1.0), (16, 1.0, 1.0), (32, 0.0, 1.0 / ALPHA), (48, 0.0, 1.0), (64, 1.0, 1.0), (80, 0.0, 1.0 / ALPHA)]):
            ts(MODP[:, l, q, :], modT[:, l, j0:j0 + 16, 0], sc, ALU.mult, addone, ALU.add, reads=[r_mod], writes=[r_modp])
    P.dma(mod_scr.ap(), modT.rearrange("p l j n -> p (l j n)"), reads=[r_mod], writes=[r_modscr])
    MT = {"t": modT, "r": r_mod}

    if stop_at == "setup1":
        return P.finish()
    phase()
    TWO_PI = 2.0 * math.pi
    s5t, r_s5t = aal(20 * 32)
    s5t = s5t.rearrange("p (a n) -> p a n", n=32)
    R5 = [r_s5t]

    def row(i):
        return s5t[:, i, :]
    P.dma(row(0), are_d.ap().rearrange("p l k -> p (l k)"), writes=R5)
    P.dma(row(1), aim_d.ap().rearrange("p l k -> p (l k)"), writes=R5)
    P.dma(row(2), ldt_d.ap().rearrange("p l k -> p (l k)"), writes=R5)
    act(row(2), row(2), AF.Exp, reads=R5, writes=R5)
    tt(row(3), row(0), row(2), ALU.mult, R5, R5)
    tt(row(4), row(1), row(2), ALU.mult, R5, R5)
    act(row(5), row(3), AF.Exp, reads=R5, writes=R5)
    act(row(6), row(4), AF.Sin, scale=1.0 / 16, reads=R5, writes=R5)
    act(row(7), row(4), AF.Sin, bias=math.pi / 2, scale=1.0 / 16, reads=R5, writes=R5)
    for _d in range(4):
        tt(row(16), row(6), row(7), ALU.mult, R5, R5)
        tt(row(17), row(7), row(7), ALU.mult, R5, R5)
        tt(row(18), row(6), row(6), ALU.mult, R5, R5)
        tt(row(7), row(17), row(18), ALU.subtract, R5, R5)
        ts(row(6), row(16), 2.0, ALU.mult, reads=R5, writes=R5)
    tt(row(8), row(5), row(7), ALU.mult, R5, R5)
    tt(row(9), row(5), row(6), ALU.mult, R5, R5)
    ts(row(10), row(8), -1.0, ALU.add, reads=R5, writes=R5)
    tt(row(11), row(0), row(0), ALU.mult, R5, R5)
    tt(row(12), row(1), row(1), ALU.mult, R5, R5)
    tt(row(11), row(11), row(12), ALU.add, R5, R5)
    recip(row(11), row(11), R5, R5)
    tt(row(12), row(10), row(0), ALU.mult, R5, R5)
    tt(row(13), row(9), row(1), ALU.mult, R5, R5)
    tt(row(12), row(12), row(13), ALU.add, R5, R5)
    tt(row(14), row(12), row(11), ALU.mult, R5, R5)
    tt(row(12), row(9), row(0), ALU.mult, R5, R5)
    tt(row(13), row(10), row(1), ALU.mult, R5, R5)
    tt(row(12), row(12), row(13), ALU.subtract, R5, R5)
    tt(row(15), row(12), row(11), ALU.mult, R5, R5)
    for l in range(2):
        cp(s5pw[:, l, 0, 0, :], s5t[:, 8, l * 16:(l + 1) * 16], R5, [r_s5pw], eng="dve")
        cp(s5pw[:, l, 0, 1, :], s5t[:, 9, l * 16:(l + 1) * 16], R5, [r_s5pw], eng="dve")
    for k in range(1, 9):
        a_re = s5pw[:, :, k - 1, 0, :]; a_im = s5pw[:, :, k - 1, 1, :]
        t1 = s5t[:, 16, :].rearrange("p (l k) -> p l k", l=2); t2 = s5t[:, 17, :].rearrange("p (l k) -> p l k", l=2)
        tt(t1, a_re, a_re, ALU.mult, [r_s5pw], R5)
        tt(t2, a_im, a_im, ALU.mult, [r_s5pw], R5)
        tt(s5pw[:, :, k, 0, :], t1, t2, ALU.subtract, R5, [r_s5pw])
        tt(t1, a_re, a_im, ALU.mult, [r_s5pw], R5)
        ts(s5pw[:, :, k, 1, :], t1, 2.0, ALU.mult, reads=R5, writes=[r_s5pw])
    ts(s5pw[:, :, :, 2, :], s5pw[:, :, :, 1, :], -1.0, ALU.mult, reads=[r_s5pw], writes=[r_s5pw])
    tt(row(16), row(8), row(8), ALU.mult, R5, R5); tt(row(17), row(9), row(9), ALU.mult, R5, R5)
    tt(row(16), row(16), row(17), ALU.add, R5, R5); recip(row(16), row(16), R5, R5)
    tt(row(18), row(8), row(16), ALU.mult, R5, R5)
    tt(row(19), row(9), row(16), ALU.mult, R5, R5)
    ts(row(19), row(19), -1.0, ALU.mult, reads=R5, writes=R5)
    for l in range(2):
        cp(s5pn[:, l, 0, 0, :], s5t[:, 18, l * 16:(l + 1) * 16], R5, [r_s5pn], eng="dve")
        cp(s5pn[:, l, 0, 1, :], s5t[:, 19, l * 16:(l + 1) * 16], R5, [r_s5pn], eng="dve")
    for k in range(1, 9):
        a_re = s5pn[:, :, k - 1, 0, :]; a_im = s5pn[:, :, k - 1, 1, :]
        t1 = s5t[:, 16, :].rearrange("p (l k) -> p l k", l=2); t2 = s5t[:, 17, :].rearrange("p (l k) -> p l k", l=2)
        tt(t1, a_re, a_re, ALU.mult, [r_s5pn], R5)
        tt(t2, a_im, a_im, ALU.mult, [r_s5pn], R5)
        tt(s5pn[:, :, k, 0, :], t1, t2, ALU.subtract, R5, [r_s5pn])
        tt(t1, a_re, a_im, ALU.mult, [r_s5pn], R5)
        ts(s5pn[:, :, k, 1, :], t1, 2.0, ALU.mult, reads=R5, writes=[r_s5pn])
    if stop_at == "s5a":
        return P.finish()
    bk, _, br = pb()
    tr(bk[0:64, 0:128], s5t[:, 14:16, :].rearrange("p a n -> p (a n)"), c_ident[:, :], R5 + [r_const], [br])
    cf64, r_cf64 = aal(128)
    cp(cf64[0:64, :], bk[0:64, 0:128], [br], [r_cf64])
    r_scr64 = P.region("scr64")
    P.dma(scr64.ap(), cf64[0:64, :], reads=[r_cf64], writes=[r_scr64])
    cb, r_cb = aal(64 * 128)
    P.dma(cb, scr64.ap().rearrange("a s -> (a s)").partition_broadcast(128), reads=[r_scr64], writes=[r_cb])
    cb = cb.rearrange("p (c l k s) -> p c l k s", c=2, l=2, k=16)
    if stop_at == "s5b":
        return P.finish()
    r_btscr = P.region("btscr")
    bre, r_bre = aal(2048); bim, r_bim = aal(2048)
    t1, r_t1 = aal(2048); t2, r_t2 = aal(2048)
    ob, r_ob = aal(2 * 2048, BF16)
    for l in range(2):
        P.dma(bre, bpr_d[l].rearrange("p k s -> p (k s)"), writes=[r_bre])
        P.dma(bim, bpi_d[l].rearrange("p k s -> p (k s)"), writes=[r_bim])
        cbr = cb[:, 0, l, :, :].rearrange("p k s -> p (k s)"); cbi = cb[:, 1, l, :, :].rearrange("p k s -> p (k s)")
        tt(t1, bre, cbr, ALU.mult, [r_bre, r_cb], [r_t1]); tt(t2, bim, cbi, ALU.mult, [r_bim, r_cb], [r_t2])
        tt(ob[:, 0:2048], t1, t2, ALU.subtract, [r_t1, r_t2], [r_ob])
        tt(t1, bre, cbi, ALU.mult, [r_bre, r_cb], [r_t1]); tt(t2, bim, cbr, ALU.mult, [r_bim, r_cb], [r_t2])
        tt(ob[:, 2048:4096], t1, t2, ALU.add, [r_t1, r_t2], [r_ob])
        if stop_at != "s5c":
            P.dma(bt_scr[l].rearrange("c p k s -> p c (k s)"), ob.rearrange("p (c n) -> p c n", c=2), reads=[r_ob], writes=[r_btscr])
    if stop_at in ("s5c", "s5d"):
        return P.finish()
    phase()
    r_tabscr = P.region("tabscr")
    Tre, r_Tre = aal(8 * T); Tim, r_Tim = aal(8 * T)
    Tre = Tre.rearrange("p (k n) -> p k n", n=T); Tim = Tim.rearrange("p (k n) -> p k n", n=T)
    q1, r_q1 = aal(8 * (T // 2)); q2, r_q2 = aal(8 * (T // 2))
    nlv = int(math.log2(T))
    for l in range(2):
        for sg, (pwt, rpw) in enumerate(((s5pw, r_s5pw), (s5pn, r_s5pn))):
            for hf in range(2):
                ks = slice(hf * 8, hf * 8 + 8)
                cp(Tre[:, :, 0:1], pwt[:, l, 0, 0, ks].rearrange("p (k o) -> p k o", o=1), [rpw], [r_Tre], eng="dve")
                cp(Tim[:, :, 0:1], pwt[:, l, 0, 1, ks].rearrange("p (k o) -> p k o", o=1), [rpw], [r_Tim], eng="dve")
                for j in range(nlv):
                    sz = 1 << j
                    pr = bc_last(pwt[:, l, j, 0, ks], sz); pi_ = bc_last(pwt[:, l, j, 1, ks], sz)
                    a1 = q1[:, 0:8 * sz].rearrange("p (k n) -> p k n", n=sz); a2 = q2[:, 0:8 * sz].rearrange("p (k n) -> p k n", n=sz)
                    sr = Tre[:, :, 0:sz]; si = Tim[:, :, 0:sz]
                    tt(a1, sr, pr, ALU.mult, [r_Tre, rpw], [r_q1]); tt(a2, si, pi_, ALU.mult, [r_Tim, rpw], [r_q2])
                    tt(Tre[:, :, sz:2 * sz], a1, a2, ALU.subtract, [r_q1, r_q2], [r_Tre])
                    tt(a1, sr, pi_, ALU.mult, [r_Tre, rpw], [r_q1]); tt(a2, si, pr, ALU.mult, [r_Tim, rpw], [r_q2])
                    tt(Tim[:, :, sz:2 * sz], a1, a2, ALU.add, [r_q1, r_q2], [r_Tim])
                P.dma(tab_scr[l, sg * 2 + 0, :, ks, :], Tre, reads=[r_Tre], writes=[r_tabscr])
                P.dma(tab_scr[l, sg * 2 + 1, :, ks, :], Tim, reads=[r_Tim], writes=[r_tabscr])
    if dbg and "s5pw" in dbg:
        P.dma(dbg_d["s5pw"].ap(), s5pw[:, :, :, :, :], reads=[r_s5pw], is_output=True)

    def proj_fm(wv, wr, c0, N, hview, hregs):
        bk, bkb, br = pb()
        for kc in range(16):
            mm(bk[:, 0:N], wv[:, kc, c0:c0 + 128], hview[:, kc, 0:N], start=(kc == 0), stop=(kc == 15), reads=[wr, hregs[kc]], writes=[br])
        return bk, br

    def ln_tail(l, which, N, st, prompt_mode, nxt):
        ps_mean, r_pm, ps_msq, r_pq = st
        m2, r_m2 = aal(N); var, r_var = aal(N); rstd, r_rstd = aal(N); nmr, r_nmr = aal(N)
        act(m2, ps_mean[:, 0:N], AF.Square, reads=[r_pm], writes=[r_m2])
        tt(var, ps_msq[:, 0:N], m2, ALU.subtract, [r_pq, r_m2], [r_var])
        act(var, var, AF.Ln, bias=LN_EPS / (ALPHA * ALPHA), reads=[r_var], writes=[r_var])
        act(rstd, var, AF.Exp, scale=-0.5, reads=[r_var], writes=[r_rstd])
        tt(nmr, ps_mean[:, 0:N], rstd, ALU.mult, [r_pm, r_rstd], [r_nmr])
        wn, bn = ("ln1_w", "ln1_b") if which == 1 else ("ln2_w", "ln2_b")
        tmp, r_tmp = aal(2 * N, nreg=2)
        for kc in range(16):
            tv = tmp[:, (kc % 2) * N:(kc % 2 + 1) * N]; rt = r_tmp[kc % 2]
            tt(tv, xT[:, kc, 0:N], rstd, ALU.mult, [r_x[kc], r_rstd], [rt])
            tt(tv, tv, nmr, ALU.subtract, [rt, r_nmr], [rt])
            act(xT[:, kc, 0:N], tv, AF.Identity, bias=lnw[bn][:, l, kc:kc + 1], scale=lnw[wn][:, l, kc:kc + 1], reads=[rt, r_const], writes=[r_x[kc]])
            if nxt is not None:
                modulate(kc, N, nxt, prompt_mode)

    def modulate(kc, N, sel_, prompt_mode):
        l, iB, iA = sel_
        if prompt_mode:
            ts(hT[:, kc, 0:N], xT[:, kc, 0:N], MODP[:, l, iA, kc:kc + 1], ALU.mult, MODP[:, l, iB, kc:kc + 1], ALU.add,
               reads=[r_x[kc], r_modp], writes=[r_h[kc]])
        else:
            jB = 0 if iB == 0 else 48
            jA = 16 if iA == 1 else 64
            tmpm, r_tm = aal(N)
            stt(tmpm, MT["t"][:, l, jA + kc, 1:1 + N], 1.0, xT[:, kc, 0:N], ALU.add, ALU.mult, reads=[MT["r"], r_x[kc]], writes=[r_tm])
            tt(hT[:, kc, 0:N], tmpm, MT["t"][:, l, jB + kc, 1:1 + N], ALU.add, [r_tm, MT["r"]], [r_h[kc]])

    def resid_and_stats(l, which, N, blocks, prompt_mode):
        gi = 2 if which == 1 else 5
        pm, _, r_pm = pb_res(5); pq, _, r_pq = pb_res(6)
        sq, r_sq = aal(2 * N, BF16, nreg=2)
        zb, r_zb = aal(2 * N, BF16, nreg=2)
        pend = []

        def flush(item, last):
            kc_, = item
            sv = sq[:, (kc_ % 2) * N:(kc_ % 2 + 1) * N]
            zv = zb[:, (kc_ % 2) * N:(kc_ % 2 + 1) * N]
            mm(pm[:, 0:N], c_olnb[:, :], zv, start=(kc_ == 0), stop=(kc_ == 15), reads=[r_zb[kc_ % 2], r_const], writes=[r_pm])
            mm(pq[:, 0:N], c_olnb[:, :], sv, start=(kc_ == 0), stop=(kc_ == 15), reads=[r_sq[kc_ % 2], r_const], writes=[r_pq])
        for kc, bk, br in blocks:
            if prompt_mode:
                stt(xT[:, kc, 0:N], bk[:, 0:N], MODP[:, l, gi, kc:kc + 1], xT[:, kc, 0:N], ALU.mult, ALU.add,
                    reads=[br, r_modp, r_x[kc]], writes=[r_x[kc]])
            else:
                j0 = 32 if which == 1 else 80
                gt, r_gt = aal(N)
                tt(gt, bk[:, 0:N], MT["t"][:, l, j0 + kc, 1:1 + N], ALU.mult, [br, MT["r"]], [r_gt])
                stt(xT[:, kc, 0:N], gt, 1.0 / ALPHA, xT[:, kc, 0:N], ALU.mult, ALU.add, reads=[r_gt, r_x[kc]], writes=[r_x[kc]])
            act(sq[:, (kc % 2) * N:(kc % 2 + 1) * N], xT[:, kc, 0:N], AF.Square, reads=[r_x[kc]], writes=[r_sq[kc % 2]])
            cp(zb[:, (kc % 2) * N:(kc % 2 + 1) * N], xT[:, kc, 0:N], [r_x[kc]], [r_zb[kc % 2]])
            pend.append((kc,))
            if len(pend) > 1:
                flush(pend.pop(0), False)
        while pend:
            flush(pend.pop(0), True)
        return (pm, r_pm, pq, r_pq)

    def wout_blocks(l, N):
        for jo in range(16):
            wv, wr = wload2("w_out_b", (l, jo), 16, 128)
            bk, _, br = pb()
            for kc in range(16):
                mm(bk[:, 0:N], wv[:, kc, :], mixT[:, kc, 0:N], start=(kc == 0), stop=(kc == 15), reads=[wr, r_mix[kc]], writes=[br])
            yield jo, bk, br

    def ffn(l, N, prompt_mode, nxt):
        phase()
        actb, r_act = aal(FK * N, BF16, nreg=FK)
        actb = actb.rearrange("p (k n) -> p k n", n=N)
        sg, r_sg = aal(2 * N, nreg=2)
        for j in range(FK):
            wv, wr = wload2("w_ffn_in_b", (l, j), 16, 256)
            bg, _, rg = pb(); bu, _, ru = pb()
            for kc in range(16):
                mm(bg[:, 0:N], wv[:, kc, 0:128], hT[:, kc, 0:N], start=(kc == 0), stop=(kc == 15), reads=[wr, r_h[kc]], writes=[rg])
            for kc in range(16):
                mm(bu[:, 0:N], wv[:, kc, 128:256], hT[:, kc, 0:N], start=(kc == 0), stop=(kc == 15), reads=[wr, r_h[kc]], writes=[ru])
            sv = sg[:, (j % 2) * N:(j % 2 + 1) * N]
            act(sv, bg[:, 0:N], AF.Silu, reads=[rg], writes=[r_sg[j % 2]])
            tt(actb[:, j, :], sv, bu[:, 0:N], ALU.mult, [r_sg[j % 2], ru], [r_act[j]])

        def blocks():
            for jo in range(16):
                bk, _, br = pb()
                for hf in range(2):
                    wv, wr = wload2("w_ffn_out_b", (l, jo, slice(None), slice(hf * 22, (hf + 1) * 22), slice(None)), 22, 128)
                    for k2 in range(22):
                        kc = hf * 22 + k2
                        mm(bk[:, 0:N], wv[:, k2, :], actb[:, kc, :], start=(kc == 0), stop=(kc == FK - 1), reads=[wr, r_act[kc]], writes=[br])
                yield jo, bk, br
        st = resid_and_stats(l, 2, N, blocks(), prompt_mode)
        ln_tail(l, 2, N, st, prompt_mode, nxt)

    def attn_out_ln(l, N, prompt_mode):
        phase()
        st = resid_and_stats(l, 1, N, wout_blocks(l, N), prompt_mode)
        ln_tail(l, 1, N, st, prompt_mode, (l, 3, 4))

    def gn_tmps(N):
        return [aal(N) for _ in range(4)]

    def groupnorm_cols(src_bank, r_src, N, ones_c, eps, center, out_ap, r_out, gate_ap, r_gate, tmpk):
        (xs, r_xs), (sq, r_sq), (var, r_var), (m2, r_m2) = tmpk
        cp(xs, src_bank, [r_src], [r_xs])
        act(sq, src_bank, AF.Square, reads=[r_src], writes=[r_sq])
        bq, _, rq = pb()
        mm(bq[:, 0:N], ones_c, sq, reads=[r_sq, r_const], writes=[rq])
        if center:
            bm, _, rm = pb()
            mm(bm[:, 0:N], ones_c, xs, reads=[r_xs, r_const], writes=[rm])
            act(m2, bm[:, 0:N], AF.Square, reads=[rm], writes=[r_m2])
            tt(var, bq[:, 0:N], m2, ALU.subtract, [rq, r_m2], [r_var])
            tt(xs, xs, bm[:, 0:N], ALU.subtract, [r_xs, rm], [r_xs])
            act(var, var, AF.Ln, bias=eps, reads=[r_var], writes=[r_var])
        else:
            act(var, bq[:, 0:N], AF.Ln, bias=eps, reads=[rq], writes=[r_var])
        act(var, var, AF.Exp, scale=-0.5, reads=[r_var], writes=[r_var])
        tt(xs, xs, var, ALU.mult, [r_xs, r_var], [r_xs])
        xo = xs
        if len(out_ap.shape) == 3:
            xo = xs.rearrange("p (h d) -> p h d", d=out_ap.shape[2])
        tt(out_ap, xo, gate_ap, ALU.mult, [r_xs] + r_gate, r_out)

    def ret_prompt(l, it):
        phase()
        N = T
        t0 = it * T
        cosT, r_cos = aal(N); sinT, r_sin = aal(N)
        P.dma(cosT, cd["rope_cos"][:, t0:t0 + N], writes=[r_cos]); P.dma(sinT, cd["rope_sin"][:, t0:t0 + N], writes=[r_sin])
        qT, r_q = aal(4 * N, BF16, nreg=4); qT = qT.rearrange("p (h n) -> p h n", n=N)
        qwT, r_qw = aal(4 * N, BF16, nreg=4); qwT = qwT.rearrange("p (h n) -> p h n", n=N)
        kT, r_k = aal(4 * N, BF16, nreg=4); kT = kT.rearrange("p (h n) -> p h n", n=N)
        vtok, r_v = aal(4 * 512, BF16, nreg=4); vtok = vtok.rearrange("p (c n) -> p c n", n=512)
        gate, r_g = aal(4 * N, BF16, nreg=4); gate = gate.rearrange("p (h n) -> p h n", n=N)
        raw, r_raw = aal(2 * N, nreg=2); t1, r_t1 = aal(2 * N, nreg=2); t2, r_t2 = aal(2 * N, nreg=2)
        pend = []

        def rope_block(j, rw, r_rw, a1, ra1, a2, ra2):
            h = j % 4
            b2, _, br2 = pb()
            mm(b2[:, 0:N], c_pswap[:, :], rw, reads=[r_rw, r_const], writes=[br2])
            tt(a1, rw, cosT, ALU.mult, [r_rw, r_cos], [ra1])
            tt(a2, b2[:, 0:N], sinT, ALU.mult, [br2, r_sin], [ra2])
            if j < 4:
                tt(a1, a1, a2, ALU.add, [ra1, ra2], [ra1])
                cp(qT[:, h, :], a1, [ra1], [r_q[h]])
                tt(qwT[:, h, :].rearrange("p (c i) -> p c i", i=128), a1.rearrange("p (c i) -> p c i", i=128),
                   bc_mid(c_grow[:, h, :], N // 128), ALU.mult, [ra1, r_const], [r_qw[h]])
            else:
                tt(kT[:, h, :], a1, a2, ALU.add, [ra1, ra2], [r_k[h]])
        for b in range(4):
            wv, wr = wload2("w_in_main", (l, b), 16, 256)
            for half in range(2):
                j = b * 2 + half
                bk, br = proj_fm(wv, wr, half * 128, N, hT, r_h)
                i2 = j % 2
                rw = raw[:, i2 * N:(i2 + 1) * N]; a1 = t1[:, i2 * N:(i2 + 1) * N]; a2 = t2[:, i2 * N:(i2 + 1) * N]
                cp(rw, bk[:, 0:N], [br], [r_raw[i2]])
                if pend:
                    rope_block(*pend.pop())
                pend.append((j, rw, r_raw[i2], a1, r_t1[i2], a2, r_t2[i2]))
        for b in (4, 5):
            wv, wr = wload2("w_in_main", (l, b), 16, 256)
            for tb in range(N // 128):
                if pend and tb == 1:
                    rope_block(*pend.pop())
                bk, _, br = pb()
                for kc in range(16):
                    mm(bk[:, 0:256], hT[:, kc, tb * 128:(tb + 1) * 128], wv[:, kc, :], start=(kc == 0), stop=(kc == 15), reads=[wr, r_h[kc]], writes=[br])
                cp(vtok[:, tb, (b - 4) * 256:(b - 3) * 256], bk[:, 0:256], [br], [r_v[tb]], eng=alt())
        for b in (6, 7):
            wv, wr = wload2("w_in_main", (l, b), 16, 256)
            for half in range(2):
                h = (b - 6) * 2 + half
                bk, br = proj_fm(wv, wr, half * 128, N, hT, r_h)
                i2 = h % 2
                act(raw[:, i2 * N:(i2 + 1) * N], bk[:, 0:N], AF.Silu, reads=[br], writes=[r_raw[i2]])
                ts(gate[:, h, :], raw[:, i2 * N:(i2 + 1) * N], c_gnw[:, l, h:h + 1], ALU.mult, reads=[r_raw[i2], r_const], writes=[r_g[h]])
        NCH = N // 128
        kw, r_kw = aal(NCH * 512, BF16, nreg=NCH); sm, r_sm = aal(NCH * 512, BF16, nreg=NCH)
        kvs, r_kvs = aal(NCH * 512, nreg=NCH)
        gnt = gn_tmps(512)
        S = S_ret[:, l, :, :]; rS = r_Sret[l]
        cp(Sb_ret[:, :, :], S, [rS], [r_Sbret])
        v4 = lambda a_: a_.rearrange("p (h d) -> p h d", d=128)
        for c in range(NCH):
            cs = slice(c * 128, (c + 1) * 128)
            kwv = v4(kw[:, c * 512:(c + 1) * 512]); smv = v4(sm[:, c * 512:(c + 1) * 512])
            bt, btb, rbt = pb()
            btv = v4(btb[:, 0:512])
            for h in range(4):
                tr(btv[:, h, :], kT[:, h, cs], c_identb[:, :], [r_k[h], r_const], [rbt])
            tt(kwv, btv, bc_last(c_kwcol[:, :], 128), ALU.mult, [rbt, r_const], [r_kw[c]])
            bs, _, rbs = pb()
            bsv = v4(bs[:, :])
            for h in range(4):
                mm(bsv[:, h, :], kT[:, h, cs], qT[:, h, cs], reads=[r_k[h], r_q[h]], writes=[rbs])
            tt(smv, bsv, c_maskT[:, :, :], ALU.mult, [rbs, r_const], [r_sm[c]])
        for c in range(NCH):
            kwv = v4(kw[:, c * 512:(c + 1) * 512])
            bkv, _, rbkv = pb()
            bkvv = v4(bkv[:, :])
            for h in range(4):
                mm(bkvv[:, h, :], kwv[:, h, :], vtok[:, c, h * 128:(h + 1) * 128], reads=[r_kw[c], r_v[c]], writes=[rbkv])
            cp(kvs[:, c * 512:(c + 1) * 512], bkv[:, :], [rbkv], [r_kvs[c]])
        for c in range(NCH):
            cs = slice(c * 128, (c + 1) * 128)
            smv = v4(sm[:, c * 512:(c + 1) * 512]); kvv = v4(kvs[:, c * 512:(c + 1) * 512])
            bo, _, rbo = pb()
            bov = v4(bo[:, :])
            for h in range(4):
                mm(bov[:, h, :], vtok[:, c, h * 128:(h + 1) * 128], smv[:, h, :], start=True, stop=False, reads=[r_v[c], r_sm[c]], writes=[rbo])
                mm(bov[:, h, :], Sb_ret[:, h, :], qwT[:, h, cs], start=False, stop=True, reads=[r_Sbret, r_qw[h]], writes=[rbo])
            for h in range(4):
                stt(S[:, h, :], S[:, h, :], C["ret_gC"][h], kvv[:, h, :], ALU.mult, ALU.add, reads=[rS, r_kvs[c]], writes=[rS])
            cp(Sb_ret[:, :, :], S, [rS], [r_Sbret])
            groupnorm_cols(bo[:, :], rbo, 512, c_o128[:, :], LN_EPS, True,
                           mixT[:, 0:4, cs], r_mix[0:4], gate[:, :, cs], r_g, gnt)
        if it == NT - 1:
            P.dma(retp_d[l].rearrange("h d e -> d h e"), S, reads=[rS], is_output=True)

    class _Stop(Exception):
        pass

    def gdn_prompt(l, it):
        phase()
        N = T
        NB = N // 128
        qT, r_q = aal(8 * N, BF16, nreg=8); qT = qT.rearrange("p (h n) -> p h n", n=N)
        kT, r_k = aal(8 * N, BF16, nreg=8); kT = kT.rearrange("p (h n) -> p h n", n=N)
        vT, r_v = aal(8 * N, BF16, nreg=8); vT = vT.rearrange("p (h n) -> p h n", n=N)
        cols, r_cols = aal(NB * 6 * 8, nreg=NB); cols = cols.rearrange("p (b q h) -> p b q h", q=6, h=8)
        gcT, r_gcT = aal(NB * 128, nreg=NB); gcT = gcT.rearrange("p (b n) -> p b n", n=128)
        keep = _ar["off"]
        xin, r_xin = aal(2 * (N + 3), nreg=2); yb, r_yb = aal(2 * N, nreg=2); sq, r_sq = aal(2 * N, nreg=2)
        cw = c_convw
        pend = []

        def norm_tail(ch, yv, r_y, sv, r_s):
            h = ch % 8
            isq = ch < 8
            bq, _, rq = pb()
            mm(bq[:, 0:N], (c_odk if isq else c_o1)[:, :], sv, reads=[r_s, r_const], writes=[rq])
            act(sv, bq[:, 0:N], AF.Ln, bias=(128.0 * L2_EPS if isq else L2_EPS), reads=[rq], writes=[r_s])
            act(sv, sv, AF.Exp, scale=-0.5, reads=[r_s], writes=[r_s])
            dst, rd = (qT, r_q) if isq else (kT, r_k)
            tt(dst[:, h, :], yv, sv, ALU.mult, [r_y, r_s], [rd[h]])
        for b in range(8, 20):
            wv, wr = wload2("w_in_main", (l, b), 16, 256)
            for half in range(2):
                ch = (b - 8) * 2 + half
                h = ch % 8
                bk, br = proj_fm(wv, wr, half * 128, N, hT, r_h)
                if pend:
                    norm_tail(*pend.pop())
                i2 = ch % 2
                xv = xin[:, i2 * (N + 3):(i2 + 1) * (N + 3)]; yv = yb[:, i2 * N:(i2 + 1) * N]; sv = sq[:, i2 * N:(i2 + 1) * N]
                cp(xv[:, 0:3], convc[:, l, ch, :], [r_convc[l]], [r_xin[i2]], eng="dve")
                cp(xv[:, 3:N + 3], bk[:, 0:N], [br], [r_xin[i2]])
                cp(convc[:, l, ch, :], xv[:, N:N + 3], [r_xin[i2]], [r_convc[l]], eng="dve")
                ts(yv, xv[:, 0:N], cw[:, l, ch, 0:1], ALU.mult, reads=[r_xin[i2], r_const], writes=[r_yb[i2]])
                for i in range(1, 4):
                    stt(yv, xv[:, i:N + i], cw[:, l, ch, i:i + 1], yv, ALU.mult, ALU.add, reads=[r_xin[i2], r_const, r_yb[i2]], writes=[r_yb[i2]])
                if ch >= 16:
                    act(vT[:, h, :], yv, AF.Silu, reads=[r_yb[i2]], writes=[r_v[h]])
                else:
                    act(yv, yv, AF.Silu, reads=[r_yb[i2]], writes=[r_yb[i2]])
                    act(sv, yv, AF.Square, reads=[r_yb[i2]], writes=[r_sq[i2]])
                    pend.append((ch, yv, r_yb[i2], sv, r_sq[i2]))
        for b in range(20, 24):
            wv, wr = wload2("w_in_main", (l, b), 16, 256)
            for half in range(2):
                h = (b - 20) * 2 + half
                bk, br = proj_fm(wv, wr, half * 128, N, hT, r_h)
                i2 = h % 2
                act(yb[:, i2 * N:(i2 + 1) * N], bk[:, 0:N], AF.Silu, reads=[br], writes=[r_yb[i2]])
                ts(mixT[:, 4 + h, :], yb[:, i2 * N:(i2 + 1) * N], c_gdnw[:, l:l + 1], ALU.mult, reads=[r_yb[i2], r_const], writes=[r_mix[4 + h]])
        if stop_at == "gA":
            raise _Stop()
        wab, r_wab = aal(256, BF16)
        wabv = wab.rearrange("p (k c) -> p k c", c=16)
        P.dma(wabv, w_ab_d[l], writes=[r_wab], queue="pool")
        ab, r_ab = aal(2 * 16, nreg=2)
        for tb in range(NB):
            bk, _, br = pb()
            for kc in range(16):
                mm(bk[:, 0:16], hT[:, kc, tb * 128:(tb + 1) * 128], wabv[:, kc, :], start=(kc == 0), stop=(kc == 15), reads=[r_wab, r_h[kc]], writes=[br])
            i2 = tb % 2
            av = ab[:, i2 * 16:(i2 + 1) * 16]; ra = r_ab[i2]
            cv = cols[:, tb, :, :]; rc = r_cols[tb]
            tt(av[:, 0:8], bk[:, 0:8], c_dtb[:, l, :], ALU.add, [br, r_const], [ra])
            act(av[:, 0:8], av[:, 0:8], AF.Exp, reads=[ra], writes=[ra])
            act(av[:, 0:8], av[:, 0:8], AF.Ln, bias=1.0, reads=[ra], writes=[ra])
            tt(av[:, 0:8], av[:, 0:8], c_nega[:, l, :], ALU.mult, [ra, r_const], [ra])
            act(cv[:, 0, :], bk[:, 8:16], AF.Sigmoid, reads=[br], writes=[rc])
            ts(cv[:, 1, :], cv[:, 0, :], -1.0, ALU.mult, reads=[rc], writes=[rc])
            b2, _, br2 = pb()
            mm(b2[:, 0:8], c_tri[:, :], av[:, 0:8], reads=[ra, r_const], writes=[br2])
            mm(b2[0:8, 128:256], av[:, 0:8], c_tri[:, :], reads=[ra, r_const], writes=[br2])
            mm(b2[:, 256:264], c_blk1[:, :], av[:, 0:8], reads=[ra, r_const], writes=[br2])
            cp(cv[:, 2, :], b2[:, 0:8], [br2], [rc], eng="dve")
            cp(gcT[0:8, tb, :], b2[0:8, 128:256], [br2], [r_gcT[tb]], eng="dve")
            act(cv[:, 3, :], b2[:, 0:8], AF.Exp, reads=[br2], writes=[rc])
            tt(cv[:, 3, :], cv[:, 3, :], cv[:, 0, :], ALU.mult, [rc], [rc])
            tt(cv[:, 4, :], b2[:, 256:264], cv[:, 2, :], ALU.subtract, [br2, rc], [rc])
            act(cv[:, 4, :], cv[:, 4, :], AF.Exp, reads=[rc], writes=[rc])
        if stop_at == "gB":
            raise _Stop()
        phase(keep)
        S = S_gdn[:, l, :, :]; rS = r_Sgdn[l]
        cp(Sb_gdn[:, :, :], S, [rS], [r_Sbgdn])
        v3 = lambda a: a.rearrange("p (h d) -> p h d", d=128)
        ktok_g, r_ktg = aal(1024, BF16); ktok_d, r_ktd = aal(1024, BF16); vtok, r_vt = aal(1024, BF16)
        ktok_g = v3(ktok_g); ktok_d = v3(ktok_d); vtok = v3(vtok)
        EG, r_EG = aal(1024, nreg=2); EG = v3(EG)
        dUf, r_dU = aal(512); dLf, r_dL = aal(512)
        dU = v3(dUf); dL = v3(dLf)
        NB_ = [[aal(512) for _ in range(4)] for _ in range(2)]
        YB_ = [(aal(512), aal(512, BF16)) for _ in range(2)]
        attnT, r_at = aal(1024, BF16, nreg=2); attnT = v3(attnT)
        qdT, r_qd = aal(1024, BF16, nreg=2); qdT = v3(qdT)
        u_sb, r_u = aal(1024, nreg=2); u_sb = v3(u_sb)
        wTn, r_w = aal(1024, BF16, nreg=2); wTn = v3(wTn)
        vnew, r_vn = aal(1024, BF16); vnew = v3(vnew)
        o_sb, r_o = aal(1024, nreg=2); o_sb = v3(o_sb)
        for cpi in range(NB):
            cs = slice(cpi * 128, (cpi + 1) * 128)
            cv = cols[:, cpi, :, :]; rc = r_cols[cpi]
            for (src, rsrc, outs) in ((kT, r_k, "k"), (vT, r_v, "v")):
                for hg in range(2):
                    bt, btb, rbt = pb()
                    btv = btb[:, 0:512].rearrange("p (h d) -> p h d", d=128)
                    for h4 in range(4):
                        h = hg * 4 + h4
                        tr(btv[:, h4, :], src[:, h, cs], c_identb[:, :], [rsrc[h], r_const], [rbt])
                    hs = slice(hg * 4, hg * 4 + 4)
                    if outs == "k":
                        tt(ktok_g[:, hs, :], btv, bc_last(cv[:, 3, hs], 128), ALU.mult, [rbt, rc], [r_ktg])
                        tt(ktok_d[:, hs, :], btv, bc_last(cv[:, 4, hs], 128), ALU.mult, [rbt, rc], [r_ktd])
                    else:
                        tt(vtok[:, hs, :], btv, bc_last(cv[:, 0, hs], 128), ALU.mult, [rbt, rc], [r_vt])
            for hg in range(2):
                hs = slice(hg * 4, hg * 4 + 4)
                (NmA, r_NmA), (LmA, r_LmA) = NB_[hg][0], NB_[hg][1]
                (Y, r_Y), _yb = YB_[hg]
                bR, _, rR = pb(); bRv = v3(bR[:, :])
                for h4 in range(4):
                    h = hg * 4 + h4
                    mm(bRv[:, h4, :], col_bc1(c_ident[0:8, h:h + 1], 128), gcT[0:8, cpi, :], reads=[r_gcT[cpi], r_const], writes=[rR])
                act(EG[:, hs, :], bRv, AF.Exp, reads=[rR], writes=[r_EG[hg]])
                for h4 in range(4):
                    h = hg * 4 + h4
                    stt(dU[:, h4, :], bRv[:, h4, :], cv[:, 2, h:h + 1], c_maskU[:, :], ALU.subtract, ALU.add, reads=[rR, rc, r_const], writes=[r_dU])
                    stt(dL[:, h4, :], bRv[:, h4, :], cv[:, 2, h:h + 1], c_maskL[:, :], ALU.subtract, ALU.add, reads=[rR, rc, r_const], writes=[r_dL])
                act(dU, dU, AF.Exp, reads=[r_dU], writes=[r_dU])
                act(dL, dL, AF.Exp, scale=-1.0, reads=[r_dL], writes=[r_dL])
                bK, _, rK = pb(); bKv = v3(bK[:, :])
                for h4 in range(4):
                    h = hg * 4 + h4
                    mm(bKv[:, h4, :], kT[:, h, cs], kT[:, h, cs], reads=[r_k[h]], writes=[rK])
                Nm = v3(NmA); Lm = v3(LmA)
                for h4 in range(4):
                    h = hg * 4 + h4
                    stt(Nm[:, h4, :], bKv[:, h4, :], cv[:, 1, h:h + 1], dL[:, h4, :], ALU.mult, ALU.mult, reads=[rK, rc, r_dL], writes=[r_NmA])
                bL, _, rL = pb(); bLv = v3(bL[:, :])
                for h4 in range(4):
                    tr(bLv[:, h4, :], Nm[:, h4, :], c_ident[:, :], [r_NmA, r_const], [rL])
                cp(Lm, bLv, [rL], [r_LmA])
                bQ, _, rQ = pb(); bQv = v3(bQ[:, :])
                for h4 in range(4):
                    h = hg * 4 + h4
                    mm(bQv[:, h4, :], kT[:, h, cs], qT[:, h, cs], reads=[r_k[h], r_q[h]], writes=[rQ])
                tt(attnT[:, hs, :], bQv, dU, ALU.mult, [rQ, r_dU], [r_at[hg]])
                tt(qdT[:, hs, :], qT[:, hs, cs], EG[:, hs, :], ALU.mult, r_q[hg * 4:hg * 4 + 4] + [r_EG[hg]], [r_qd[hg]])
                tt(v3(Y), Lm, bc_mid(c_ident[:, :], 4), ALU.add, [r_LmA, r_const], [r_Y])
            cur = [(NB_[hg][0], NB_[hg][1]) for hg in range(2)]
            nxt = [(NB_[hg][2], NB_[hg][3]) for hg in range(2)]
            for lev in range(1, 6):
                for hg in range(2):
                    (cNf, rN), (cLf, rLm) = cur[hg]; (nNf, rN2), (nLf, rL2) = nxt[hg]
                    cN = v3(cNf); cL = v3(cLf); nN = v3(nNf); nL = v3(nLf)
                    bb, _, rbb = pb(); bbv = v3(bb[:, :])
                    for h4 in range(4):
                        mm(bbv[:, h4, :], cL[:, h4, :], cN[:, h4, :], reads=[rN, rLm], writes=[rbb])
                    cp(nN, bbv, [rbb], [rN2], eng="dve")
                    if lev < 5:
                        ba, _, rba = pb(); bav = v3(ba[:, :])
                        for h4 in range(4):
                            mm(bav[:, h4, :], cN[:, h4, :], cL[:, h4, :], reads=[rN, rLm], writes=[rba])
                        cp(nL, bav, [rba], [rL2])
                for hg in range(2):
                    (nNf, rN2), _ = nxt[hg]
                    (Y, r_Y), _yb = YB_[hg]
                    nN = v3(nNf); Yv = v3(Y)
                    bc_, _, rbc = pb(); bcv = v3(bc_[:, :])
                    for h4 in range(4):
                        mm(bcv[:, h4, :], nN[:, h4, :], Yv[:, h4, :], reads=[rN2, r_Y], writes=[rbc])
                    tt(Yv, Yv, bcv, ALU.add, [r_Y, rbc], [r_Y])
                cur, nxt = nxt, cur
            for hg in range(2):
                hs = slice(hg * 4, hg * 4 + 4)
                (Y, r_Y), (Yb, r_Yb) = YB_[hg]
                cp(v3(Yb), v3(Y), [r_Y], [r_Yb])
                Ybv = v3(Yb)
                bU, _, rU = pb(); bUv = v3(bU[:, :])
                for h4 in range(4):
                    h = hg * 4 + h4
                    mm(bUv[:, h4, :], Ybv[:, h4, :], vtok[:, h, :], reads=[r_Yb, r_vt], writes=[rU])
                cp(u_sb[:, hs, :], bUv, [rU], [r_u[hg]])
                bW, _, rW = pb(); bWv = v3(bW[:, :])
                for h4 in range(4):
                    h = hg * 4 + h4
                    mm(bWv[:, h4, :], ktok_g[:, h, :], Ybv[:, h4, :], reads=[r_Yb, r_ktg], writes=[rW])
                amul(wTn[:, hs, :], bWv, -1.0, reads=[rW], writes=[r_w[hg]])
            for ch in range(2):
                rr = slice(ch * 64, (ch + 1) * 64)
                for hg in range(2):
                    bws, _, rws = pb(); bwsv = v3(bws[:, :])
                    for h4 in range(4):
                        h = hg * 4 + h4
                        mm(bwsv[rr, h4, :], wTn[:, h, rr], Sb_gdn[:, h, :], reads=[r_w[hg], r_Sbgdn], writes=[rws])
                    hs = slice(hg * 4, hg * 4 + 4)
                    tt(vnew[rr, hs, :], u_sb[rr, hs, :], bwsv[rr, :, :], ALU.add, [r_u[hg], rws], [r_vn])
                for hg in range(2):
                    bo, _, rbo = pb(); bov = bo[:, 0:256].rearrange("p (h i) -> p h i", i=64)
                    for h4 in range(4):
                        h = hg * 4 + h4
                        mm(bov[:, h4, :], Sb_gdn[:, h, :], qdT[:, h, rr], start=True, stop=False, reads=[r_Sbgdn, r_qd[hg]], writes=[rbo])
                        mm(bov[:, h4, :], vnew[rr, h, :], attnT[rr, h, rr], start=False, stop=True, reads=[r_vn, r_at[hg]], writes=[rbo])
                    hs = slice(hg * 4, hg * 4 + 4)
                    cp(o_sb[:, hs, rr], bov, [rbo], [r_o[hg]])
                for hg in range(2):
                    bkd, _, rkd = pb(); bkdv = v3(bkd[:, :])
                    for h4 in range(4):
                        h = hg * 4 + h4
                        mm(bkdv[:, h4, :], ktok_d[rr, h, :], vnew[rr, h, :], reads=[r_ktd, r_vn], writes=[rkd])
                    for h4 in range(4):
                        h = hg * 4 + h4
                        last = ch * 64 + 63
                        stt(S[:, h, :], S[:, h, :], EG[:, h, last:last + 1], bkdv[:, h4, :], ALU.mult, ALU.add, reads=[rS, r_EG[hg], rkd], writes=[rS])
                cp(Sb_gdn[:, :, :], S, [rS], [r_Sbgdn])
            for hg in range(2):
                hs = slice(hg * 4, hg * 4 + 4)
                sq2, r_sq2 = (dUf, r_dU) if hg == 0 else (dLf, r_dL)
                act(sq2, o_sb[:, hs, :].rearrange("p h d -> p (h d)"), AF.Square, reads=[r_o[hg]], writes=[r_sq2])
                bq, _, rq = pb()
                mm(bq[:, :], c_o128[:, :], sq2, reads=[r_sq2, r_const], writes=[rq])
                act(sq2, bq[:, :], AF.Ln, bias=RMS_EPS, reads=[rq], writes=[r_sq2])
                act(sq2, sq2, AF.Exp, scale=-0.5, reads=[r_sq2], writes=[r_sq2])
                tt(sq2, sq2, o_sb[:, hs, :].rearrange("p h d -> p (h d)"), ALU.mult, [r_sq2, r_o[hg]], [r_sq2])
                mx = mixT[:, 4 + hg * 4:8 + hg * 4, cs]
                tt(mx, v3(sq2), mx, ALU.mult, [r_sq2] + r_mix[4 + hg * 4:8 + hg * 4], r_mix[4 + hg * 4:8 + hg * 4])
        if it == NT - 1:
            P.dma(gdnp_d[l].rearrange("h d e -> d h e"), S, reads=[rS], is_output=True)
            P.dma(convp_d[l], convc[:, l, :, :], reads=[r_convc[l]], is_output=True)

    def s5_glu(l, N, yg, r_yg, ygb, r_ygb, tmps=None):
        wv, wr = wload2("w_glu_b", (l,), 4, 512)
        if tmps is None:
            sgl, r_sgl = aal(2 * N, nreg=2)
            tmps = [(sgl[:, 0:N], r_sgl[0]), (sgl[:, N:2 * N], r_sgl[1])]
        for jo in range(4):
            bk, _, br = pb()
            for kc in range(4):
                mm(bk[:, 0:N], wv[:, kc, jo * 128:(jo + 1) * 128], ygb[:, kc, :], start=(kc == 0), stop=(kc == 3), reads=[wr, r_ygb[kc]], writes=[br])
            sv, rsv = tmps[jo % 2]
            act(sv, bk[:, 0:N], AF.Sigmoid, reads=[br], writes=[rsv])
            tt(mixT[:, 12 + jo, 0:N], yg[:, jo, :], sv, ALU.mult, [r_yg[jo], rsv], [r_mix[12 + jo]])

    def s5_prompt(l, it):
        phase()
        N = T
        su32, r_su32 = aal(4 * N, nreg=4); su32 = su32.rearrange("p (k n) -> p k n", n=N)
        sub, r_sub = aal(4 * N, BF16, nreg=4); sub = sub.rearrange("p (k n) -> p k n", n=N)
        for b in range(2):
            wv, wr = wload2("w_in_su", (l, b), 16, 256)
            for half in range(2):
                kc = b * 2 + half
                bk, br = proj_fm(wv, wr, half * 128, N, hT, r_h)
                cp(su32[:, kc, :], bk[:, 0:N], [br], [r_su32[kc]])
                cp(sub[:, kc, :], bk[:, 0:N], [br], [r_sub[kc]], eng="dve")
        BT, r_BT = aal(2 * 2048, BF16); BT = BT.rearrange("p (c k s) -> p c k s", c=2, k=16)
        P.dma(BT.rearrange("p c k s -> p c (k s)"), bt_scr[l].rearrange("c p k s -> p c (k s)"), reads=[r_btscr], writes=[r_BT])
        CTr, r_CTr = aal(2048, BF16); CTi, r_CTi = aal(2048, BF16)
        CTr = CTr.rearrange("p (k s) -> p k s", k=16); CTi = CTi.rearrange("p (k s) -> p k s", k=16)
        P.dma(CTr, cpr_d[l], writes=[r_CTr], queue="pool"); P.dma(CTi, cpi_d[l], writes=[r_CTi], queue="pool")
        tabb, r_tab = aal(2 * 4 * N, nreg=2)
        bufs = [aal(N) for _ in range(4)]
        hbuf = [aal(N) for _ in range(4)]
        hb, r_hb = aal(4 * N, BF16, nreg=4)
        yg, r_yg = aal(4 * N, nreg=4); yg = yg.rearrange("p (k n) -> p k n", n=N)
        ygb, r_ygb = aal(4 * N, BF16, nreg=4); ygb = ygb.rearrange("p (k n) -> p k n", n=N)
        by = None
        ones_b = col_bc1(c_o1[:, 0:1], N)
        for k in range(16):
            kc = k // 4
            i2 = k % 2
            tab = tabb[:, i2 * 4 * N:(i2 + 1) * 4 * N].rearrange("p (c n) -> p c n", n=N); rt = r_tab[i2]
            P.dma(tab, tab_scr[l, :, :, k, :].rearrange("c p n -> p c n"), reads=[r_tabscr], writes=[rt])
            EPr, EPi, ENr, ENi = tab[:, 0, :], tab[:, 1, :], tab[:, 2, :], tab[:, 3, :]
            bre, _, rbre = pb(); bim, _, rbim = pb()
            mm(bre[:, 0:N], BT[:, 0, k, :], sub[:, kc, :], reads=[r_BT, r_sub[kc]], writes=[rbre])
            mm(bim[:, 0:N], BT[:, 1, k, :], sub[:, kc, :], reads=[r_BT, r_sub[kc]], writes=[rbim])
            (A, rA), (B, rB), (XR, rXR), (XI, rXI) = bufs
            ytmp, r_ytmp = A, rA
            (HR, rHR), (HI, rHI) = hbuf[i2 * 2], hbuf[i2 * 2 + 1]
            tt(A, ENr, bre[:, 0:N], ALU.mult, [rt, rbre], [rA]); tt(B, ENi, bim[:, 0:N], ALU.mult, [rt, rbim], [rB])
            tt(XR, A, B, ALU.subtract, [rA, rB], [rXR])
            tt(A, ENr, bim[:, 0:N], ALU.mult, [rt, rbim], [rA]); tt(B, ENi, bre[:, 0:N], ALU.mult, [rt, rbre], [rB])
            tt(XI, A, B, ALU.add, [rA, rB], [rXI])
            P.op("dve", lambda e, A=A, XR=XR, c0=ssmc[:, l, k, 0:1]: e.tensor_tensor_scan(A, ones_b, XR, c0, ALU.mult, ALU.add), [rXR, r_ssmc[l][k], r_const], [rA])
            P.op("dve", lambda e, B=B, XI=XI, c0=ssmc[:, l, k, 1:2]: e.tensor_tensor_scan(B, ones_b, XI, c0, ALU.mult, ALU.add), [rXI, r_ssmc[l][k], r_const], [rB])
            tt(XR, EPr, A, ALU.mult, [rt, rA], [rXR]); tt(XI, EPi, B, ALU.mult, [rt, rB], [rXI])
            tt(HR, XR, XI, ALU.subtract, [rXR, rXI], [rHR])
            tt(XR, EPr, B, ALU.mult, [rt, rB], [rXR]); tt(XI, EPi, A, ALU.mult, [rt, rA], [rXI])
            tt(HI, XR, XI, ALU.add, [rXR, rXI], [rHI])
            cp(ssmc[:, l, k, 0:1], HR[:, N - 1:N], [rHR], [r_ssmc[l][k]], eng="dve")
            cp(ssmc[:, l, k, 1:2], HI[:, N - 1:N], [rHI], [r_ssmc[l][k]], eng="dve")
            hbr = hb[:, (i2 * 2) * N:(i2 * 2 + 1) * N]; hbi = hb[:, (i2 * 2 + 1) * N:(i2 * 2 + 2) * N]
            cp(hbr, HR, [rHR], [r_hb[i2 * 2]])
            amul(hbi, HI, -1.0, reads=[rHI], writes=[r_hb[i2 * 2 + 1]])
            if k % 4 == 0:
                by, _, rby = pb_res(7)
            mm(by[:, 0:N], CTr[:, k, :], hbr, start=(k % 4 == 0), stop=False, reads=[r_CTr, r_hb[i2 * 2]], writes=[rby])
            mm(by[:, 0:N], CTi[:, k, :], hbi, start=False, stop=(k % 4 == 3), reads=[r_CTi, r_hb[i2 * 2 + 1]], writes=[rby])
            if k % 4 == 3:
                stt(ytmp, su32[:, kc, :], c_ssmd[:, l, kc:kc + 1], by[:, 0:N], ALU.mult, ALU.add, reads=[r_su32[kc], r_const, rby], writes=[r_ytmp])
                act(yg[:, kc, :], ytmp, AF.Gelu, reads=[r_ytmp], writes=[r_yg[kc]])
                cp(ygb[:, kc, :], yg[:, kc, :], [r_yg[kc]], [r_ygb[kc]], eng="dve")
        s5_glu(l, N, yg, r_yg, ygb, r_ygb, tmps=[bufs[2], bufs[3]])
        if it == NT - 1:
            P.dma(ssmp_d[l], ssmc[:, l, :, :], reads=r_ssmc[l], is_output=True)


    scr_l = P.dram("scr_l", [2, 2048], F32)
    r_scrs = P.region("scr_s"); r_scrl = P.region("scr_l")

    def col_bc(a, n):
        return bass.AP(a.tensor, a.offset, [list(a.ap[0]), [0, n]])

    def tokproj(l, blocks, dst, r_dst):
        for i, b in enumerate(blocks):
            wv, wr = wload2("w_in_main", (l, b), 16, 256)
            bk, _, br = pb()
            for kc in range(16):
                mm(bk[0:NS, 0:256], hT[:, kc, 0:NS], wv[:, kc, :], start=(kc == 0), stop=(kc == 15), reads=[wr, r_h[kc]], writes=[br])
            cp(dst[0:NS, i * 256:(i + 1) * 256], bk[0:NS, 0:256], [br], [r_dst], eng=alt())

    def ret_sample(l):
        phase()
        N = NS
        ptok, r_pt = aal(1536)
        tokproj(l, range(6), ptok, r_pt)
        csr, r_csr = aal(256)
        P.dma(csr[0:NS, 0:128], cd["rope_cos_s"].ap().rearrange("d o -> (d o)").partition_broadcast(NS), writes=[r_csr])
        P.dma(csr[0:NS, 128:256], cd["rope_sin_s"].ap().rearrange("d o -> (d o)").partition_broadcast(NS), writes=[r_csr])
        qk = ptok[0:NS, 0:1024].rearrange("p (g d) -> p g d", d=128)
        tq, r_tq = aal(1024); tqv = tq[0:NS, :].rearrange("p (g d) -> p g d", d=128)
        rq, r_rq = aal(1024); rqv = rq[0:NS, :].rearrange("p (g d) -> p g d", d=128)
        tt(tqv[:, :, 0:64], qk[:, :, 64:128], bc_mid(csr[0:NS, 128:192], 8), ALU.mult, [r_pt, r_csr], [r_tq])
        tt(tqv[:, :, 64:128], qk[:, :, 0:64], bc_mid(csr[0:NS, 192:256], 8), ALU.mult, [r_pt, r_csr], [r_tq])
        tt(rqv, qk, bc_mid(csr[0:NS, 0:128], 8), ALU.mult, [r_pt, r_csr], [r_rq])
        tt(rqv, rqv, tqv, ALU.add, [r_rq, r_tq], [r_rq])
        ts(rqv[:, 4:8, :], rqv[:, 4:8, :], 128 ** -0.5, ALU.mult, reads=[r_rq], writes=[r_rq])
        bk, _, br = pb()
        for g in range(8):
            tr(bk[:, g * NS:(g + 1) * NS], rqv[:, g, :], c_ident[0:NS, 0:NS], [r_rq, r_const], [br])
        qkT, r_qkT = aal(8 * NS); qkT = qkT.rearrange("p (g s) -> p g s", s=NS)
        cp(qkT, bk[:, 0:8 * NS].rearrange("p (g s) -> p g s", s=NS), [br], [r_qkT])
        P.dma(scr_s[:, 0:512], ptok[0:NS, 1024:1536], reads=[r_pt], writes=[r_scrs])
        gate, r_gate = aal(4 * NS); gate = gate.rearrange("p (h s) -> p h s", s=NS)
        gtmp, r_gtmp = aal(NS)
        for b in (6, 7):
            wv, wr = wload2("w_in_main", (l, b), 16, 256)
            for half in range(2):
                h = (b - 6) * 2 + half
                bk2, br2 = proj_fm(wv, wr, half * 128, N, hT, r_h)
                act(gtmp, bk2[:, 0:N], AF.Silu, reads=[br2], writes=[r_gtmp])
                ts(gate[:, h, :], gtmp, c_gnw[:, l, h:h + 1], ALU.mult, reads=[r_gtmp, r_const], writes=[r_gate])
        po, _, rpo = pb_res(7)
        S0b, r_S0 = aal(2 * 512, nreg=2); vbb, r_vb = aal(2 * 512, nreg=2); Snb, r_Sn = aal(2 * 512, nreg=8)
        for s_ in range(NS):
            i2 = s_ % 2
            S0 = S0b[:, i2 * 512:(i2 + 1) * 512].rearrange("p (h e) -> p h e", e=128)
            Sn = Snb[:, i2 * 512:(i2 + 1) * 512].rearrange("p (h e) -> p h e", e=128)
            vb = vbb[:, i2 * 512:(i2 + 1) * 512]
            P.dma(S0, st_ret_d[l, s_].rearrange("h d e -> d h e"), writes=[r_S0[i2]])
            P.dma(vb, scr_s[s_, 0:512].partition_broadcast(128), reads=[r_scrs], writes=[r_vb[i2]])
            for h in range(4):
                ts(Sn[:, h, :], vb[:, h * 128:(h + 1) * 128], qkT[:, 4 + h, s_:s_ + 1], ALU.mult, reads=[r_vb[i2], r_qkT], writes=[r_Sn[i2 * 4 + h]])
                stt(Sn[:, h, :], S0[:, h, :], C["ret_g1"][h], Sn[:, h, :], ALU.mult, ALU.add, reads=[r_S0[i2], r_Sn[i2 * 4 + h]], writes=[r_Sn[i2 * 4 + h]])
                mm(po[:, h * NS + s_:h * NS + s_ + 1], Sn[:, h, :], qkT[:, h, s_:s_ + 1], reads=[r_Sn[i2 * 4 + h], r_qkT], writes=[rpo])
            P.dma(rets_d[l, s_].rearrange("h d e -> d h e"), Sn, reads=r_Sn[i2 * 4:i2 * 4 + 4], is_output=True)
        groupnorm_cols(po[:, 0:4 * NS], rpo, 4 * NS, c_o128[:, :], LN_EPS, True,
                       mixT[:, 0:4, 0:NS], r_mix[0:4], gate, [r_gate], gn_tmps(4 * NS))

    def gdn_sample(l):
        phase()
        N = NS
        xin, r_xin = aal(3072)
        tokproj(l, range(8, 20), xin, r_xin)
        yq, r_yq = aal(3072)
        mark1 = _ar["off"]
        bufb, r_buf = aal(2 * 1536, nreg=2); wrb, r_wr = aal(2 * 2048, nreg=2); ytb, r_yt = aal(2 * 512, nreg=2)
        for cc in range(6):
            i2 = cc % 2
            cs_ = slice(cc * 512, (cc + 1) * 512)
            buf = bufb[0:NS, i2 * 1536:(i2 + 1) * 1536].rearrange("p (i c) -> p i c", c=512)
            wr_ = wrb[0:NS, i2 * 2048:(i2 + 1) * 2048].rearrange("p (i c) -> p i c", c=512)
            yt = ytb[0:NS, i2 * 512:(i2 + 1) * 512]
            P.dma(buf, st_conv_d[l, :, :, cs_], writes=[r_buf[i2]])
            for i in range(4):
                P.dma(wr_[:, i, :], convw_raw_d[l, i, cs_].partition_broadcast(NS), writes=[r_wr[i2]])
            P.dma(convs_d[l, :, 0:2, cs_], buf[:, 1:3, :], reads=[r_buf[i2]], is_output=True)
            P.dma(convs_d[l, :, 2, cs_], xin[0:NS, cs_], reads=[r_xin], is_output=True)
            tt(yt, xin[0:NS, cs_], wr_[:, 3, :], ALU.mult, [r_xin, r_wr[i2]], [r_yt[i2]])
            tt(buf, buf, wr_[:, 0:3, :], ALU.mult, [r_buf[i2], r_wr[i2]], [r_buf[i2]])
            for i in range(3):
                tt(yt, yt, buf[:, i, :], ALU.add, [r_yt[i2], r_buf[i2]], [r_yt[i2]])
            act(yq[0:NS, cs_], yt, AF.Silu, reads=[r_yt[i2]], writes=[r_yq])
        phase(mark1)
        sq, r_sq = aal(2048); ssq, r_ssq = aal(16)
        y3 = yq[0:NS, 0:2048].rearrange("p (g d) -> p g d", d=128)
        tt(sq[0:NS, :], yq[0:NS, 0:2048], yq[0:NS, 0:2048], ALU.mult, [r_yq], [r_sq])
        P.op("dve", lambda e: e.tensor_reduce(ssq[0:NS, :], sq[0:NS, :].rearrange("p (g d) -> p g d", d=128), mybir.AxisListType.X, ALU.add), [r_sq], [r_ssq])
        act(ssq[0:NS, :], ssq[0:NS, :], AF.Sqrt, bias=L2_EPS, reads=[r_ssq], writes=[r_ssq])
        recip(ssq[0:NS, :], ssq[0:NS, :], [r_ssq], [r_ssq])
        ts(ssq[0:NS, 0:8], ssq[0:NS, 0:8], 128 ** -0.5, ALU.mult, reads=[r_ssq], writes=[r_ssq])
        sq3 = sq[0:NS, :].rearrange("p (g d) -> p g d", d=128)
        tt(sq3, y3, bc_last(ssq[0:NS, :], 128), ALU.mult, [r_yq, r_ssq], [r_sq])
        bk, _, br = pb()
        for g in range(16):
            tr(bk[:, g * NS:(g + 1) * NS], sq3[:, g, :], c_ident[0:NS, 0:NS], [r_sq, r_const], [br])
        qkT, r_qkT = aal(16 * NS); qkT = qkT.rearrange("p (g s) -> p g s", s=NS)
        cp(qkT, bk[:, 0:16 * NS].rearrange("p (g s) -> p g s", s=NS), [br], [r_qkT])
        P.dma(scr_s[:, 0:1024], yq[0:NS, 2048:3072], reads=[r_yq], writes=[r_scrs])
        wab, r_wab = aal(256, BF16)
        wabv = wab.rearrange("p (k c) -> p k c", c=16)
        P.dma(wabv, w_ab_d[l], writes=[r_wab], queue="pool")
        bk2, _, br2 = pb()
        for kc in range(16):
            mm(bk2[0:NS, 0:16], hT[:, kc, 0:NS], wabv[:, kc, :], start=(kc == 0), stop=(kc == 15), reads=[r_wab, r_h[kc]], writes=[br2])
        eb, r_eb = aal(16)
        tt(eb[0:NS, 0:8], bk2[0:NS, 0:8], c_dtb[0:NS, l, :], ALU.add, [br2, r_const], [r_eb])
        act(eb[0:NS, 0:8], eb[0:NS, 0:8], AF.Exp, reads=[r_eb], writes=[r_eb])
        act(eb[0:NS, 0:8], eb[0:NS, 0:8], AF.Ln, bias=1.0, reads=[r_eb], writes=[r_eb])
        tt(eb[0:NS, 0:8], eb[0:NS, 0:8], c_nega[0:NS, l, :], ALU.mult, [r_eb, r_const], [r_eb])
        act(eb[0:NS, 0:8], eb[0:NS, 0:8], AF.Exp, reads=[r_eb], writes=[r_eb])
        act(eb[0:NS, 8:16], bk2[0:NS, 8:16], AF.Sigmoid, reads=[br2], writes=[r_eb])
        P.dma(scr_s[:, 1024:1040], eb[0:NS, 0:16], reads=[r_eb], writes=[r_scrs])
        egb, r_egb = aal(NS * 16)
        for s_ in range(NS):
            P.dma(egb[:, s_ * 16:(s_ + 1) * 16], scr_s[s_, 1024:1040].partition_broadcast(128), reads=[r_scrs], writes=[r_egb])
        gate, r_gate = aal(8 * NS); gate = gate.rearrange("p (h s) -> p h s", s=NS)
        gtmp, r_gtmp = aal(NS)
        for b in range(20, 24):
            wv, wr = wload2("w_in_main", (l, b), 16, 256)
            for half in range(2):
                h = (b - 20) * 2 + half
                bk3, br3 = proj_fm(wv, wr, half * 128, N, hT, r_h)
                act(gtmp, bk3[:, 0:N], AF.Silu, reads=[br3], writes=[r_gtmp])
                ts(gate[:, h, :], gtmp, c_gdnw[:, l:l + 1], ALU.mult, reads=[r_gtmp, r_const], writes=[r_gate])
        po, _, rpo = pb_res(7)
        S0b, r_S0 = aal(2 * 1024, nreg=2); vbb, r_vb = aal(2 * 1024, nreg=2); Snb, r_Sn = aal(2 * 1024, nreg=16)
        kb, r_kb = aal(2 * 128, nreg=2); tb_, r_tb = aal(2 * 128, nreg=2)
        for s_ in range(NS):
            i2 = s_ % 2
            S0 = S0b[:, i2 * 1024:(i2 + 1) * 1024].rearrange("p (h e) -> p h e", e=128)
            Sn = Snb[:, i2 * 1024:(i2 + 1) * 1024].rearrange("p (h e) -> p h e", e=128)
            vb = vbb[:, i2 * 1024:(i2 + 1) * 1024]
            P.dma(S0, st_gdn_d[l, s_].rearrange("h d e -> d h e"), writes=[r_S0[i2]])
            P.dma(vb, scr_s[s_, 0:1024].partition_broadcast(128), reads=[r_scrs], writes=[r_vb[i2]])
            for h in range(8):
                j2 = h % 2
                kbv = kb[:, j2 * 128:(j2 + 1) * 128]; tv = tb_[:, j2 * 128:(j2 + 1) * 128]
                kcol = qkT[:, 8 + h, s_:s_ + 1]
                ts(Sn[:, h, :], S0[:, h, :], egb[:, s_ * 16 + h:s_ * 16 + h + 1], ALU.mult, reads=[r_S0[i2], r_egb], writes=[r_Sn[i2 * 8 + h]])
                pk, _, rpk = pb()
                mm(pk[:, 0:128], col_bc(kcol, 128), Sn[:, h, :], reads=[r_qkT, r_Sn[i2 * 8 + h]], writes=[rpk])
                stt(tv, pk[:, 0:128], -1.0, vb[:, h * 128:(h + 1) * 128], ALU.mult, ALU.add, reads=[rpk, r_vb[i2]], writes=[r_tb[j2]])
                ts(tv, tv, egb[:, s_ * 16 + 8 + h:s_ * 16 + 9 + h], ALU.mult, reads=[r_tb[j2], r_egb], writes=[r_tb[j2]])
                stt(Sn[:, h, :], tv, kcol, Sn[:, h, :], ALU.mult, ALU.add, reads=[r_tb[j2], r_qkT, r_Sn[i2 * 8 + h]], writes=[r_Sn[i2 * 8 + h]])
                mm(po[:, h * NS + s_:h * NS + s_ + 1], Sn[:, h, :], qkT[:, h, s_:s_ + 1], reads=[r_Sn[i2 * 8 + h], r_qkT], writes=[rpo])
            P.dma(gdns_d[l, s_].rearrange("h d e -> d h e"), Sn, reads=r_Sn[i2 * 8:i2 * 8 + 8], is_output=True)
        groupnorm_cols(po[:, 0:8 * NS], rpo, 8 * NS, c_o128[:, :], RMS_EPS, False,
                       mixT[:, 4:12, 0:NS], r_mix[4:12], gate, [r_gate], gn_tmps(8 * NS))

    def s5_sample(l):
        phase()
        N = NS
        su32, r_su32 = aal(4 * N, nreg=4); su32 = su32.rearrange("p (k n) -> p k n", n=N)
        sub, r_sub = aal(4 * N, BF16, nreg=4); sub = sub.rearrange("p (k n) -> p k n", n=N)
        for b in range(2):
            wv, wr = wload2("w_in_su", (l, b), 16, 256)
            for half in range(2):
                kc = b * 2 + half
                bk, br = proj_fm(wv, wr, half * 128, N, hT, r_h)
                cp(su32[:, kc, :], bk[:, 0:N], [br], [r_su32[kc]])
                cp(sub[:, kc, :], bk[:, 0:N], [br], [r_sub[kc]], eng="dve")
        CTr, r_CTr = aal(2048, BF16); CTi, r_CTi = aal(2048, BF16)
        CTr = CTr.rearrange("p (k s) -> p k s", k=16); CTi = CTi.rearrange("p (k s) -> p k s", k=16)
        P.dma(CTr, cpr_d[l], writes=[r_CTr], queue="pool"); P.dma(CTi, cpi_d[l], writes=[r_CTi], queue="pool")
        bu, r_bu = aal(2 * 2048)
        mark = _ar["off"]
        BT, r_BT = aal(2 * 2048, BF16); BT = BT.rearrange("p (c k s) -> p c k s", c=2, k=16)
        P.dma(BT.rearrange("p c k s -> p c (k s)"), bt_scr[l].rearrange("c p k s -> p c (k s)"), reads=[r_btscr], writes=[r_BT])
        for c_ in range(2):
            for g4 in range(4):
                bk, _, br = pb()
                for kk in range(4):
                    k = g4 * 4 + kk
                    mm(bk[0:NS, kk * 128:(kk + 1) * 128], sub[:, k // 4, :], BT[:, c_, k, :], reads=[r_sub[k // 4], r_BT], writes=[br])
                cp(bu[0:NS, c_ * 2048 + g4 * 512:c_ * 2048 + (g4 + 1) * 512], bk[0:NS, :], [br], [r_bu], eng=alt())
        phase(mark)
        h0, r_h0 = aal(2 * 2048)
        P.dma(h0[0:NS, 0:2048], st_sre_d[l], writes=[r_h0]); P.dma(h0[0:NS, 2048:4096], st_sim_d[l], writes=[r_h0])
        bk, _, br = pb()
        for c_ in range(2):
            tr(bk[0:16, c_ * 128:(c_ + 1) * 128], s5pw[:, l, 0, c_, :], c_ident[:, :], [r_s5pw, r_const], [br])
        lt, r_lt = aal(256)
        cp(lt[0:16, :], bk[0:16, 0:256], [br], [r_lt])
        for c_ in range(2):
            P.dma(scr_l[c_].rearrange("(k p) -> k p", p=128), lt[0:16, c_ * 128:(c_ + 1) * 128], reads=[r_lt], writes=[r_scrl])
        lrow, r_lrow = aal(2 * 2048)
        for c_ in range(2):
            P.dma(lrow[0:NS, c_ * 2048:(c_ + 1) * 2048], scr_l[c_].partition_broadcast(NS), reads=[r_scrl], writes=[r_lrow])
        tmp, r_tmp = aal(2048)
        lr = lrow[0:NS, 0:2048]; li = lrow[0:NS, 2048:4096]; h0r = h0[0:NS, 0:2048]; h0i = h0[0:NS, 2048:4096]
        hr = bu[0:NS, 0:2048]; hi = bu[0:NS, 2048:4096]; tm = tmp[0:NS, :]
        tt(tm, lr, h0r, ALU.mult, [r_lrow, r_h0], [r_tmp]); tt(hr, hr, tm, ALU.add, [r_bu, r_tmp], [r_bu])
        tt(tm, li, h0i, ALU.mult, [r_lrow, r_h0], [r_tmp]); tt(hr, hr, tm, ALU.subtract, [r_bu, r_tmp], [r_bu])
        tt(tm, lr, h0i, ALU.mult, [r_lrow, r_h0], [r_tmp]); tt(hi, hi, tm, ALU.add, [r_bu, r_tmp], [r_bu])
        tt(tm, li, h0r, ALU.mult, [r_lrow, r_h0], [r_tmp]); tt(hi, hi, tm, ALU.add, [r_bu, r_tmp], [r_bu])
        P.dma(ssms_re_d[l], hr, reads=[r_bu], is_output=True); P.dma(ssms_im_d[l], hi, reads=[r_bu], is_output=True)
        hb, r_hb = aal(2 * 16 * NS, BF16); hb = hb.rearrange("p (c k s) -> p c k s", c=2, k=16)
        for c_ in range(2):
            bk, _, br = pb()
            for k in range(16):
                tr(bk[:, k * NS:(k + 1) * NS], bu[0:NS, c_ * 2048 + k * 128:c_ * 2048 + (k + 1) * 128], c_ident[0:NS, 0:NS], [r_bu, r_const], [br])
            src = bk[:, 0:16 * NS].rearrange("p (k s) -> p k s", s=NS)
            if c_ == 0:
                cp(hb[:, 0, :, :], src, [br], [r_hb])
            else:
                amul(hb[:, 1, :, :], src, -1.0, reads=[br], writes=[r_hb])
        yg, r_yg = aal(4 * N, nreg=4); yg = yg.rearrange("p (k n) -> p k n", n=N)
        ygb, r_ygb = aal(4 * N, BF16, nreg=4); ygb = ygb.rearrange("p (k n) -> p k n", n=N)
        ytmp, r_ytmp = aal(N)
        for kc in range(4):
            by, _, rby = pb_res(7)
            for k in range(kc * 4, kc * 4 + 4):
                mm(by[:, 0:N], CTr[:, k, :], hb[:, 0, k, :], start=(k % 4 == 0), stop=False, reads=[r_CTr, r_hb], writes=[rby])
                mm(by[:, 0:N], CTi[:, k, :], hb[:, 1, k, :], start=False, stop=(k % 4 == 3), reads=[r_CTi, r_hb], writes=[rby])
            stt(ytmp, su32[:, kc, :], c_ssmd[:, l, kc:kc + 1], by[:, 0:N], ALU.mult, ALU.add, reads=[r_su32[kc], r_const, rby], writes=[r_ytmp])
            act(yg[:, kc, :], ytmp, AF.Gelu, reads=[r_ytmp], writes=[r_yg[kc]])
            cp(ygb[:, kc, :], yg[:, kc, :], [r_yg[kc]], [r_ygb[kc]], eng="dve")
        s5_glu(l, N, yg, r_yg, ygb, r_ygb)

    def sample_pass():
        N = NS
        phase()
        mt, r_mt = aal(2 * 96 * 17)
        P.dma(mt, mod_scr.ap(), reads=[r_modscr], writes=[r_mt])
        MT["t"] = mt.rearrange("p (l j n) -> p l j n", l=2, j=96)
        MT["r"] = r_mt
        _ar["base"] = _ar["off"]
        for kc in range(16):
            P.dma(xT[:, kc, 0:N], xsT_d[:, kc, :], writes=[r_x[kc]])
        for kc in range(16):
            modulate(kc, N, (0, 0, 1), False)
        for l in range(2):
            ret_sample(l)
            gdn_sample(l)
            s5_sample(l)
            attn_out_ln(l, N, False)
            ffn(l, N, False, (1, 0, 1) if l == 0 else None)
        for kc in range(16):
            P.dma(ysT_d[:, kc, :], xT[:, kc, 0:N], reads=[r_x[kc]], is_output=True)
        _ar["base"] = 0

    if prompt:
        r_y = P.region("yout")
        for it in range(NT):
            t0 = it * T
            for kc in range(16):
                P.dma(xT[:, kc, :], xT_d[:, kc, t0:t0 + T], writes=[r_x[kc]])
            phase()
            for kc in range(16):
                modulate(kc, T, (0, 0, 1), True)
            if dbg and "h" in dbg and it == 0:
                phase()
                mf, r_mf = aal(16 * T)
                cp(mf.rearrange("p (k n) -> p k n", n=T), hT[:, :, :], r_h, [r_mf], eng="dve")
                P.dma(dbg_d["h"].ap(), mf.rearrange("p (k n) -> p k n", n=T), reads=[r_mf], is_output=True)
            if stop_at == "mod0":
                return P.finish()
            for l in range(2):
                ret_prompt(l, it)
                if stop_at == "ret":
                    return P.finish()
                try:
                    gdn_prompt(l, it)
                except _Stop:
                    return P.finish()
                if stop_at == "gdn":
                    return P.finish()
                s5_prompt(l, it)
                if stop_at == "s5":
                    return P.finish()
                if dbg and "mix" in dbg and it == 0 and l == 0:
                    phase()
                    mf, r_mf = aal(16 * T)
                    cp(mf.rearrange("p (k n) -> p k n", n=T), mixT[:, :, :], r_mix, [r_mf], eng="dve")
                    P.dma(dbg_d["mix"].ap(), mf.rearrange("p (k n) -> p k n", n=T), reads=[r_mf], is_output=True)
                attn_out_ln(l, T, True)
                if dbg and "x1" in dbg and it == 0 and l == 0:
                    P.dma(dbg_d["x1"].ap(), xT[:, :, :], reads=r_x, is_output=True)
                if stop_at == "ln1":
                    return P.finish()
                ffn(l, T, True, (1, 0, 1) if l == 0 else None)
                if dbg and "x2" in dbg and it == 0 and l == 0:
                    P.dma(dbg_d["x2"].ap(), xT[:, :, :], reads=r_x, is_output=True)
                if stop_at == "ffn":
                    return P.finish()
            for kc in range(16):
                P.dma(yT_d[:, kc, t0:t0 + T], xT[:, kc, :], reads=[r_x[kc]], is_output=True)

    if sample:
        sample_pass()
    return P.finish()


def kernel(**inputs):
    inp = {k: np.ascontiguousarray(np.asarray(v)) for k, v in inputs.items()}
    shared = host_shared(inp)
    consts = host_constants()
    in_maps = []
    for core in range(8):
        m = dict(shared)
        m.update(host_core(inp, core))
        for k, v in consts.items():
            if isinstance(v, np.ndarray):
                m["c_" + k] = v
        in_maps.append(m)
    nc = build()
    res = run_bass_kernel_spmd(nc, in_maps, core_ids=list(range(8)))
    r = res.results
    f32 = np.float32
    y_prompt = np.stack([r[b]["yT"].transpose(2, 1, 0).reshape(SEQ, D) for b in range(4)]).astype(f32)
    y_sample = np.concatenate([r[c]["ysT"].transpose(2, 1, 0).reshape(NS, 1, D) for c in range(8)], 0).astype(f32)
    ret_p = np.stack([r[b]["ret_p"] for b in range(4)], 1).astype(f32)
    gdn_p = np.stack([r[b]["gdn_p"] for b in range(4)], 1).astype(f32)
    conv_p = np.stack([r[b]["conv_p"].transpose(0, 3, 2, 1).reshape(2, 3, 3072) for b in range(4)], 1).astype(f32)
    ssm_re_p = np.stack([r[b]["ssm_p"][..., 0].transpose(0, 2, 1).reshape(2, 32, 64) for b in range(4)], 1).astype(f32)
    ssm_im_p = np.stack([r[b]["ssm_p"][..., 1].transpose(0, 2, 1).reshape(2, 32, 64) for b in range(4)], 1).astype(f32)
    ret_s = np.concatenate([r[c]["ret_s"] for c in range(8)], 1).astype(f32)
    gdn_s = np.concatenate([r[c]["gdn_s"] for c in range(8)], 1).astype(f32)
    conv_s = np.concatenate([r[c]["conv_s"] for c in range(8)], 1).astype(f32)
    ssm_re_s = np.concatenate([r[c]["ssm_s_re"].reshape(2, NS, 32, 64) for c in range(8)], 1).astype(f32)
    ssm_im_s = np.concatenate([r[c]["ssm_s_im"].reshape(2, NS, 32, 64) for c in range(8)], 1).astype(f32)
    return (y_prompt, y_sample, ret_p, gdn_p, conv_p, ssm_re_p, ssm_im_p, ret_s, gdn_s, conv_s, ssm_re_s, ssm_im_s)
```

```python
from contextlib import ExitStack
import math
import numpy as np
import concourse.bass as bass
import concourse.mybir as mybir
from concourse.bass_utils import run_bass_kernel_spmd

F32 = mybir.dt.float32
BF16 = mybir.dt.bfloat16
AF = mybir.ActivationFunctionType
ALU = mybir.AluOpType

ENGS = ("pe", "act", "dve", "pool", "sp")
N_DMA_SEMS = 28
N_SP_SEMS = 16


class Region:
    __slots__ = ("name", "w", "r", "arena", "excl")

    def __init__(self, name, arena=False, excl=False):
        self.name = name
        self.w = None
        self.r = []
        self.arena = arena
        self.excl = excl


class Op:
    __slots__ = ("eng", "idx", "fn", "deps", "is_dma", "dsem", "dval", "marked", "semval")

    def __init__(self, eng, idx, fn, is_dma=False):
        self.eng = eng
        self.idx = idx
        self.fn = fn
        self.deps = []
        self.is_dma = is_dma
        self.dsem = None
        self.dval = None
        self.marked = False
        self.semval = None


class Prog:
    def __init__(self):
        self.nc = bass.Bass("TRN2", target_bir_lowering=False)
        self.stack = ExitStack()
        self.ops = {e: [] for e in ENGS}
        self.known = {e: {} for e in ENGS}
        self.known_dma = {e: {} for e in ENGS}
        self.dma_cnt = [0] * N_DMA_SEMS
        self.dma_last = [None] * N_DMA_SEMS
        self.dma_rr = {"sp": 0, "pool": 0}
        self.nreg = 0
        self.out_dmas = []
        self.arena_last = {}
        self.arena_dmas = []
        self.arena_dmas_prev = []
        self.fence = []

    def sb(self, name, shape, dtype=F32):
        return self.stack.enter_context(self.nc.sbuf_tensor(name, list(shape), dtype))

    def ps(self, name, shape, dtype=F32):
        return self.stack.enter_context(self.nc.psum_tensor(name, list(shape), dtype))

    def dram(self, name, shape, dtype=F32, kind="Internal"):
        return self.nc.dram_tensor(name, list(shape), dtype, kind=kind)

    def region(self, name=None):
        self.nreg += 1
        return Region(name or f"r{self.nreg}")

    def regions(self, n, name="r", excl=False):
        rs = [self.region(f"{name}{i}") for i in range(n)]
        for r in rs:
            r.excl = excl
        return rs

    def aregion(self, name=None):
        self.nreg += 1
        r = Region(name or f"a{self.nreg}", arena=True)
        r.r = list(self.fence)
        return r

    def arena_phase(self):
        f = list(self.arena_last.values()) + list(self.arena_dmas) + list(self.arena_dmas_prev)
        self.fence = f + [o for o in self.fence if o.is_dma is False and o.eng not in self.arena_last]
        self.arena_last = {}
        self.arena_dmas_prev = self.arena_dmas
        self.arena_dmas = []

    def _add_deps(self, op, reads, writes):
        deps = []
        arena = False
        for r in reads:
            arena |= r.arena
            if r.w is not None:
                deps.append(r.w)
            if r.excl:
                deps.extend(o for o in r.r if o.eng != op.eng)
        for w in writes:
            arena |= w.arena
            if w.w is not None:
                deps.append(w.w)
            deps.extend(w.r)
        eng = op.eng
        need = {}
        for d in deps:
            if d is op:
                continue
            if d.is_dma:
                key = ("d", d.dsem)
                if need.get(key, (0, None))[0] < d.dval:
                    need[key] = (d.dval, d)
            else:
                if d.eng == eng and eng == "pe":
                    continue
                key = ("e", d.eng)
                if need.get(key, (-1, None))[0] < d.idx:
                    need[key] = (d.idx, d)
        for key, (v, d) in need.items():
            if key[0] == "d":
                if self.known_dma[eng].get(key[1], 0) >= v:
                    continue
                self.known_dma[eng][key[1]] = v
            else:
                if self.known[eng].get(key[1], -1) >= v:
                    continue
                self.known[eng][key[1]] = v
                d.marked = True
            op.deps.append(d)
        for r in reads:
            r.r.append(op)
        for w in writes:
            w.w = op
            w.r = []
        if arena:
            if op.is_dma:
                self.arena_dmas.append(op)
            else:
                self.arena_last[eng] = op

    def op(self, eng, fn, reads=(), writes=()):
        o = Op(eng, len(self.ops[eng]), fn)
        self._add_deps(o, reads, writes)
        self.ops[eng].append(o)
        return o

    def dma(self, out_ap, in_ap, reads=(), writes=(), queue="sp", is_output=False, **kw):
        if queue == "pool":
            s = N_SP_SEMS + self.dma_rr["pool"]
            self.dma_rr["pool"] = (self.dma_rr["pool"] + 1) % (N_DMA_SEMS - N_SP_SEMS)
        else:
            s = self.dma_rr["sp"]
            self.dma_rr["sp"] = (self.dma_rr["sp"] + 1) % N_SP_SEMS
        o = Op(queue, len(self.ops[queue]), None, is_dma=True)
        o.fn = lambda e, _o=out_ap, _i=in_ap, _kw=kw: e.dma_start(out=_o, in_=_i, **_kw)
        prev = self.dma_last[s]
        if prev is not None:
            if self.known_dma[queue].get(s, 0) < prev.dval:
                self.known_dma[queue][s] = prev.dval
                o.deps.append(prev)
        self.dma_cnt[s] += 1
        o.dsem = s
        o.dval = 16 * self.dma_cnt[s]
        self.dma_last[s] = o
        self._add_deps(o, reads, writes)
        self.ops[queue].append(o)
        if is_output:
            self.out_dmas.append(o)
        return o

    def finish(self):
        nc = self.nc
        fin = Op("sp", len(self.ops["sp"]), None)
        for d in self.out_dmas:
            if self.known_dma["sp"].get(d.dsem, 0) < d.dval:
                self.known_dma["sp"][d.dsem] = d.dval
                fin.deps.append(d)
        for e in ENGS:
            if e != "sp" and self.ops[e]:
                last = None
                for o in reversed(self.ops[e]):
                    if not o.is_dma and o.fn is not None:
                        last = o
                        break
                if last is not None:
                    last.marked = True
                    fin.deps.append(last)
        self.ops["sp"].append(fin)
        for e in ENGS:
            c = 0
            for o in self.ops[e]:
                if o.is_dma:
                    continue
                if o.marked:
                    c += 1
                    o.semval = c
        self.esem = {e: self.stack.enter_context(nc.semaphore(f"s_{e}")) for e in ENGS}
        self.dsems = [self.stack.enter_context(nc.semaphore(f"s_dma{i}")) for i in range(N_DMA_SEMS)]
        with nc.Block() as block:
            def emit(ename, e):
                for o in self.ops[ename]:
                    for d in o.deps:
                        if d.is_dma:
                            e.wait_ge(self.dsems[d.dsem], d.dval)
                        else:
                            e.wait_ge(self.esem[d.eng], d.semval)
                    if o.fn is None:
                        continue
                    ins = o.fn(e)
                    if o.is_dma:
                        ins.then_inc(self.dsems[o.dsem], 16)
                    elif o.marked:
                        ins.then_inc(self.esem[ename], 1)

            @block.tensor
            def _(e):
                emit("pe", e)

            @block.scalar
            def _(e):
                emit("act", e)

            @block.vector
            def _(e):
                emit("dve", e)

            @block.gpsimd
            def _(e):
                emit("pool", e)

            @block.sync
            def _(e):
                emit("sp", e)
        self.stack.close()
        return nc

    def stats(self):
        return {e: len(self.ops[e]) for e in ENGS}


def bc_last(ap, n):
    return bass.AP(ap.tensor, ap.offset, [list(x) for x in ap.ap] + [[0, n]])


def bc_mid(ap, n):
    a = [list(x) for x in ap.ap]
    return bass.AP(ap.tensor, ap.offset, [a[0], [0, n]] + a[1:])


D = 2048
KC = 16
SEQ = 2048
T = 512
NT = SEQ // T
NS = 16
DEPTH = 2
PAST_LEN = 16384
DFF = 5632
FK = DFF // 128
ALPHA = (2 * DEPTH) ** 0.25
LN_EPS = 1e-5
RMS_EPS = 1e-6
L2_EPS = 1e-6
NEG = -1.0e30


def _blk(w, cols):
    K, N = w.shape
    return np.ascontiguousarray(w.reshape(K // 128, 128, N // cols, cols).transpose(2, 1, 0, 3))


def _fm(v):
    F = v.shape[-1]
    lead = v.shape[:-1]
    a = v.reshape(lead + (F // 128, 128))
    nd = a.ndim
    return np.ascontiguousarray(np.moveaxis(np.moveaxis(a, nd - 1, 0), nd - 1, 1))


def host_constants():
    c = {}
    f32 = np.float32
    h = np.arange(4, dtype=np.float64)
    log_g = np.log1p(-np.power(2.0, -5.0 - h))
    i = np.arange(128)
    diff = i[None, :] - i[:, None]
    m = np.zeros((128, 4, 128))
    for hh in range(4):
        m[:, hh, :] = np.where(diff >= 0, np.exp(log_g[hh] * diff) * 128 ** -0.5, 0.0)
    c["ret_maskT"] = m.astype(f32)
    gr = np.exp(log_g[:, None] * (i[None, :] + 1))
    c["ret_grow"] = np.broadcast_to(gr[None], (128, 4, 128)).astype(f32).copy()
    c["ret_kwcol"] = (np.exp(log_g[None, :] * (127 - i[:, None])) * 128 ** -0.5).astype(f32)
    c["ret_gC"] = [float(np.exp(log_g[hh] * 128)) for hh in range(4)]
    c["ret_g1"] = [float(np.exp(log_g[hh])) for hh in range(4)]
    half = 64
    freq = (np.float32(10000.0) ** (-np.arange(half, dtype=f32) / np.float32(half))).astype(f32)
    pos = np.arange(SEQ, dtype=f32)
    ang = (pos[:, None] * freq[None, :]).astype(f32)
    cs, sn = np.cos(ang).astype(f32), np.sin(ang).astype(f32)
    c["rope_cos"] = np.ascontiguousarray(np.concatenate([cs, cs], 1).T)
    c["rope_sin"] = np.ascontiguousarray(np.concatenate([-sn, sn], 1).T)
    angs = (np.float32(PAST_LEN) * freq).astype(f32)
    c["rope_cos_s"] = np.concatenate([np.cos(angs), np.cos(angs)]).astype(f32)[:, None]
    c["rope_sin_s"] = np.concatenate([-np.sin(angs), np.sin(angs)]).astype(f32)[:, None]
    k = np.arange(128)
    c["pswap"] = (k[:, None] == ((k[None, :] + 64) % 128)).astype(f32)
    same = (k[:, None] // 64) == (k[None, :] // 64)
    c["maskU"] = np.where(same & (k[None, :] >= k[:, None]), 0.0, NEG).astype(f32)
    c["maskL"] = np.where(same & (k[:, None] > k[None, :]), 0.0, -NEG).astype(f32)
    c["tri"] = (same & (k[:, None] <= k[None, :])).astype(f32)
    c["blk1"] = same.astype(f32)
    sel = np.zeros((8, 8, 128), f32)
    for hh in range(8):
        sel[hh, hh, :] = 1.0
    c["sel"] = sel
    c["ident"] = np.eye(128, dtype=f32)
    return c


def host_shared(inp):
    f32 = np.float32
    s = {}
    w_mod = inp["w_mod"]
    s["w_mod_b"] = np.stack([_blk(w_mod[l], 256) for l in range(2)])
    s["b_mod_t"] = np.ascontiguousarray(inp["b_mod"].reshape(2, 96, 128).transpose(2, 0, 1))
    w_in = inp["w_in"]
    s["w_in_main"] = np.stack([_blk(w_in[l][:, :6144], 256) for l in range(2)])
    s["w_in_ab"] = np.ascontiguousarray(np.stack([w_in[l][:, 6144:6160].reshape(16, 128, 16).transpose(1, 0, 2) for l in range(2)]))
    s["w_in_su"] = np.stack([_blk(w_in[l][:, 6160:6672], 256) for l in range(2)])
    s["conv_w_t"] = np.ascontiguousarray(inp["conv_w"].reshape(2, 4, 24, 128).transpose(3, 0, 2, 1))
    s["conv_w"] = np.ascontiguousarray(inp["conv_w"])
    s["ret_gn_w_t"] = np.ascontiguousarray(inp["ret_gn_w"].reshape(2, 4, 128).transpose(2, 0, 1))
    s["gdn_a_log"] = np.ascontiguousarray(inp["gdn_a_log"])
    s["gdn_dt_bias"] = np.ascontiguousarray(inp["gdn_dt_bias"])
    s["gdn_norm_w_t"] = np.ascontiguousarray(inp["gdn_norm_w"].T)
    def st(a):
        return np.ascontiguousarray(a.reshape(2, 16, 128).transpose(2, 0, 1))
    s["ssm_a_re_t"] = st(inp["ssm_a_re"])
    s["ssm_a_im_t"] = st(inp["ssm_a_im"])
    s["ssm_log_dt_t"] = st(np.repeat(inp["ssm_log_dt"][:, :, None], 64, axis=2))
    def bpad(b):
        out = np.zeros((2, 128, 16, 128), f32)
        for g in range(32):
            k_, half_ = g // 2, g % 2
            r0 = (k_ % 4) * 32 + half_ * 16
            out[:, r0:r0 + 16, k_, half_ * 64:(half_ + 1) * 64] = b[:, g].transpose(0, 2, 1)
        return out
    s["ssm_bpad_re"] = bpad(inp["ssm_b_re"])
    s["ssm_bpad_im"] = bpad(inp["ssm_b_im"])
    def cpad(cc):
        out = np.zeros((2, 128, 16, 128), f32)
        for g in range(32):
            k_, half_ = g // 2, g % 2
            c0 = (k_ % 4) * 32 + half_ * 16
            out[:, half_ * 64:(half_ + 1) * 64, k_, c0:c0 + 16] = cc[:, g].transpose(0, 2, 1)
        return out
    s["ssm_cpad_re"] = cpad(inp["ssm_c_re"])
    s["ssm_cpad_im"] = cpad(inp["ssm_c_im"])
    s["ssm_d_t"] = np.ascontiguousarray(inp["ssm_d"].reshape(2, 4, 128).transpose(2, 0, 1))
    s["w_glu_b"] = np.ascontiguousarray(np.stack([inp["ssm_w_glu"][l].reshape(4, 128, 512).transpose(1, 0, 2) for l in range(2)]))
    s["w_out_b"] = np.stack([_blk(inp["w_out"][l], 128) for l in range(2)])
    for nm in ("ln1_w", "ln1_b", "ln2_w", "ln2_b"):
        s[nm + "_t"] = np.ascontiguousarray(inp[nm].reshape(2, 16, 128).transpose(2, 0, 1))
    wfi = inp["w_ffn_in"]
    gu = []
    for l in range(2):
        g_ = _blk(wfi[l][:, :DFF], 128)
        u_ = _blk(wfi[l][:, DFF:], 128)
        gu.append(np.concatenate([g_, u_], axis=3))
    s["w_ffn_in_b"] = np.stack(gu)
    s["w_ffn_out_b"] = np.stack([_blk(inp["w_ffn_out"][l], 128) for l in range(2)])
    return s


def host_core(inp, core):
    b = core % 4
    s0 = core * NS
    m = {}
    m["xT"] = _fm(inp["x_prompt"][b])
    m["xsT"] = _fm(inp["x_sample"][s0:s0 + NS, 0])
    call = np.concatenate([inp["c_prompt"][b:b + 1], inp["c_sample"][s0:s0 + NS]], 0)
    m["cT"] = _fm(call)
    m["st_ret"] = np.ascontiguousarray(inp["state_ret"][:, s0:s0 + NS])
    m["st_gdn"] = np.ascontiguousarray(inp["state_gdn"][:, s0:s0 + NS])
    m["st_conv"] = np.ascontiguousarray(inp["state_conv"][:, s0:s0 + NS])
    m["st_sre"] = np.ascontiguousarray(inp["state_ssm_re"][:, s0:s0 + NS].reshape(2, NS, 2048))
    m["st_sim"] = np.ascontiguousarray(inp["state_ssm_im"][:, s0:s0 + NS].reshape(2, NS, 2048))
    return m


AR_BYTES = 84 * 1024 + 512
WB_ELEMS = 4096


def build(prompt=True, sample=True, dbg=None, stop_at=None):
    P = Prog()
    nc = P.nc
    C = host_constants()
    D_ = {}

    def din(name, shape, dt=F32):
        D_[name] = P.dram(name, shape, dt, kind="ExternalInput")
        return D_[name]

    def dout(name, shape, dt=F32):
        D_[name] = P.dram(name, shape, dt, kind="ExternalOutput")
        return D_[name]

    xT_d = din("xT", [128, 16, SEQ]); xsT_d = din("xsT", [128, 16, NS]); cT_d = din("cT", [128, 16, 17])
    w_mod_d = din("w_mod_b", [2, 48, 128, 16, 256]); b_mod_d = din("b_mod_t", [128, 2, 96])
    w_in_d = din("w_in_main", [2, 24, 128, 16, 256]); w_ab_d = din("w_in_ab", [2, 128, 16, 16]); w_su_d = din("w_in_su", [2, 2, 128, 16, 256])
    convw_d = din("conv_w_t", [128, 2, 24, 4]); convw_raw_d = din("conv_w", [2, 4, 3072])
    gnw_d = din("ret_gn_w_t", [128, 2, 4]); alog_d = din("gdn_a_log", [2, 8]); dtb_d = din("gdn_dt_bias", [2, 8]); gdnw_d = din("gdn_norm_w_t", [128, 2])
    are_d = din("ssm_a_re_t", [128, 2, 16]); aim_d = din("ssm_a_im_t", [128, 2, 16]); ldt_d = din("ssm_log_dt_t", [128, 2, 16])
    bpr_d = din("ssm_bpad_re", [2, 128, 16, 128]); bpi_d = din("ssm_bpad_im", [2, 128, 16, 128])
    cpr_d = din("ssm_cpad_re", [2, 128, 16, 128]); cpi_d = din("ssm_cpad_im", [2, 128, 16, 128])
    ssmd_d = din("ssm_d_t", [128, 2, 4]); wglu_d = din("w_glu_b", [2, 128, 4, 512])
    w_out_d = din("w_out_b", [2, 16, 128, 16, 128])
    ln_d = {nm: din(nm + "_t", [128, 2, 16]) for nm in ("ln1_w", "ln1_b", "ln2_w", "ln2_b")}
    wfi_d = din("w_ffn_in_b", [2, 44, 128, 16, 256]); wfo_d = din("w_ffn_out_b", [2, 16, 128, 44, 128])
    st_ret_d = din("st_ret", [2, NS, 4, 128, 128]); st_gdn_d = din("st_gdn", [2, NS, 8, 128, 128])
    st_conv_d = din("st_conv", [2, NS, 3, 3072]); st_sre_d = din("st_sre", [2, NS, 2048]); st_sim_d = din("st_sim", [2, NS, 2048])
    cd = {}
    for nm in ("ret_maskT", "ret_grow", "ret_kwcol", "rope_cos", "rope_sin", "rope_cos_s", "rope_sin_s", "pswap",
               "maskU", "maskL", "tri", "blk1", "sel", "ident"):
        cd[nm] = din("c_" + nm, list(C[nm].shape))
    yT_d = dout("yT", [128, 16, SEQ]); ysT_d = dout("ysT", [128, 16, NS])
    retp_d = dout("ret_p", [2, 4, 128, 128]); gdnp_d = dout("gdn_p", [2, 8, 128, 128])
    convp_d = dout("conv_p", [2, 128, 24, 3]); ssmp_d = dout("ssm_p", [2, 128, 16, 2])
    rets_d = dout("ret_s", [2, NS, 4, 128, 128]); gdns_d = dout("gdn_s", [2, NS, 8, 128, 128])
    convs_d = dout("conv_s", [2, NS, 3, 3072]); ssms_re_d = dout("ssm_s_re", [2, NS, 2048]); ssms_im_d = dout("ssm_s_im", [2, NS, 2048])
    bt_scr = P.dram("bt_scr", [2, 2, 128, 16, 128], BF16)
    scr64 = P.dram("scr64", [64, 128], F32)
    tab_scr = P.dram("tab_scr", [2, 4, 128, 16, T], F32)
    scr_s = P.dram("scr_s", [NS, 4096], F32)
    dbg_d = {}
    if dbg:
        for nm, shp in dbg.items():
            dbg_d[nm] = dout("dbg_" + nm, shp)

    def mm(out, lhsT, rhs, start=True, stop=True, reads=(), writes=()):
        P.op("pe", lambda e: e.matmul(out, lhsT, rhs, start=start, stop=stop), reads, writes)

    def tr(out, in_, ident, reads=(), writes=()):
        P.op("pe", lambda e: e.transpose(out, in_, ident), reads, writes)

    def tt(out, a, b, op, reads=(), writes=(), eng="dve"):
        P.op(eng, lambda e: e.tensor_tensor(out, a, b, op), reads, writes)

    def ts(out, a, s1, op0, s2=None, op1=None, reads=(), writes=(), eng="dve"):
        if s2 is None:
            P.op(eng, lambda e: e.tensor_scalar(out, a, s1, None, op0), reads, writes)
        else:
            P.op(eng, lambda e: e.tensor_scalar(out, a, s1, s2, op0, op1), reads, writes)

    def stt(out, a, s, b, op0, op1, reads=(), writes=(), eng="dve"):
        P.op(eng, lambda e: e.scalar_tensor_tensor(out, a, s, b, op0, op1), reads, writes)

    def act(out, in_, func, bias=0.0, scale=1.0, reads=(), writes=()):
        P.op("act", lambda e: e.activation(out, in_, func, bias=bias, scale=scale), reads, writes)

    def cp(out, in_, reads=(), writes=(), eng="act"):
        if eng == "act":
            P.op("act", lambda e: e.copy(out, in_), reads, writes)
        else:
            P.op(eng, lambda e: e.tensor_copy(out, in_), reads, writes)

    def amul(out, in_, c, reads=(), writes=()):
        P.op("act", lambda e: e.mul(out, in_, c), reads, writes)

    def memset(ap, v, writes=(), eng="dve"):
        P.op(eng, lambda e: e.memset(ap, v), (), writes)

    def recip(out, in_, reads=(), writes=()):
        P.op("dve", lambda e: e.reciprocal(out, in_), reads, writes)

    def col_bc1(a, n):
        return bass.AP(a.tensor, a.offset, [list(a.ap[0]), [0, n]])

    _alt = {"i": 0}

    def alt():
        _alt["i"] ^= 1
        return "act" if _alt["i"] else "dve"

    banks = [P.ps(f"bank{i}", [128, 512], F32) for i in range(8)]
    banks_bf = [b.bitcast(BF16) for b in banks]
    bank_r = P.regions(8, "bank", excl=True)
    _bk = {"i": 0}

    def pb():
        i = _bk["i"]
        _bk["i"] = (i + 1) % 5
        return banks[i], banks_bf[i], bank_r[i]

    def pb_res(i):
        return banks[i], banks_bf[i], bank_r[i]

    NWB = 3
    wbufs = [P.sb(f"wbuf{i}", [128, WB_ELEMS], BF16) for i in range(NWB)]
    wregs = P.regions(NWB, "wbuf")
    _wb = {"i": 0}

    def wload(src, kc, cols):
        i = _wb["i"]
        _wb["i"] = (i + 1) % NWB
        view = wbufs[i][:, 0:kc * cols].rearrange("p (k c) -> p k c", c=cols)
        P.dma(view, src, writes=[wregs[i]], queue="pool")
        return view, wregs[i]

    WSH = {}
    for _nm, _t, _shape in (("w_in_main", w_in_d, [2, 24, 128, 16, 256]), ("w_in_su", w_su_d, [2, 2, 128, 16, 256]),
                            ("w_out_b", w_out_d, [2, 16, 128, 16, 128]), ("w_ffn_in_b", wfi_d, [2, 44, 128, 16, 256]),
                            ("w_ffn_out_b", wfo_d, [2, 16, 128, 44, 128]), ("w_glu_b", wglu_d, [2, 128, 4, 512])):
        WSH[_nm] = (_t, P.dram("sh_" + _nm, _shape, BF16))
    w_written = {}

    def wload2(name, idx, kc, cols):
        f32_t, sh_t = WSH[name]
        key = (name, repr(idx))
        i = _wb["i"]
        _wb["i"] = (i + 1) % NWB
        view = wbufs[i][:, 0:kc * cols].rearrange("p (k c) -> p k c", c=cols)
        if key in w_written:
            P.dma(view, sh_t[idx], reads=[w_written[key]], writes=[wregs[i]], queue="pool")
        else:
            P.dma(view, f32_t[idx], writes=[wregs[i]], queue="pool")
            rk = P.region("wsh")
            P.dma(sh_t[idx], view, reads=[wregs[i]], writes=[rk], queue="sp")
            w_written[key] = rk
        return view, wregs[i]

    arena_bf = P.sb("arena", [128, AR_BYTES // 2], BF16)
    arena_f = arena_bf.bitcast(F32)
    _ar = {"off": 0}

    def phase(keep=None):
        P.arena_phase()
        _ar["off"] = _ar.get("base", 0) if keep is None else keep

    def aal(n, dt=F32, nreg=1):
        sz = 4 if dt == F32 else 2
        off = (_ar["off"] + 31) // 32 * 32
        assert off + n * sz <= AR_BYTES, ("arena overflow", off, n, sz)
        _ar["off"] = off + n * sz
        _ar["hw"] = max(_ar.get("hw", 0), _ar["off"])
        if dt == F32:
            v = arena_f[:, off // 4: off // 4 + n]
        else:
            v = arena_bf[:, off // 2: off // 2 + n]
        if nreg == 1:
            return v, P.aregion()
        return v, [P.aregion() for _ in range(nreg)]

    xT = P.sb("xT_sb", [128, 16, T], F32); r_x = P.regions(16, "x")
    hT = P.sb("hT", [128, 16, T], BF16); r_h = P.regions(16, "h")
    mixT = P.sb("mixT", [128, 16, T], BF16); r_mix = P.regions(16, "mix")
    mod_scr = P.dram("mod_scr", [128, 2 * 96 * 17], F32); r_modscr = P.region("modscr")
    MODP = P.sb("MODP", [128, 2, 6, 16], F32); r_modp = P.region("modp")
    lnw = {nm: P.sb("s_" + nm, [128, 2, 16], F32) for nm in ln_d}
    r_const = P.region("const")
    S_ret = P.sb("S_ret", [128, 2, 4, 128], F32); r_Sret = P.regions(2, "Sret")
    Sb_ret = P.sb("Sb_ret", [128, 4, 128], BF16); r_Sbret = P.region("Sbret")
    S_gdn = P.sb("S_gdn", [128, 2, 8, 128], F32); r_Sgdn = P.regions(2, "Sgdn")
    Sb_gdn = P.sb("Sb_gdn", [128, 8, 128], BF16); r_Sbgdn = P.region("Sbgdn")
    convc = P.sb("convc", [128, 2, 24, 3], F32); r_convc = P.regions(2, "convc")
    ssmc = P.sb("ssmc", [128, 2, 16, 2], F32); r_ssmc = [P.regions(16, f"ssmc{l_}") for l_ in range(2)]
    s5pw = P.sb("s5pw", [128, 2, 9, 3, 16], F32); r_s5pw = P.region("s5pw")
    s5pn = P.sb("s5pn", [128, 2, 9, 2, 16], F32); r_s5pn = P.region("s5pn")
    c_maskT = P.sb("s_maskT", [128, 4, 128], F32); c_grow = P.sb("s_grow", [128, 4, 128], F32); c_kwcol = P.sb("s_kwcol", [128, 4], F32)
    c_pswap = P.sb("s_pswap", [128, 128], F32); c_maskU = P.sb("s_maskU", [128, 128], F32); c_maskL = P.sb("s_maskL", [128, 128], F32)
    c_tri = P.sb("s_tri", [128, 128], F32); c_blk1 = P.sb("s_blk1", [128, 128], F32)
    c_ident = P.sb("s_ident", [128, 128], F32); c_identb = P.sb("s_identb", [128, 128], BF16)
    c_o128 = P.sb("s_o128", [128, 128], F32); c_oln = P.sb("s_oln", [128, 128], F32); c_o1 = P.sb("s_o1", [128, 128], F32); c_odk = P.sb("s_odk", [128, 128], F32)
    c_convw = P.sb("s_convw", [128, 2, 24, 4], F32); c_gnw = P.sb("s_gnw", [128, 2, 4], F32); c_gdnw = P.sb("s_gdnw", [128, 2], F32)
    c_nega = P.sb("s_nega", [128, 2, 8], F32); c_dtb = P.sb("s_dtb", [128, 2, 8], F32); c_ssmd = P.sb("s_ssmd", [128, 2, 4], F32)
    c_bmod = P.sb("s_bmod", [128, 2, 96], F32)
    c_olnb = P.sb("s_olnb", [128, 128], BF16)
    c_ropes = P.sb("s_ropes", [128, 2], F32)

    def ld(dst, src):
        P.dma(dst, src, writes=[r_const])

    ld(c_maskT[:, :, :], cd["ret_maskT"].ap()); ld(c_grow[:, :, :], cd["ret_grow"].ap()); ld(c_kwcol[:, :], cd["ret_kwcol"].ap())
    ld(c_pswap[:, :], cd["pswap"].ap()); ld(c_maskU[:, :], cd["maskU"].ap()); ld(c_maskL[:, :], cd["maskL"].ap())
    ld(c_tri[:, :], cd["tri"].ap()); ld(c_blk1[:, :], cd["blk1"].ap()); ld(c_ident[:, :], cd["ident"].ap())
    ld(c_convw[:, :, :, :], convw_d.ap()); ld(c_gnw[:, :, :], gnw_d.ap()); ld(c_gdnw[:, :], gdnw_d.ap()); ld(c_ssmd[:, :, :], ssmd_d.ap())
    ld(c_bmod[:, :, :], b_mod_d.ap())
    ld(c_ropes[:, 0:1], cd["rope_cos_s"].ap()); ld(c_ropes[:, 1:2], cd["rope_sin_s"].ap())
    for nm in ln_d:
        ld(lnw[nm][:, :, :], ln_d[nm].ap())
    ld(c_nega[:, :, :].rearrange("p l h -> p (l h)"), alog_d.ap().rearrange("l h -> (l h)").partition_broadcast(128))
    ld(c_dtb[:, :, :].rearrange("p l h -> p (l h)"), dtb_d.ap().rearrange("l h -> (l h)").partition_broadcast(128))
    cp(c_identb[:, :], c_ident[:, :], [r_const], [r_const])
    memset(c_o128[:, :], 1.0 / 128, [r_const]); memset(c_oln[:, :], 1.0 / D, [r_const]); memset(c_o1[:, :], 1.0, [r_const]); memset(c_odk[:, :], 128.0, [r_const])
    cp(c_olnb[:, :], c_oln[:, :], [r_const], [r_const])
    act(c_nega[:, :, :], c_nega[:, :, :], AF.Exp, reads=[r_const], writes=[r_const])
    ts(c_nega[:, :, :], c_nega[:, :, :], -1.0, ALU.mult, reads=[r_const], writes=[r_const])
    memset(S_ret[:, :, :, :], 0.0, r_Sret); memset(S_gdn[:, :, :, :], 0.0, r_Sgdn)
    memset(convc[:, :, :, :], 0.0, r_convc); memset(ssmc[:, :, :, :], 0.0, r_ssmc[0] + r_ssmc[1])

    if stop_at == "setup1":
        return P.finish()
    phase()
    TWO_PI = 2.0 * math.pi
    s5t, r_s5t = aal(20 * 32)
    s5t = s5t.rearrange("p (a n) -> p a n", n=32)
    R5 = [r_s5t]

    def row(i):
        return s5t[:, i, :]
    P.dma(row(0), are_d.ap().rearrange("p l k -> p (l k)"), writes=R5)
    P.dma(row(1), aim_d.ap().rearrange("p l k -> p (l k)"), writes=R5)
    P.dma(row(2), ldt_d.ap().rearrange("p l k -> p (l k)"), writes=R5)
    act(row(2), row(2), AF.Exp, reads=R5, writes=R5)
    tt(row(3), row(0), row(2), ALU.mult, R5, R5)
    tt(row(4), row(1), row(2), ALU.mult, R5, R5)
    act(row(5), row(3), AF.Exp, reads=R5, writes=R5)
    act(row(6), row(4), AF.Sin, scale=1.0 / 16, reads=R5, writes=R5)
    act(row(7), row(4), AF.Sin, bias=math.pi / 2, scale=1.0 / 16, reads=R5, writes=R5)
    for _d in range(4):
        tt(row(16), row(6), row(7), ALU.mult, R5, R5)
        tt(row(17), row(7), row(7), ALU.mult, R5, R5)
        tt(row(18), row(6), row(6), ALU.mult, R5, R5)
        tt(row(7), row(17), row(18), ALU.subtract, R5, R5)
        ts(row(6), row(16), 2.0, ALU.mult, reads=R5, writes=R5)
    tt(row(8), row(5), row(7), ALU.mult, R5, R5)
    tt(row(9), row(5), row(6), ALU.mult, R5, R5)
    ts(row(10), row(8), -1.0, ALU.add, reads=R5, writes=R5)
    tt(row(11), row(0), row(0), ALU.mult, R5, R5)
    tt(row(12), row(1), row(1), ALU.mult, R5, R5)
    tt(row(11), row(11), row(12), ALU.add, R5, R5)
    recip(row(11), row(11), R5, R5)
    tt(row(12), row(10), row(0), ALU.mult, R5, R5)
    tt(row(13), row(9), row(1), ALU.mult, R5, R5)
    tt(row(12), row(12), row(13), ALU.add, R5, R5)
    tt(row(14), row(12), row(11), ALU.mult, R5, R5)
    tt(row(12), row(9), row(0), ALU.mult, R5, R5)
    tt(row(13), row(10), row(1), ALU.mult, R5, R5)
    tt(row(12), row(12), row(13), ALU.subtract, R5, R5)
    tt(row(15), row(12), row(11), ALU.mult, R5, R5)
    for l in range(2):
        cp(s5pw[:, l, 0, 0, :], s5t[:, 8, l * 16:(l + 1) * 16], R5, [r_s5pw], eng="dve")
        cp(s5pw[:, l, 0, 1, :], s5t[:, 9, l * 16:(l + 1) * 16], R5, [r_s5pw], eng="dve")
    for k in range(1, 9):
        a_re = s5pw[:, :, k - 1, 0, :]; a_im = s5pw[:, :, k - 1, 1, :]
        t1 = s5t[:, 16, :].rearrange("p (l k) -> p l k", l=2); t2 = s5t[:, 17, :].rearrange("p (l k) -> p l k", l=2)
        tt(t1, a_re, a_re, ALU.mult, [r_s5pw], R5)
        tt(t2, a_im, a_im, ALU.mult, [r_s5pw], R5)
        tt(s5pw[:, :, k, 0, :], t1, t2, ALU.subtract, R5, [r_s5pw])
        tt(t1, a_re, a_im, ALU.mult, [r_s5pw], R5)
        ts(s5pw[:, :, k, 1, :], t1, 2.0, ALU.mult, reads=R5, writes=[r_s5pw])
    ts(s5pw[:, :, :, 2, :], s5pw[:, :, :, 1, :], -1.0, ALU.mult, reads=[r_s5pw], writes=[r_s5pw])
    tt(row(16), row(8), row(8), ALU.mult, R5, R5); tt(row(17), row(9), row(9), ALU.mult, R5, R5)
    tt(row(16), row(16), row(17), ALU.add, R5, R5); recip(row(16), row(16), R5, R5)
    tt(row(18), row(8), row(16), ALU.mult, R5, R5)
    tt(row(19), row(9), row(16), ALU.mult, R5, R5)
    ts(row(19), row(19), -1.0, ALU.mult, reads=R5, writes=R5)
    for l in range(2):
        cp(s5pn[:, l, 0, 0, :], s5t[:, 18, l * 16:(l + 1) * 16], R5, [r_s5pn], eng="dve")
        cp(s5pn[:, l, 0, 1, :], s5t[:, 19, l * 16:(l + 1) * 16], R5, [r_s5pn], eng="dve")
    for k in range(1, 9):
        a_re = s5pn[:, :, k - 1, 0, :]; a_im = s5pn[:, :, k - 1, 1, :]
        t1 = s5t[:, 16, :].rearrange("p (l k) -> p l k", l=2); t2 = s5t[:, 17, :].rearrange("p (l k) -> p l k", l=2)
        tt(t1, a_re, a_re, ALU.mult, [r_s5pn], R5)
        tt(t2, a_im, a_im, ALU.mult, [r_s5pn], R5)
        tt(s5pn[:, :, k, 0, :], t1, t2, ALU.subtract, R5, [r_s5pn])
        tt(t1, a_re, a_im, ALU.mult, [r_s5pn], R5)
        ts(s5pn[:, :, k, 1, :], t1, 2.0, ALU.mult, reads=R5, writes=[r_s5pn])
    if stop_at == "s5a":
        return P.finish()
    bk, _, br = pb()
    tr(bk[0:64, 0:128], s5t[:, 14:16, :].rearrange("p a n -> p (a n)"), c_ident[:, :], R5 + [r_const], [br])
    cf64, r_cf64 = aal(128)
    cp(cf64[0:64, :], bk[0:64, 0:128], [br], [r_cf64])
    r_scr64 = P.region("scr64")
    P.dma(scr64.ap(), cf64[0:64, :], reads=[r_cf64], writes=[r_scr64])
    cb, r_cb = aal(64 * 128)
    P.dma(cb, scr64.ap().rearrange("a s -> (a s)").partition_broadcast(128), reads=[r_scr64], writes=[r_cb])
    cb = cb.rearrange("p (c l k s) -> p c l k s", c=2, l=2, k=16)
    if stop_at == "s5b":
        return P.finish()
    r_btscr = P.region("btscr")
    bre, r_bre = aal(2048); bim, r_bim = aal(2048)
    t1, r_t1 = aal(2048); t2, r_t2 = aal(2048)
    ob, r_ob = aal(2 * 2048, BF16)
    for l in range(2):
        P.dma(bre, bpr_d[l].rearrange("p k s -> p (k s)"), writes=[r_bre])
        P.dma(bim, bpi_d[l].rearrange("p k s -> p (k s)"), writes=[r_bim])
        cbr = cb[:, 0, l, :, :].rearrange("p k s -> p (k s)"); cbi = cb[:, 1, l, :, :].rearrange("p k s -> p (k s)")
        tt(t1, bre, cbr, ALU.mult, [r_bre, r_cb], [r_t1]); tt(t2, bim, cbi, ALU.mult, [r_bim, r_cb], [r_t2])
        tt(ob[:, 0:2048], t1, t2, ALU.subtract, [r_t1, r_t2], [r_ob])
        tt(t1, bre, cbi, ALU.mult, [r_bre, r_cb], [r_t1]); tt(t2, bim, cbr, ALU.mult, [r_bim, r_cb], [r_t2])
        tt(ob[:, 2048:4096], t1, t2, ALU.add, [r_t1, r_t2], [r_ob])
        if stop_at != "s5c":
            P.dma(bt_scr[l].rearrange("c p k s -> p c (k s)"), ob.rearrange("p (c n) -> p c n", c=2), reads=[r_ob], writes=[r_btscr])
    if stop_at in ("s5c", "s5d"):
        return P.finish()
    if stop_at == "const":
        return P.finish()
    phase()
    modT, r_mod = aal(2 * 96 * 17); modT = modT.rearrange("p (l j n) -> p l j n", l=2, j=96)
    c32, r_c32 = aal(16 * 17); c32 = c32.rearrange("p (k n) -> p k n", n=17)
    csb, r_csb = aal(16 * 17, BF16); csb = csb.rearrange("p (k n) -> p k n", n=17)
    P.dma(c32, cT_d.ap(), writes=[r_c32])
    act(csb, c32, AF.Silu, reads=[r_c32], writes=[r_csb])
    for l in range(2):
        for jb in range(48):
            wv, wr = wload(w_mod_d[l, jb], 16, 256)
            for sub in range(2):
                j = jb * 2 + sub
                bk, _, br = pb()
                for kc in range(16):
                    mm(bk[:, 0:17], wv[:, kc, sub * 128:(sub + 1) * 128], csb[:, kc, :], start=(kc == 0), stop=(kc == 15),
                       reads=[wr, r_csb], writes=[br])
                act(modT[:, l, j, :], bk[:, 0:17], AF.Identity, bias=c_bmod[:, l, j:j + 1], reads=[br, r_const], writes=[r_mod])
    r_tabscr = P.region("tabscr")
    Tre, r_Tre = aal(8 * T); Tim, r_Tim = aal(8 * T)
    Tre = Tre.rearrange("p (k n) -> p k n", n=T); Tim = Tim.rearrange("p (k n) -> p k n", n=T)
    q1, r_q1 = aal(8 * (T // 2)); q2, r_q2 = aal(8 * (T // 2))
    nlv = int(math.log2(T))
    for l in range(2):
        for sg, (pwt, rpw) in enumerate(((s5pw, r_s5pw), (s5pn, r_s5pn))):
            for hf in range(2):
                ks = slice(hf * 8, hf * 8 + 8)
                cp(Tre[:, :, 0:1], pwt[:, l, 0, 0, ks].rearrange("p (k o) -> p k o", o=1), [rpw], [r_Tre], eng="dve")
                cp(Tim[:, :, 0:1], pwt[:, l, 0, 1, ks].rearrange("p (k o) -> p k o", o=1), [rpw], [r_Tim], eng="dve")
                for j in range(nlv):
                    sz = 1 << j
                    pr = bc_last(pwt[:, l, j, 0, ks], sz); pi_ = bc_last(pwt[:, l, j, 1, ks], sz)
                    a1 = q1[:, 0:8 * sz].rearrange("p (k n) -> p k n", n=sz); a2 = q2[:, 0:8 * sz].rearrange("p (k n) -> p k n", n=sz)
                    sr = Tre[:, :, 0:sz]; si = Tim[:, :, 0:sz]
                    tt(a1, sr, pr, ALU.mult, [r_Tre, rpw], [r_q1]); tt(a2, si, pi_, ALU.mult, [r_Tim, rpw], [r_q2])
                    tt(Tre[:, :, sz:2 * sz], a1, a2, ALU.subtract, [r_q1, r_q2], [r_Tre])
                    tt(a1, sr, pi_, ALU.mult, [r_Tre, rpw], [r_q1]); tt(a2, si, pr, ALU.mult, [r_Tim, rpw], [r_q2])
                    tt(Tim[:, :, sz:2 * sz], a1, a2, ALU.add, [r_q1, r_q2], [r_Tim])
                P.dma(tab_scr[l, sg * 2 + 0, :, ks, :], Tre, reads=[r_Tre], writes=[r_tabscr])
                P.dma(tab_scr[l, sg * 2 + 1, :, ks, :], Tim, reads=[r_Tim], writes=[r_tabscr])
    if dbg and "s5pw" in dbg:
        P.dma(dbg_d["s5pw"].ap(), s5pw[:, :, :, :, :], reads=[r_s5pw], is_output=True)

    for l in range(2):
        for q, (j0, addone, sc) in enumerate([(0, 0.0, 1.0), (16, 1.0, 1.0), (32, 0.0, 1.0 / ALPHA), (48, 0.0, 1.0), (64, 1.0, 1.0), (80, 0.0, 1.0 / ALPHA)]):
            ts(MODP[:, l, q, :], modT[:, l, j0:j0 + 16, 0], sc, ALU.mult, addone, ALU.add, reads=[r_mod], writes=[r_modp])
    P.dma(mod_scr.ap(), modT.rearrange("p l j n -> p (l j n)"), reads=[r_mod], writes=[r_modscr])
    MT = {"t": modT, "r": r_mod}

    def proj_fm(wv, wr, c0, N, hview, hregs):
        bk, bkb, br = pb()
        for kc in range(16):
            mm(bk[:, 0:N], wv[:, kc, c0:c0 + 128], hview[:, kc, 0:N], start=(kc == 0), stop=(kc == 15), reads=[wr, hregs[kc]], writes=[br])
        return bk, br

    def ln_tail(l, which, N, st, prompt_mode, nxt):
        ps_mean, r_pm, ps_msq, r_pq = st
        m2, r_m2 = aal(N); var, r_var = aal(N); rstd, r_rstd = aal(N); nmr, r_nmr = aal(N)
        act(m2, ps_mean[:, 0:N], AF.Square, reads=[r_pm], writes=[r_m2])
        tt(var, ps_msq[:, 0:N], m2, ALU.subtract, [r_pq, r_m2], [r_var])
        act(var, var, AF.Ln, bias=LN_EPS / (ALPHA * ALPHA), reads=[r_var], writes=[r_var])
        act(rstd, var, AF.Exp, scale=-0.5, reads=[r_var], writes=[r_rstd])
        tt(nmr, ps_mean[:, 0:N], rstd, ALU.mult, [r_pm, r_rstd], [r_nmr])
        wn, bn = ("ln1_w", "ln1_b") if which == 1 else ("ln2_w", "ln2_b")
        tmp, r_tmp = aal(2 * N, nreg=2)
        pendm = []
        for kc in range(16):
            tv = tmp[:, (kc % 2) * N:(kc % 2 + 1) * N]; rt = r_tmp[kc % 2]
            tt(tv, xT[:, kc, 0:N], rstd, ALU.mult, [r_x[kc], r_rstd], [rt])
            tt(tv, tv, nmr, ALU.subtract, [rt, r_nmr], [rt])
            act(xT[:, kc, 0:N], tv, AF.Identity, bias=lnw[bn][:, l, kc:kc + 1], scale=lnw[wn][:, l, kc:kc + 1], reads=[rt, r_const], writes=[r_x[kc]])
            if nxt is not None:
                if pendm:
                    modulate(pendm.pop(), N, nxt, prompt_mode)
                pendm.append(kc)
        if nxt is not None and pendm:
            modulate(pendm.pop(), N, nxt, prompt_mode)

    def modulate(kc, N, sel_, prompt_mode):
        l, iB, iA = sel_
        if prompt_mode:
            ts(hT[:, kc, 0:N], xT[:, kc, 0:N], MODP[:, l, iA, kc:kc + 1], ALU.mult, MODP[:, l, iB, kc:kc + 1], ALU.add,
               reads=[r_x[kc], r_modp], writes=[r_h[kc]])
        else:
            jB = 0 if iB == 0 else 48
            jA = 16 if iA == 1 else 64
            tmpm, r_tm = aal(N)
            stt(tmpm, MT["t"][:, l, jA + kc, 1:1 + N], 1.0, xT[:, kc, 0:N], ALU.add, ALU.mult, reads=[MT["r"], r_x[kc]], writes=[r_tm])
            tt(hT[:, kc, 0:N], tmpm, MT["t"][:, l, jB + kc, 1:1 + N], ALU.add, [r_tm, MT["r"]], [r_h[kc]])

    def resid_and_stats(l, which, N, blocks, prompt_mode):
        gi = 2 if which == 1 else 5
        pm, _, r_pm = pb_res(5); pq, _, r_pq = pb_res(6)
        sq, r_sq = aal(2 * N, BF16, nreg=2)
        zb, r_zb = aal(2 * N, BF16, nreg=2)
        pend = []

        def flush(item, last):
            kc_, = item
            sv = sq[:, (kc_ % 2) * N:(kc_ % 2 + 1) * N]
            zv = zb[:, (kc_ % 2) * N:(kc_ % 2 + 1) * N]
            mm(pm[:, 0:N], c_olnb[:, :], zv, start=(kc_ == 0), stop=(kc_ == 15), reads=[r_zb[kc_ % 2], r_const], writes=[r_pm])
            mm(pq[:, 0:N], c_olnb[:, :], sv, start=(kc_ == 0), stop=(kc_ == 15), reads=[r_sq[kc_ % 2], r_const], writes=[r_pq])
        for kc, bk, br in blocks:
            if prompt_mode:
                stt(xT[:, kc, 0:N], bk[:, 0:N], MODP[:, l, gi, kc:kc + 1], xT[:, kc, 0:N], ALU.mult, ALU.add,
                    reads=[br, r_modp, r_x[kc]], writes=[r_x[kc]])
            else:
                j0 = 32 if which == 1 else 80
                gt, r_gt = aal(N)
                tt(gt, bk[:, 0:N], MT["t"][:, l, j0 + kc, 1:1 + N], ALU.mult, [br, MT["r"]], [r_gt])
                stt(xT[:, kc, 0:N], gt, 1.0 / ALPHA, xT[:, kc, 0:N], ALU.mult, ALU.add, reads=[r_gt, r_x[kc]], writes=[r_x[kc]])
            act(sq[:, (kc % 2) * N:(kc % 2 + 1) * N], xT[:, kc, 0:N], AF.Square, reads=[r_x[kc]], writes=[r_sq[kc % 2]])
            cp(zb[:, (kc % 2) * N:(kc % 2 + 1) * N], xT[:, kc, 0:N], [r_x[kc]], [r_zb[kc % 2]])
            pend.append((kc,))
            if len(pend) > 1:
                flush(pend.pop(0), False)
        while pend:
            flush(pend.pop(0), True)
        return (pm, r_pm, pq, r_pq)

    def wout_blocks(l, N):
        for jo in range(16):
            wv, wr = wload2("w_out_b", (l, jo), 16, 128)
            bk, _, br = pb()
            for kc in range(16):
                mm(bk[:, 0:N], wv[:, kc, :], mixT[:, kc, 0:N], start=(kc == 0), stop=(kc == 15), reads=[wr, r_mix[kc]], writes=[br])
            yield jo, bk, br

    def ffn(l, N, prompt_mode, nxt):
        phase()
        actb, r_act = aal(FK * N, BF16, nreg=FK)
        actb = actb.rearrange("p (k n) -> p k n", n=N)
        sg, r_sg = aal(2 * N, nreg=2)
        for j in range(FK):
            wv, wr = wload2("w_ffn_in_b", (l, j), 16, 256)
            bg, _, rg = pb(); bu, _, ru = pb()
            for kc in range(16):
                mm(bg[:, 0:N], wv[:, kc, 0:128], hT[:, kc, 0:N], start=(kc == 0), stop=(kc == 15), reads=[wr, r_h[kc]], writes=[rg])
            for kc in range(16):
                mm(bu[:, 0:N], wv[:, kc, 128:256], hT[:, kc, 0:N], start=(kc == 0), stop=(kc == 15), reads=[wr, r_h[kc]], writes=[ru])
            sv = sg[:, (j % 2) * N:(j % 2 + 1) * N]
            act(sv, bg[:, 0:N], AF.Silu, reads=[rg], writes=[r_sg[j % 2]])
            tt(actb[:, j, :], sv, bu[:, 0:N], ALU.mult, [r_sg[j % 2], ru], [r_act[j]])

        def blocks():
            for jo in range(16):
                bk, _, br = pb()
                for hf in range(2):
                    wv, wr = wload2("w_ffn_out_b", (l, jo, slice(None), slice(hf * 22, (hf + 1) * 22), slice(None)), 22, 128)
                    for k2 in range(22):
                        kc = hf * 22 + k2
                        mm(bk[:, 0:N], wv[:, k2, :], actb[:, kc, :], start=(kc == 0), stop=(kc == FK - 1), reads=[wr, r_act[kc]], writes=[br])
                yield jo, bk, br
        st = resid_and_stats(l, 2, N, blocks(), prompt_mode)
        ln_tail(l, 2, N, st, prompt_mode, nxt)

    def attn_out_ln(l, N, prompt_mode):
        phase()
        st = resid_and_stats(l, 1, N, wout_blocks(l, N), prompt_mode)
        ln_tail(l, 1, N, st, prompt_mode, (l, 3, 4))

    def gn_tmps(N):
        return [aal(N) for _ in range(4)]

    def groupnorm_cols(src_bank, r_src, N, ones_c, eps, center, out_ap, r_out, gate_ap, r_gate, tmpk):
        (xs, r_xs), (sq, r_sq), (var, r_var), (m2, r_m2) = tmpk
        cp(xs, src_bank, [r_src], [r_xs])
        act(sq, src_bank, AF.Square, reads=[r_src], writes=[r_sq])
        bq, _, rq = pb()
        mm(bq[:, 0:N], ones_c, sq, reads=[r_sq, r_const], writes=[rq])
        if center:
            bm, _, rm = pb()
            mm(bm[:, 0:N], ones_c, xs, reads=[r_xs, r_const], writes=[rm])
            act(m2, bm[:, 0:N], AF.Square, reads=[rm], writes=[r_m2])
            tt(var, bq[:, 0:N], m2, ALU.subtract, [rq, r_m2], [r_var])
            tt(xs, xs, bm[:, 0:N], ALU.subtract, [r_xs, rm], [r_xs])
            act(var, var, AF.Ln, bias=eps, reads=[r_var], writes=[r_var])
        else:
            act(var, bq[:, 0:N], AF.Ln, bias=eps, reads=[rq], writes=[r_var])
        act(var, var, AF.Exp, scale=-0.5, reads=[r_var], writes=[r_var])
        tt(xs, xs, var, ALU.mult, [r_xs, r_var], [r_xs])
        xo = xs
        if len(out_ap.shape) == 3:
            xo = xs.rearrange("p (h d) -> p h d", d=out_ap.shape[2])
        tt(out_ap, xo, gate_ap, ALU.mult, [r_xs] + r_gate, r_out)

    def ret_prompt(l, it):
        phase()
        N = T
        t0 = it * T
        cosT, r_cos = aal(N); sinT, r_sin = aal(N)
        P.dma(cosT, cd["rope_cos"][:, t0:t0 + N], writes=[r_cos]); P.dma(sinT, cd["rope_sin"][:, t0:t0 + N], writes=[r_sin])
        qT, r_q = aal(4 * N, BF16, nreg=4); qT = qT.rearrange("p (h n) -> p h n", n=N)
        qwT, r_qw = aal(4 * N, BF16, nreg=4); qwT = qwT.rearrange("p (h n) -> p h n", n=N)
        kT, r_k = aal(4 * N, BF16, nreg=4); kT = kT.rearrange("p (h n) -> p h n", n=N)
        vtok, r_v = aal(4 * 512, BF16, nreg=4); vtok = vtok.rearrange("p (c n) -> p c n", n=512)
        gate, r_g = aal(4 * N, BF16, nreg=4); gate = gate.rearrange("p (h n) -> p h n", n=N)
        raw, r_raw = aal(3 * N, nreg=3); t1, r_t1 = aal(3 * N, nreg=3); t2, r_t2 = aal(3 * N, nreg=3)
        pend = []

        def rope_block(j, rw, r_rw, a1, ra1, a2, ra2):
            h = j % 4
            b2, _, br2 = pb()
            mm(b2[:, 0:N], c_pswap[:, :], rw, reads=[r_rw, r_const], writes=[br2])
            tt(a1, rw, cosT, ALU.mult, [r_rw, r_cos], [ra1])
            tt(a2, b2[:, 0:N], sinT, ALU.mult, [br2, r_sin], [ra2])
            if j < 4:
                tt(a1, a1, a2, ALU.add, [ra1, ra2], [ra1])
                cp(qT[:, h, :], a1, [ra1], [r_q[h]], eng="dve")
                tt(qwT[:, h, :].rearrange("p (c i) -> p c i", i=128), a1.rearrange("p (c i) -> p c i", i=128),
                   bc_mid(c_grow[:, h, :], N // 128), ALU.mult, [ra1, r_const], [r_qw[h]])
            else:
                tt(kT[:, h, :], a1, a2, ALU.add, [ra1, ra2], [r_k[h]])
        for b in range(4):
            wv, wr = wload2("w_in_main", (l, b), 16, 256)
            for half in range(2):
                j = b * 2 + half
                bk, br = proj_fm(wv, wr, half * 128, N, hT, r_h)
                i2 = j % 3
                rw = raw[:, i2 * N:(i2 + 1) * N]; a1 = t1[:, i2 * N:(i2 + 1) * N]; a2 = t2[:, i2 * N:(i2 + 1) * N]
                cp(rw, bk[:, 0:N], [br], [r_raw[i2]])
                if len(pend) > 1:
                    rope_block(*pend.pop(0))
                pend.append((j, rw, r_raw[i2], a1, r_t1[i2], a2, r_t2[i2]))
        for b in (4, 5):
            wv, wr = wload2("w_in_main", (l, b), 16, 256)
            for tb in range(N // 128):
                if pend and tb >= 1:
                    rope_block(*pend.pop(0))
                bk, _, br = pb()
                for kc in range(16):
                    mm(bk[:, 0:256], hT[:, kc, tb * 128:(tb + 1) * 128], wv[:, kc, :], start=(kc == 0), stop=(kc == 15), reads=[wr, r_h[kc]], writes=[br])
                cp(vtok[:, tb, (b - 4) * 256:(b - 3) * 256], bk[:, 0:256], [br], [r_v[tb]], eng=alt())
        for b in (6, 7):
            wv, wr = wload2("w_in_main", (l, b), 16, 256)
            for half in range(2):
                h = (b - 6) * 2 + half
                bk, br = proj_fm(wv, wr, half * 128, N, hT, r_h)
                i2 = h % 2
                act(raw[:, i2 * N:(i2 + 1) * N], bk[:, 0:N], AF.Silu, reads=[br], writes=[r_raw[i2]])
                ts(gate[:, h, :], raw[:, i2 * N:(i2 + 1) * N], c_gnw[:, l, h:h + 1], ALU.mult, reads=[r_raw[i2], r_const], writes=[r_g[h]])
        NCH = N // 128
        kw, r_kw = aal(NCH * 512, BF16, nreg=NCH); sm, r_sm = aal(NCH * 512, BF16, nreg=NCH)
        kvs, r_kvs = aal(NCH * 512, nreg=NCH)
        gnt = gn_tmps(512)
        S = S_ret[:, l, :, :]; rS = r_Sret[l]
        cp(Sb_ret[:, :, :], S, [rS], [r_Sbret])
        v4 = lambda a_: a_.rearrange("p (h d) -> p h d", d=128)
        for c in range(NCH):
            cs = slice(c * 128, (c + 1) * 128)
            kwv = v4(kw[:, c * 512:(c + 1) * 512]); smv = v4(sm[:, c * 512:(c + 1) * 512])
            bt, btb, rbt = pb()
            btv = v4(btb[:, 0:512])
            for h in range(4):
                tr(btv[:, h, :], kT[:, h, cs], c_identb[:, :], [r_k[h], r_const], [rbt])
            tt(kwv, btv, bc_last(c_kwcol[:, :], 128), ALU.mult, [rbt, r_const], [r_kw[c]])
            bs, _, rbs = pb()
            bsv = v4(bs[:, :])
            for h in range(4):
                mm(bsv[:, h, :], kT[:, h, cs], qT[:, h, cs], reads=[r_k[h], r_q[h]], writes=[rbs])
            tt(smv, bsv, c_maskT[:, :, :], ALU.mult, [rbs, r_const], [r_sm[c]])
        for c in range(NCH):
            kwv = v4(kw[:, c * 512:(c + 1) * 512])
            bkv, _, rbkv = pb()
            bkvv = v4(bkv[:, :])
            for h in range(4):
                mm(bkvv[:, h, :], kwv[:, h, :], vtok[:, c, h * 128:(h + 1) * 128], reads=[r_kw[c], r_v[c]], writes=[rbkv])
            cp(kvs[:, c * 512:(c + 1) * 512], bkv[:, :], [rbkv], [r_kvs[c]])
        for c in range(NCH):
            cs = slice(c * 128, (c + 1) * 128)
            smv = v4(sm[:, c * 512:(c + 1) * 512]); kvv = v4(kvs[:, c * 512:(c + 1) * 512])
            bo, _, rbo = pb()
            bov = v4(bo[:, :])
            for h in range(4):
                mm(bov[:, h, :], vtok[:, c, h * 128:(h + 1) * 128], smv[:, h, :], start=True, stop=False, reads=[r_v[c], r_sm[c]], writes=[rbo])
                mm(bov[:, h, :], Sb_ret[:, h, :], qwT[:, h, cs], start=False, stop=True, reads=[r_Sbret, r_qw[h]], writes=[rbo])
            for h in range(4):
                stt(S[:, h, :], S[:, h, :], C["ret_gC"][h], kvv[:, h, :], ALU.mult, ALU.add, reads=[rS, r_kvs[c]], writes=[rS])
            cp(Sb_ret[:, :, :], S, [rS], [r_Sbret])
            groupnorm_cols(bo[:, :], rbo, 512, c_o128[:, :], LN_EPS, True,
                           mixT[:, 0:4, cs], r_mix[0:4], gate[:, :, cs], r_g, gnt)
        if it == NT - 1:
            P.dma(retp_d[l].rearrange("h d e -> d h e"), S, reads=[rS], is_output=True)

    class _Stop(Exception):
        pass

    def gdn_prompt(l, it):
        phase()
        N = T
        NB = N // 128
        qT, r_q = aal(8 * N, BF16, nreg=8); qT = qT.rearrange("p (h n) -> p h n", n=N)
        kT, r_k = aal(8 * N, BF16, nreg=8); kT = kT.rearrange("p (h n) -> p h n", n=N)
        vT, r_v = aal(8 * N, BF16, nreg=8); vT = vT.rearrange("p (h n) -> p h n", n=N)
        cols, r_cols = aal(NB * 6 * 8, nreg=NB); cols = cols.rearrange("p (b q h) -> p b q h", q=6, h=8)
        gcT, r_gcT = aal(NB * 128, nreg=NB); gcT = gcT.rearrange("p (b n) -> p b n", n=128)
        keep = _ar["off"]
        xin, r_xin = aal(3 * (N + 3), nreg=3); yb, r_yb = aal(3 * N, nreg=3); sq, r_sq = aal(3 * N, nreg=3)
        cw = c_convw
        pend = []

        def norm_tail(ch, yv, r_y, sv, r_s):
            h = ch % 8
            isq = ch < 8
            bq, _, rq = pb()
            mm(bq[:, 0:N], (c_odk if isq else c_o1)[:, :], sv, reads=[r_s, r_const], writes=[rq])
            act(sv, bq[:, 0:N], AF.Ln, bias=(128.0 * L2_EPS if isq else L2_EPS), reads=[rq], writes=[r_s])
            act(sv, sv, AF.Exp, scale=-0.5, reads=[r_s], writes=[r_s])
            dst, rd = (qT, r_q) if isq else (kT, r_k)
            tt(dst[:, h, :], yv, sv, ALU.mult, [r_y, r_s], [rd[h]])
        for b in range(8, 20):
            wv, wr = wload2("w_in_main", (l, b), 16, 256)
            for half in range(2):
                ch = (b - 8) * 2 + half
                h = ch % 8
                bk, br = proj_fm(wv, wr, half * 128, N, hT, r_h)
                i2 = ch % 3
                xv = xin[:, i2 * (N + 3):(i2 + 1) * (N + 3)]; yv = yb[:, i2 * N:(i2 + 1) * N]; sv = sq[:, i2 * N:(i2 + 1) * N]
                cp(xv[:, 0:3], convc[:, l, ch, :], [r_convc[l]], [r_xin[i2]], eng="dve")
                cp(xv[:, 3:N + 3], bk[:, 0:N], [br], [r_xin[i2]])
                if len(pend) > 1 or (pend and ch >= 16):
                    norm_tail(*pend.pop(0))
                cp(convc[:, l, ch, :], xv[:, N:N + 3], [r_xin[i2]], [r_convc[l]], eng="dve")
                ts(yv, xv[:, 0:N], cw[:, l, ch, 0:1], ALU.mult, reads=[r_xin[i2], r_const], writes=[r_yb[i2]])
                for i in range(1, 4):
                    stt(yv, xv[:, i:N + i], cw[:, l, ch, i:i + 1], yv, ALU.mult, ALU.add, reads=[r_xin[i2], r_const, r_yb[i2]], writes=[r_yb[i2]])
                if ch >= 16:
                    act(vT[:, h, :], yv, AF.Silu, reads=[r_yb[i2]], writes=[r_v[h]])
                else:
                    act(yv, yv, AF.Silu, reads=[r_yb[i2]], writes=[r_yb[i2]])
                    act(sv, yv, AF.Square, reads=[r_yb[i2]], writes=[r_sq[i2]])
                    pend.append((ch, yv, r_yb[i2], sv, r_sq[i2]))
        for b in range(20, 24):
            wv, wr = wload2("w_in_main", (l, b), 16, 256)
            for half in range(2):
                h = (b - 20) * 2 + half
                bk, br = proj_fm(wv, wr, half * 128, N, hT, r_h)
                i2 = h % 2
                act(yb[:, i2 * N:(i2 + 1) * N], bk[:, 0:N], AF.Silu, reads=[br], writes=[r_yb[i2]])
                ts(mixT[:, 4 + h, :], yb[:, i2 * N:(i2 + 1) * N], c_gdnw[:, l:l + 1], ALU.mult, reads=[r_yb[i2], r_const], writes=[r_mix[4 + h]])
        if stop_at == "gA":
            raise _Stop()
        wab, r_wab = aal(256, BF16)
        wabv = wab.rearrange("p (k c) -> p k c", c=16)
        P.dma(wabv, w_ab_d[l], writes=[r_wab], queue="pool")
        ab, r_ab = aal(2 * 16, nreg=2)
        for tb in range(NB):
            bk, _, br = pb()
            for kc in range(16):
                mm(bk[:, 0:16], hT[:, kc, tb * 128:(tb + 1) * 128], wabv[:, kc, :], start=(kc == 0), stop=(kc == 15), reads=[r_wab, r_h[kc]], writes=[br])
            i2 = tb % 2
            av = ab[:, i2 * 16:(i2 + 1) * 16]; ra = r_ab[i2]
            cv = cols[:, tb, :, :]; rc = r_cols[tb]
            tt(av[:, 0:8], bk[:, 0:8], c_dtb[:, l, :], ALU.add, [br, r_const], [ra])
            act(av[:, 0:8], av[:, 0:8], AF.Exp, reads=[ra], writes=[ra])
            act(av[:, 0:8], av[:, 0:8], AF.Ln, bias=1.0, reads=[ra], writes=[ra])
            tt(av[:, 0:8], av[:, 0:8], c_nega[:, l, :], ALU.mult, [ra, r_const], [ra])
            act(cv[:, 0, :], bk[:, 8:16], AF.Sigmoid, reads=[br], writes=[rc])
            ts(cv[:, 1, :], cv[:, 0, :], -1.0, ALU.mult, reads=[rc], writes=[rc])
            b2, _, br2 = pb()
            mm(b2[:, 0:8], c_tri[:, :], av[:, 0:8], reads=[ra, r_const], writes=[br2])
            mm(b2[0:8, 128:256], av[:, 0:8], c_tri[:, :], reads=[ra, r_const], writes=[br2])
            mm(b2[:, 256:264], c_blk1[:, :], av[:, 0:8], reads=[ra, r_const], writes=[br2])
            cp(cv[:, 2, :], b2[:, 0:8], [br2], [rc], eng="dve")
            cp(gcT[0:8, tb, :], b2[0:8, 128:256], [br2], [r_gcT[tb]], eng="dve")
            act(cv[:, 3, :], b2[:, 0:8], AF.Exp, reads=[br2], writes=[rc])
            tt(cv[:, 3, :], cv[:, 3, :], cv[:, 0, :], ALU.mult, [rc], [rc])
            tt(cv[:, 4, :], b2[:, 256:264], cv[:, 2, :], ALU.subtract, [br2, rc], [rc])
            act(cv[:, 4, :], cv[:, 4, :], AF.Exp, reads=[rc], writes=[rc])
        if stop_at == "gB":
            raise _Stop()
        phase(keep)
        S = S_gdn[:, l, :, :]; rS = r_Sgdn[l]
        cp(Sb_gdn[:, :, :], S, [rS], [r_Sbgdn])
        v3 = lambda a: a.rearrange("p (h d) -> p h d", d=128)
        ktok_g, r_ktg = aal(1024, BF16); ktok_d, r_ktd = aal(1024, BF16); vtok, r_vt = aal(1024, BF16)
        ktok_g = v3(ktok_g); ktok_d = v3(ktok_d); vtok = v3(vtok)
        EG, r_EG = aal(1024, nreg=2); EG = v3(EG)
        dUf, r_dU = aal(512); dLf, r_dL = aal(512)
        dU = v3(dUf); dL = v3(dLf)
        NB_ = [[aal(512) for _ in range(4)] for _ in range(2)]
        YB_ = [(aal(512), aal(512, BF16)) for _ in range(2)]
        attnT, r_at = aal(1024, BF16, nreg=2); attnT = v3(attnT)
        qdT, r_qd = aal(1024, BF16, nreg=2); qdT = v3(qdT)
        u_sb, r_u = aal(1024, nreg=2); u_sb = v3(u_sb)
        wTn, r_w = aal(1024, BF16, nreg=2); wTn = v3(wTn)
        vnew, r_vn = aal(1024, BF16); vnew = v3(vnew)
        o_sb, r_o = aal(1024, nreg=2); o_sb = v3(o_sb)
        for cpi in range(NB):
            cs = slice(cpi * 128, (cpi + 1) * 128)
            cv = cols[:, cpi, :, :]; rc = r_cols[cpi]
            for (src, rsrc, outs) in ((kT, r_k, "k"), (vT, r_v, "v")):
                for hg in range(2):
                    bt, btb, rbt = pb()
                    btv = btb[:, 0:512].rearrange("p (h d) -> p h d", d=128)
                    for h4 in range(4):
                        h = hg * 4 + h4
                        tr(btv[:, h4, :], src[:, h, cs], c_identb[:, :], [rsrc[h], r_const], [rbt])
                    hs = slice(hg * 4, hg * 4 + 4)
                    if outs == "k":
                        tt(ktok_g[:, hs, :], btv, bc_last(cv[:, 3, hs], 128), ALU.mult, [rbt, rc], [r_ktg])
                        tt(ktok_d[:, hs, :], btv, bc_last(cv[:, 4, hs], 128), ALU.mult, [rbt, rc], [r_ktd])
                    else:
                        tt(vtok[:, hs, :], btv, bc_last(cv[:, 0, hs], 128), ALU.mult, [rbt, rc], [r_vt])
            for hg in range(2):
                hs = slice(hg * 4, hg * 4 + 4)
                (NmA, r_NmA), (LmA, r_LmA) = NB_[hg][0], NB_[hg][1]
                (Y, r_Y), _yb = YB_[hg]
                bR, _, rR = pb(); bRv = v3(bR[:, :])
                for h4 in range(4):
                    h = hg * 4 + h4
                    mm(bRv[:, h4, :], col_bc1(c_ident[0:8, h:h + 1], 128), gcT[0:8, cpi, :], reads=[r_gcT[cpi], r_const], writes=[rR])
                act(EG[:, hs, :], bRv, AF.Exp, reads=[rR], writes=[r_EG[hg]])
                for h4 in range(4):
                    h = hg * 4 + h4
                    stt(dU[:, h4, :], bRv[:, h4, :], cv[:, 2, h:h + 1], c_maskU[:, :], ALU.subtract, ALU.add, reads=[rR, rc, r_const], writes=[r_dU])
                    stt(dL[:, h4, :], bRv[:, h4, :], cv[:, 2, h:h + 1], c_maskL[:, :], ALU.subtract, ALU.add, reads=[rR, rc, r_const], writes=[r_dL])
                act(dU, dU, AF.Exp, reads=[r_dU], writes=[r_dU])
                act(dL, dL, AF.Exp, scale=-1.0, reads=[r_dL], writes=[r_dL])
                bK, _, rK = pb(); bKv = v3(bK[:, :])
                for h4 in range(4):
                    h = hg * 4 + h4
                    mm(bKv[:, h4, :], kT[:, h, cs], kT[:, h, cs], reads=[r_k[h]], writes=[rK])
                Nm = v3(NmA); Lm = v3(LmA)
                for h4 in range(4):
                    h = hg * 4 + h4
                    stt(Nm[:, h4, :], bKv[:, h4, :], cv[:, 1, h:h + 1], dL[:, h4, :], ALU.mult, ALU.mult, reads=[rK, rc, r_dL], writes=[r_NmA])
                bL, _, rL = pb(); bLv = v3(bL[:, :])
                for h4 in range(4):
                    tr(bLv[:, h4, :], Nm[:, h4, :], c_ident[:, :], [r_NmA, r_const], [rL])
                cp(Lm, bLv, [rL], [r_LmA])
                bQ, _, rQ = pb(); bQv = v3(bQ[:, :])
                for h4 in range(4):
                    h = hg * 4 + h4
                    mm(bQv[:, h4, :], kT[:, h, cs], qT[:, h, cs], reads=[r_k[h], r_q[h]], writes=[rQ])
                tt(attnT[:, hs, :], bQv, dU, ALU.mult, [rQ, r_dU], [r_at[hg]])
                tt(qdT[:, hs, :], qT[:, hs, cs], EG[:, hs, :], ALU.mult, r_q[hg * 4:hg * 4 + 4] + [r_EG[hg]], [r_qd[hg]])
                tt(v3(Y), Lm, bc_mid(c_ident[:, :], 4), ALU.add, [r_LmA, r_const], [r_Y])
            cur = [(NB_[hg][0], NB_[hg][1]) for hg in range(2)]
            nxt = [(NB_[hg][2], NB_[hg][3]) for hg in range(2)]
            for lev in range(1, 6):
                for hg in range(2):
                    (cNf, rN), (cLf, rLm) = cur[hg]; (nNf, rN2), (nLf, rL2) = nxt[hg]
                    cN = v3(cNf); cL = v3(cLf); nN = v3(nNf); nL = v3(nLf)
                    bb, _, rbb = pb(); bbv = v3(bb[:, :])
                    for h4 in range(4):
                        mm(bbv[:, h4, :], cL[:, h4, :], cN[:, h4, :], reads=[rN, rLm], writes=[rbb])
                    cp(nN, bbv, [rbb], [rN2], eng="dve")
                    if lev < 5:
                        ba, _, rba = pb(); bav = v3(ba[:, :])
                        for h4 in range(4):
                            mm(bav[:, h4, :], cN[:, h4, :], cL[:, h4, :], reads=[rN, rLm], writes=[rba])
                        cp(nL, bav, [rba], [rL2])
                for hg in range(2):
                    (nNf, rN2), _ = nxt[hg]
                    (Y, r_Y), _yb = YB_[hg]
                    nN = v3(nNf); Yv = v3(Y)
                    bc_, _, rbc = pb(); bcv = v3(bc_[:, :])
                    for h4 in range(4):
                        mm(bcv[:, h4, :], nN[:, h4, :], Yv[:, h4, :], reads=[rN2, r_Y], writes=[rbc])
                    tt(Yv, Yv, bcv, ALU.add, [r_Y, rbc], [r_Y])
                cur, nxt = nxt, cur
            for hg in range(2):
                hs = slice(hg * 4, hg * 4 + 4)
                (Y, r_Y), (Yb, r_Yb) = YB_[hg]
                cp(v3(Yb), v3(Y), [r_Y], [r_Yb])
                Ybv = v3(Yb)
                bU, _, rU = pb(); bUv = v3(bU[:, :])
                for h4 in range(4):
                    h = hg * 4 + h4
                    mm(bUv[:, h4, :], Ybv[:, h4, :], vtok[:, h, :], reads=[r_Yb, r_vt], writes=[rU])
                cp(u_sb[:, hs, :], bUv, [rU], [r_u[hg]])
                bW, _, rW = pb(); bWv = v3(bW[:, :])
                for h4 in range(4):
                    h = hg * 4 + h4
                    mm(bWv[:, h4, :], ktok_g[:, h, :], Ybv[:, h4, :], reads=[r_Yb, r_ktg], writes=[rW])
                amul(wTn[:, hs, :], bWv, -1.0, reads=[rW], writes=[r_w[hg]])
            for ch in range(2):
                rr = slice(ch * 64, (ch + 1) * 64)
                for hg in range(2):
                    bws, _, rws = pb(); bwsv = v3(bws[:, :])
                    for h4 in range(4):
                        h = hg * 4 + h4
                        mm(bwsv[rr, h4, :], wTn[:, h, rr], Sb_gdn[:, h, :], reads=[r_w[hg], r_Sbgdn], writes=[rws])
                    hs = slice(hg * 4, hg * 4 + 4)
                    tt(vnew[rr, hs, :], u_sb[rr, hs, :], bwsv[rr, :, :], ALU.add, [r_u[hg], rws], [r_vn])
                for hg in range(2):
                    bo, _, rbo = pb(); bov = bo[:, 0:256].rearrange("p (h i) -> p h i", i=64)
                    for h4 in range(4):
                        h = hg * 4 + h4
                        mm(bov[:, h4, :], Sb_gdn[:, h, :], qdT[:, h, rr], start=True, stop=False, reads=[r_Sbgdn, r_qd[hg]], writes=[rbo])
                        mm(bov[:, h4, :], vnew[rr, h, :], attnT[rr, h, rr], start=False, stop=True, reads=[r_vn, r_at[hg]], writes=[rbo])
                    hs = slice(hg * 4, hg * 4 + 4)
                    cp(o_sb[:, hs, rr], bov, [rbo], [r_o[hg]])
                for hg in range(2):
                    bkd, _, rkd = pb(); bkdv = v3(bkd[:, :])
                    for h4 in range(4):
                        h = hg * 4 + h4
                        mm(bkdv[:, h4, :], ktok_d[rr, h, :], vnew[rr, h, :], reads=[r_ktd, r_vn], writes=[rkd])
                    for h4 in range(4):
                        h = hg * 4 + h4
                        last = ch * 64 + 63
                        stt(S[:, h, :], S[:, h, :], EG[:, h, last:last + 1], bkdv[:, h4, :], ALU.mult, ALU.add, reads=[rS, r_EG[hg], rkd], writes=[rS])
                cp(Sb_gdn[:, :, :], S, [rS], [r_Sbgdn])
            for hg in range(2):
                hs = slice(hg * 4, hg * 4 + 4)
                sq2, r_sq2 = (dUf, r_dU) if hg == 0 else (dLf, r_dL)
                act(sq2, o_sb[:, hs, :].rearrange("p h d -> p (h d)"), AF.Square, reads=[r_o[hg]], writes=[r_sq2])
                bq, _, rq = pb()
                mm(bq[:, :], c_o128[:, :], sq2, reads=[r_sq2, r_const], writes=[rq])
                act(sq2, bq[:, :], AF.Ln, bias=RMS_EPS, reads=[rq], writes=[r_sq2])
                act(sq2, sq2, AF.Exp, scale=-0.5, reads=[r_sq2], writes=[r_sq2])
                tt(sq2, sq2, o_sb[:, hs, :].rearrange("p h d -> p (h d)"), ALU.mult, [r_sq2, r_o[hg]], [r_sq2])
                mx = mixT[:, 4 + hg * 4:8 + hg * 4, cs]
                tt(mx, v3(sq2), mx, ALU.mult, [r_sq2] + r_mix[4 + hg * 4:8 + hg * 4], r_mix[4 + hg * 4:8 + hg * 4])
        if it == NT - 1:
            P.dma(gdnp_d[l].rearrange("h d e -> d h e"), S, reads=[rS], is_output=True)
            P.dma(convp_d[l], convc[:, l, :, :], reads=[r_convc[l]], is_output=True)

    def s5_glu(l, N, yg, r_yg, ygb, r_ygb, tmps=None):
        wv, wr = wload2("w_glu_b", (l,), 4, 512)
        if tmps is None:
            sgl, r_sgl = aal(2 * N, nreg=2)
            tmps = [(sgl[:, 0:N], r_sgl[0]), (sgl[:, N:2 * N], r_sgl[1])]
        for jo in range(4):
            bk, _, br = pb()
            for kc in range(4):
                mm(bk[:, 0:N], wv[:, kc, jo * 128:(jo + 1) * 128], ygb[:, kc, :], start=(kc == 0), stop=(kc == 3), reads=[wr, r_ygb[kc]], writes=[br])
            sv, rsv = tmps[jo % 2]
            act(sv, bk[:, 0:N], AF.Sigmoid, reads=[br], writes=[rsv])
            tt(mixT[:, 12 + jo, 0:N], yg[:, jo, :], sv, ALU.mult, [r_yg[jo], rsv], [r_mix[12 + jo]])

    def s5_prompt(l, it):
        phase()
        N = T
        su32, r_su32 = aal(4 * N, nreg=4); su32 = su32.rearrange("p (k n) -> p k n", n=N)
        sub, r_sub = aal(4 * N, BF16, nreg=4); sub = sub.rearrange("p (k n) -> p k n", n=N)
        for b in range(2):
            wv, wr = wload2("w_in_su", (l, b), 16, 256)
            for half in range(2):
                kc = b * 2 + half
                bk, br = proj_fm(wv, wr, half * 128, N, hT, r_h)
                cp(su32[:, kc, :], bk[:, 0:N], [br], [r_su32[kc]])
                cp(sub[:, kc, :], bk[:, 0:N], [br], [r_sub[kc]], eng="dve")
        BT, r_BT = aal(2 * 2048, BF16); BT = BT.rearrange("p (c k s) -> p c k s", c=2, k=16)
        P.dma(BT.rearrange("p c k s -> p c (k s)"), bt_scr[l].rearrange("c p k s -> p c (k s)"), reads=[r_btscr], writes=[r_BT])
        CTr, r_CTr = aal(2048, BF16); CTi, r_CTi = aal(2048, BF16)
        CTr = CTr.rearrange("p (k s) -> p k s", k=16); CTi = CTi.rearrange("p (k s) -> p k s", k=16)
        P.dma(CTr, cpr_d[l], writes=[r_CTr], queue="pool"); P.dma(CTi, cpi_d[l], writes=[r_CTi], queue="pool")
        tabb, r_tab = aal(2 * 4 * N, nreg=2)
        bufs = [aal(N) for _ in range(4)]
        hbuf = [aal(N) for _ in range(4)]
        hb, r_hb = aal(4 * N, BF16, nreg=4)
        yg, r_yg = aal(4 * N, nreg=4); yg = yg.rearrange("p (k n) -> p k n", n=N)
        ygb, r_ygb = aal(4 * N, BF16, nreg=4); ygb = ygb.rearrange("p (k n) -> p k n", n=N)
        by = None
        ones_b = col_bc1(c_o1[:, 0:1], N)

        def bu_mm(k_):
            b1, _, r1 = pb(); b2, _, r2 = pb()
            mm(b1[:, 0:N], BT[:, 0, k_, :], sub[:, k_ // 4, :], reads=[r_BT, r_sub[k_ // 4]], writes=[r1])
            mm(b2[:, 0:N], BT[:, 1, k_, :], sub[:, k_ // 4, :], reads=[r_BT, r_sub[k_ // 4]], writes=[r2])
            return b1, r1, b2, r2
        bu_next = None
        for k in range(16):
            kc = k // 4
            i2 = k % 2
            tab = tabb[:, i2 * 4 * N:(i2 + 1) * 4 * N].rearrange("p (c n) -> p c n", n=N); rt = r_tab[i2]
            P.dma(tab, tab_scr[l, :, :, k, :].rearrange("c p n -> p c n"), reads=[r_tabscr], writes=[rt])
            EPr, EPi, ENr, ENi = tab[:, 0, :], tab[:, 1, :], tab[:, 2, :], tab[:, 3, :]
            if k == 0:
                bu_next = bu_mm(0)
            bre, rbre, bim, rbim = bu_next
            if k + 1 < 16:
                bu_next = bu_mm(k + 1)
            (A, rA), (B, rB), (XR, rXR), (XI, rXI) = bufs
            ytmp, r_ytmp = A, rA
            (HR, rHR), (HI, rHI) = hbuf[i2 * 2], hbuf[i2 * 2 + 1]
            tt(A, ENr, bre[:, 0:N], ALU.mult, [rt, rbre], [rA]); tt(B, ENi, bim[:, 0:N], ALU.mult, [rt, rbim], [rB])
            tt(XR, A, B, ALU.subtract, [rA, rB], [rXR])
            tt(A, ENr, bim[:, 0:N], ALU.mult, [rt, rbim], [rA]); tt(B, ENi, bre[:, 0:N], ALU.mult, [rt, rbre], [rB])
            tt(XI, A, B, ALU.add, [rA, rB], [rXI])
            P.op("dve", lambda e, A=A, XR=XR, c0=ssmc[:, l, k, 0:1]: e.tensor_tensor_scan(A, ones_b, XR, c0, ALU.mult, ALU.add), [rXR, r_ssmc[l][k], r_const], [rA])
            P.op("dve", lambda e, B=B, XI=XI, c0=ssmc[:, l, k, 1:2]: e.tensor_tensor_scan(B, ones_b, XI, c0, ALU.mult, ALU.add), [rXI, r_ssmc[l][k], r_const], [rB])
            tt(XR, EPr, A, ALU.mult, [rt, rA], [rXR]); tt(XI, EPi, B, ALU.mult, [rt, rB], [rXI])
            tt(HR, XR, XI, ALU.subtract, [rXR, rXI], [rHR])
            tt(XR, EPr, B, ALU.mult, [rt, rB], [rXR]); tt(XI, EPi, A, ALU.mult, [rt, rA], [rXI])
            tt(HI, XR, XI, ALU.add, [rXR, rXI], [rHI])
            cp(ssmc[:, l, k, 0:1], HR[:, N - 1:N], [rHR], [r_ssmc[l][k]], eng="dve")
            cp(ssmc[:, l, k, 1:2], HI[:, N - 1:N], [rHI], [r_ssmc[l][k]], eng="dve")
            hbr = hb[:, (i2 * 2) * N:(i2 * 2 + 1) * N]; hbi = hb[:, (i2 * 2 + 1) * N:(i2 * 2 + 2) * N]
            cp(hbr, HR, [rHR], [r_hb[i2 * 2]])
            amul(hbi, HI, -1.0, reads=[rHI], writes=[r_hb[i2 * 2 + 1]])
            if k % 4 == 0:
                by, _, rby = pb_res(7)
            mm(by[:, 0:N], CTr[:, k, :], hbr, start=(k % 4 == 0), stop=False, reads=[r_CTr, r_hb[i2 * 2]], writes=[rby])
            mm(by[:, 0:N], CTi[:, k, :], hbi, start=False, stop=(k % 4 == 3), reads=[r_CTi, r_hb[i2 * 2 + 1]], writes=[rby])
            if k % 4 == 3:
                stt(ytmp, su32[:, kc, :], c_ssmd[:, l, kc:kc + 1], by[:, 0:N], ALU.mult, ALU.add, reads=[r_su32[kc], r_const, rby], writes=[r_ytmp])
                act(yg[:, kc, :], ytmp, AF.Gelu, reads=[r_ytmp], writes=[r_yg[kc]])
                cp(ygb[:, kc, :], yg[:, kc, :], [r_yg[kc]], [r_ygb[kc]], eng="dve")
        s5_glu(l, N, yg, r_yg, ygb, r_ygb, tmps=[bufs[2], bufs[3]])
        if it == NT - 1:
            P.dma(ssmp_d[l], ssmc[:, l, :, :], reads=r_ssmc[l], is_output=True)


    scr_l = P.dram("scr_l", [2, 2048], F32)
    r_scrs = P.region("scr_s"); r_scrl = P.region("scr_l")

    def col_bc(a, n):
        return bass.AP(a.tensor, a.offset, [list(a.ap[0]), [0, n]])

    def tokproj(l, blocks, dst, r_dst):
        for i, b in enumerate(blocks):
            wv, wr = wload2("w_in_main", (l, b), 16, 256)
            bk, _, br = pb()
            for kc in range(16):
                mm(bk[0:NS, 0:256], hT[:, kc, 0:NS], wv[:, kc, :], start=(kc == 0), stop=(kc == 15), reads=[wr, r_h[kc]], writes=[br])
            cp(dst[0:NS, i * 256:(i + 1) * 256], bk[0:NS, 0:256], [br], [r_dst], eng=alt())

    def ret_sample(l):
        phase()
        N = NS
        ptok, r_pt = aal(1536)
        tokproj(l, range(6), ptok, r_pt)
        csr, r_csr = aal(256)
        P.dma(csr[0:NS, 0:128], cd["rope_cos_s"].ap().rearrange("d o -> (d o)").partition_broadcast(NS), writes=[r_csr])
        P.dma(csr[0:NS, 128:256], cd["rope_sin_s"].ap().rearrange("d o -> (d o)").partition_broadcast(NS), writes=[r_csr])
        qk = ptok[0:NS, 0:1024].rearrange("p (g d) -> p g d", d=128)
        tq, r_tq = aal(1024); tqv = tq[0:NS, :].rearrange("p (g d) -> p g d", d=128)
        rq, r_rq = aal(1024); rqv = rq[0:NS, :].rearrange("p (g d) -> p g d", d=128)
        tt(tqv[:, :, 0:64], qk[:, :, 64:128], bc_mid(csr[0:NS, 128:192], 8), ALU.mult, [r_pt, r_csr], [r_tq])
        tt(tqv[:, :, 64:128], qk[:, :, 0:64], bc_mid(csr[0:NS, 192:256], 8), ALU.mult, [r_pt, r_csr], [r_tq])
        tt(rqv, qk, bc_mid(csr[0:NS, 0:128], 8), ALU.mult, [r_pt, r_csr], [r_rq])
        tt(rqv, rqv, tqv, ALU.add, [r_rq, r_tq], [r_rq])
        ts(rqv[:, 4:8, :], rqv[:, 4:8, :], 128 ** -0.5, ALU.mult, reads=[r_rq], writes=[r_rq])
        bk, _, br = pb()
        for g in range(8):
            tr(bk[:, g * NS:(g + 1) * NS], rqv[:, g, :], c_ident[0:NS, 0:NS], [r_rq, r_const], [br])
        qkT, r_qkT = aal(8 * NS); qkT = qkT.rearrange("p (g s) -> p g s", s=NS)
        cp(qkT, bk[:, 0:8 * NS].rearrange("p (g s) -> p g s", s=NS), [br], [r_qkT])
        P.dma(scr_s[:, 0:512], ptok[0:NS, 1024:1536], reads=[r_pt], writes=[r_scrs])
        gate, r_gate = aal(4 * NS); gate = gate.rearrange("p (h s) -> p h s", s=NS)
        gtmp, r_gtmp = aal(NS)
        for b in (6, 7):
            wv, wr = wload2("w_in_main", (l, b), 16, 256)
            for half in range(2):
                h = (b - 6) * 2 + half
                bk2, br2 = proj_fm(wv, wr, half * 128, N, hT, r_h)
                act(gtmp, bk2[:, 0:N], AF.Silu, reads=[br2], writes=[r_gtmp])
                ts(gate[:, h, :], gtmp, c_gnw[:, l, h:h + 1], ALU.mult, reads=[r_gtmp, r_const], writes=[r_gate])
        po, _, rpo = pb_res(7)
        S0b, r_S0 = aal(2 * 512, nreg=2); vbb, r_vb = aal(2 * 512, nreg=2); Snb, r_Sn = aal(2 * 512, nreg=8)
        for s_ in range(NS):
            i2 = s_ % 2
            S0 = S0b[:, i2 * 512:(i2 + 1) * 512].rearrange("p (h e) -> p h e", e=128)
            Sn = Snb[:, i2 * 512:(i2 + 1) * 512].rearrange("p (h e) -> p h e", e=128)
            vb = vbb[:, i2 * 512:(i2 + 1) * 512]
            P.dma(S0, st_ret_d[l, s_].rearrange("h d e -> d h e"), writes=[r_S0[i2]])
            P.dma(vb, scr_s[s_, 0:512].partition_broadcast(128), reads=[r_scrs], writes=[r_vb[i2]])
            for h in range(4):
                ts(Sn[:, h, :], vb[:, h * 128:(h + 1) * 128], qkT[:, 4 + h, s_:s_ + 1], ALU.mult, reads=[r_vb[i2], r_qkT], writes=[r_Sn[i2 * 4 + h]])
                stt(Sn[:, h, :], S0[:, h, :], C["ret_g1"][h], Sn[:, h, :], ALU.mult, ALU.add, reads=[r_S0[i2], r_Sn[i2 * 4 + h]], writes=[r_Sn[i2 * 4 + h]])
                mm(po[:, h * NS + s_:h * NS + s_ + 1], Sn[:, h, :], qkT[:, h, s_:s_ + 1], reads=[r_Sn[i2 * 4 + h], r_qkT], writes=[rpo])
            P.dma(rets_d[l, s_].rearrange("h d e -> d h e"), Sn, reads=r_Sn[i2 * 4:i2 * 4 + 4], is_output=True)
        groupnorm_cols(po[:, 0:4 * NS], rpo, 4 * NS, c_o128[:, :], LN_EPS, True,
                       mixT[:, 0:4, 0:NS], r_mix[0:4], gate, [r_gate], gn_tmps(4 * NS))

    def gdn_sample(l):
        phase()
        N = NS
        xin, r_xin = aal(3072)
        tokproj(l, range(8, 20), xin, r_xin)
        yq, r_yq = aal(3072)
        mark1 = _ar["off"]
        bufb, r_buf = aal(2 * 1536, nreg=2); wrb, r_wr = aal(2 * 2048, nreg=2); ytb, r_yt = aal(2 * 512, nreg=2)
        for cc in range(6):
            i2 = cc % 2
            cs_ = slice(cc * 512, (cc + 1) * 512)
            buf = bufb[0:NS, i2 * 1536:(i2 + 1) * 1536].rearrange("p (i c) -> p i c", c=512)
            wr_ = wrb[0:NS, i2 * 2048:(i2 + 1) * 2048].rearrange("p (i c) -> p i c", c=512)
            yt = ytb[0:NS, i2 * 512:(i2 + 1) * 512]
            P.dma(buf, st_conv_d[l, :, :, cs_], writes=[r_buf[i2]])
            for i in range(4):
                P.dma(wr_[:, i, :], convw_raw_d[l, i, cs_].partition_broadcast(NS), writes=[r_wr[i2]])
            P.dma(convs_d[l, :, 0:2, cs_], buf[:, 1:3, :], reads=[r_buf[i2]], is_output=True)
            P.dma(convs_d[l, :, 2, cs_], xin[0:NS, cs_], reads=[r_xin], is_output=True)
            tt(yt, xin[0:NS, cs_], wr_[:, 3, :], ALU.mult, [r_xin, r_wr[i2]], [r_yt[i2]])
            tt(buf, buf, wr_[:, 0:3, :], ALU.mult, [r_buf[i2], r_wr[i2]], [r_buf[i2]])
            for i in range(3):
                tt(yt, yt, buf[:, i, :], ALU.add, [r_yt[i2], r_buf[i2]], [r_yt[i2]])
            act(yq[0:NS, cs_], yt, AF.Silu, reads=[r_yt[i2]], writes=[r_yq])
        phase(mark1)
        sq, r_sq = aal(2048); ssq, r_ssq = aal(16)
        y3 = yq[0:NS, 0:2048].rearrange("p (g d) -> p g d", d=128)
        tt(sq[0:NS, :], yq[0:NS, 0:2048], yq[0:NS, 0:2048], ALU.mult, [r_yq], [r_sq])
        P.op("dve", lambda e: e.tensor_reduce(ssq[0:NS, :], sq[0:NS, :].rearrange("p (g d) -> p g d", d=128), mybir.AxisListType.X, ALU.add), [r_sq], [r_ssq])
        act(ssq[0:NS, :], ssq[0:NS, :], AF.Sqrt, bias=L2_EPS, reads=[r_ssq], writes=[r_ssq])
        recip(ssq[0:NS, :], ssq[0:NS, :], [r_ssq], [r_ssq])
        ts(ssq[0:NS, 0:8], ssq[0:NS, 0:8], 128 ** -0.5, ALU.mult, reads=[r_ssq], writes=[r_ssq])
        sq3 = sq[0:NS, :].rearrange("p (g d) -> p g d", d=128)
        tt(sq3, y3, bc_last(ssq[0:NS, :], 128), ALU.mult, [r_yq, r_ssq], [r_sq])
        bk, _, br = pb()
        for g in range(16):
            tr(bk[:, g * NS:(g + 1) * NS], sq3[:, g, :], c_ident[0:NS, 0:NS], [r_sq, r_const], [br])
        qkT, r_qkT = aal(16 * NS); qkT = qkT.rearrange("p (g s) -> p g s", s=NS)
        cp(qkT, bk[:, 0:16 * NS].rearrange("p (g s) -> p g s", s=NS), [br], [r_qkT])
        P.dma(scr_s[:, 0:1024], yq[0:NS, 2048:3072], reads=[r_yq], writes=[r_scrs])
        wab, r_wab = aal(256, BF16)
        wabv = wab.rearrange("p (k c) -> p k c", c=16)
        P.dma(wabv, w_ab_d[l], writes=[r_wab], queue="pool")
        bk2, _, br2 = pb()
        for kc in range(16):
            mm(bk2[0:NS, 0:16], hT[:, kc, 0:NS], wabv[:, kc, :], start=(kc == 0), stop=(kc == 15), reads=[r_wab, r_h[kc]], writes=[br2])
        eb, r_eb = aal(16)
        tt(eb[0:NS, 0:8], bk2[0:NS, 0:8], c_dtb[0:NS, l, :], ALU.add, [br2, r_const], [r_eb])
        act(eb[0:NS, 0:8], eb[0:NS, 0:8], AF.Exp, reads=[r_eb], writes=[r_eb])
        act(eb[0:NS, 0:8], eb[0:NS, 0:8], AF.Ln, bias=1.0, reads=[r_eb], writes=[r_eb])
        tt(eb[0:NS, 0:8], eb[0:NS, 0:8], c_nega[0:NS, l, :], ALU.mult, [r_eb, r_const], [r_eb])
        act(eb[0:NS, 0:8], eb[0:NS, 0:8], AF.Exp, reads=[r_eb], writes=[r_eb])
        act(eb[0:NS, 8:16], bk2[0:NS, 8:16], AF.Sigmoid, reads=[br2], writes=[r_eb])
        P.dma(scr_s[:, 1024:1040], eb[0:NS, 0:16], reads=[r_eb], writes=[r_scrs])
        egb, r_egb = aal(NS * 16)
        for s_ in range(NS):
            P.dma(egb[:, s_ * 16:(s_ + 1) * 16], scr_s[s_, 1024:1040].partition_broadcast(128), reads=[r_scrs], writes=[r_egb])
        gate, r_gate = aal(8 * NS); gate = gate.rearrange("p (h s) -> p h s", s=NS)
        gtmp, r_gtmp = aal(NS)
        for b in range(20, 24):
            wv, wr = wload2("w_in_main", (l, b), 16, 256)
            for half in range(2):
                h = (b - 20) * 2 + half
                bk3, br3 = proj_fm(wv, wr, half * 128, N, hT, r_h)
                act(gtmp, bk3[:, 0:N], AF.Silu, reads=[br3], writes=[r_gtmp])
                ts(gate[:, h, :], gtmp, c_gdnw[:, l:l + 1], ALU.mult, reads=[r_gtmp, r_const], writes=[r_gate])
        po, _, rpo = pb_res(7)
        S0b, r_S0 = aal(2 * 1024, nreg=2); vbb, r_vb = aal(2 * 1024, nreg=2); Snb, r_Sn = aal(2 * 1024, nreg=16)
        kb, r_kb = aal(2 * 128, nreg=2); tb_, r_tb = aal(2 * 128, nreg=2)
        for s_ in range(NS):
            i2 = s_ % 2
            S0 = S0b[:, i2 * 1024:(i2 + 1) * 1024].rearrange("p (h e) -> p h e", e=128)
            Sn = Snb[:, i2 * 1024:(i2 + 1) * 1024].rearrange("p (h e) -> p h e", e=128)
            vb = vbb[:, i2 * 1024:(i2 + 1) * 1024]
            P.dma(S0, st_gdn_d[l, s_].rearrange("h d e -> d h e"), writes=[r_S0[i2]])
            P.dma(vb, scr_s[s_, 0:1024].partition_broadcast(128), reads=[r_scrs], writes=[r_vb[i2]])
            pks = {}

            def stage1(h):
                kcol = qkT[:, 8 + h, s_:s_ + 1]
                ts(Sn[:, h, :], S0[:, h, :], egb[:, s_ * 16 + h:s_ * 16 + h + 1], ALU.mult, reads=[r_S0[i2], r_egb], writes=[r_Sn[i2 * 8 + h]])
                pk, _, rpk = pb()
                mm(pk[:, 0:128], col_bc(kcol, 128), Sn[:, h, :], reads=[r_qkT, r_Sn[i2 * 8 + h]], writes=[rpk])
                pks[h] = (pk, rpk)

            def stage2(h):
                j2 = h % 2
                tv = tb_[:, j2 * 128:(j2 + 1) * 128]
                kcol = qkT[:, 8 + h, s_:s_ + 1]
                pk, rpk = pks.pop(h)
                stt(tv, pk[:, 0:128], -1.0, vb[:, h * 128:(h + 1) * 128], ALU.mult, ALU.add, reads=[rpk, r_vb[i2]], writes=[r_tb[j2]])
                ts(tv, tv, egb[:, s_ * 16 + 8 + h:s_ * 16 + 9 + h], ALU.mult, reads=[r_tb[j2], r_egb], writes=[r_tb[j2]])
                stt(Sn[:, h, :], tv, kcol, Sn[:, h, :], ALU.mult, ALU.add, reads=[r_tb[j2], r_qkT, r_Sn[i2 * 8 + h]], writes=[r_Sn[i2 * 8 + h]])
                mm(po[:, h * NS + s_:h * NS + s_ + 1], Sn[:, h, :], qkT[:, h, s_:s_ + 1], reads=[r_Sn[i2 * 8 + h], r_qkT], writes=[rpo])
            stage1(0)
            for h in range(8):
                if h + 1 < 8:
                    stage1(h + 1)
                stage2(h)
            P.dma(gdns_d[l, s_].rearrange("h d e -> d h e"), Sn, reads=r_Sn[i2 * 8:i2 * 8 + 8], is_output=True)
        groupnorm_cols(po[:, 0:8 * NS], rpo, 8 * NS, c_o128[:, :], RMS_EPS, False,
                       mixT[:, 4:12, 0:NS], r_mix[4:12], gate, [r_gate], gn_tmps(8 * NS))

    def s5_sample(l):
        phase()
        N = NS
        su32, r_su32 = aal(4 * N, nreg=4); su32 = su32.rearrange("p (k n) -> p k n", n=N)
        sub, r_sub = aal(4 * N, BF16, nreg=4); sub = sub.rearrange("p (k n) -> p k n", n=N)
        for b in range(2):
            wv, wr = wload2("w_in_su", (l, b), 16, 256)
            for half in range(2):
                kc = b * 2 + half
                bk, br = proj_fm(wv, wr, half * 128, N, hT, r_h)
                cp(su32[:, kc, :], bk[:, 0:N], [br], [r_su32[kc]])
                cp(sub[:, kc, :], bk[:, 0:N], [br], [r_sub[kc]], eng="dve")
        CTr, r_CTr = aal(2048, BF16); CTi, r_CTi = aal(2048, BF16)
        CTr = CTr.rearrange("p (k s) -> p k s", k=16); CTi = CTi.rearrange("p (k s) -> p k s", k=16)
        P.dma(CTr, cpr_d[l], writes=[r_CTr], queue="pool"); P.dma(CTi, cpi_d[l], writes=[r_CTi], queue="pool")
        bu, r_bu = aal(2 * 2048)
        mark = _ar["off"]
        BT, r_BT = aal(2 * 2048, BF16); BT = BT.rearrange("p (c k s) -> p c k s", c=2, k=16)
        P.dma(BT.rearrange("p c k s -> p c (k s)"), bt_scr[l].rearrange("c p k s -> p c (k s)"), reads=[r_btscr], writes=[r_BT])
        for c_ in range(2):
            for g4 in range(4):
                bk, _, br = pb()
                for kk in range(4):
                    k = g4 * 4 + kk
                    mm(bk[0:NS, kk * 128:(kk + 1) * 128], sub[:, k // 4, :], BT[:, c_, k, :], reads=[r_sub[k // 4], r_BT], writes=[br])
                cp(bu[0:NS, c_ * 2048 + g4 * 512:c_ * 2048 + (g4 + 1) * 512], bk[0:NS, :], [br], [r_bu], eng=alt())
        phase(mark)
        h0, r_h0 = aal(2 * 2048)
        P.dma(h0[0:NS, 0:2048], st_sre_d[l], writes=[r_h0]); P.dma(h0[0:NS, 2048:4096], st_sim_d[l], writes=[r_h0])
        bk, _, br = pb()
        for c_ in range(2):
            tr(bk[0:16, c_ * 128:(c_ + 1) * 128], s5pw[:, l, 0, c_, :], c_ident[:, :], [r_s5pw, r_const], [br])
        lt, r_lt = aal(256)
        cp(lt[0:16, :], bk[0:16, 0:256], [br], [r_lt])
        for c_ in range(2):
            P.dma(scr_l[c_].rearrange("(k p) -> k p", p=128), lt[0:16, c_ * 128:(c_ + 1) * 128], reads=[r_lt], writes=[r_scrl])
        lrow, r_lrow = aal(2 * 2048)
        for c_ in range(2):
            P.dma(lrow[0:NS, c_ * 2048:(c_ + 1) * 2048], scr_l[c_].partition_broadcast(NS), reads=[r_scrl], writes=[r_lrow])
        tmp, r_tmp = aal(2048)
        lr = lrow[0:NS, 0:2048]; li = lrow[0:NS, 2048:4096]; h0r = h0[0:NS, 0:2048]; h0i = h0[0:NS, 2048:4096]
        hr = bu[0:NS, 0:2048]; hi = bu[0:NS, 2048:4096]; tm = tmp[0:NS, :]
        tt(tm, lr, h0r, ALU.mult, [r_lrow, r_h0], [r_tmp]); tt(hr, hr, tm, ALU.add, [r_bu, r_tmp], [r_bu])
        tt(tm, li, h0i, ALU.mult, [r_lrow, r_h0], [r_tmp]); tt(hr, hr, tm, ALU.subtract, [r_bu, r_tmp], [r_bu])
        tt(tm, lr, h0i, ALU.mult, [r_lrow, r_h0], [r_tmp]); tt(hi, hi, tm, ALU.add, [r_bu, r_tmp], [r_bu])
        tt(tm, li, h0r, ALU.mult, [r_lrow, r_h0], [r_tmp]); tt(hi, hi, tm, ALU.add, [r_bu, r_tmp], [r_bu])
        P.dma(ssms_re_d[l], hr, reads=[r_bu], is_output=True); P.dma(ssms_im_d[l], hi, reads=[r_bu], is_output=True)
        hb, r_hb = aal(2 * 16 * NS, BF16); hb = hb.rearrange("p (c k s) -> p c k s", c=2, k=16)
        for c_ in range(2):
            bk, _, br = pb()
            for k in range(16):
                tr(bk[:, k * NS:(k + 1) * NS], bu[0:NS, c_ * 2048 + k * 128:c_ * 2048 + (k + 1) * 128], c_ident[0:NS, 0:NS], [r_bu, r_const], [br])
            src = bk[:, 0:16 * NS].rearrange("p (k s) -> p k s", s=NS)
            if c_ == 0:
                cp(hb[:, 0, :, :], src, [br], [r_hb])
            else:
                amul(hb[:, 1, :, :], src, -1.0, reads=[br], writes=[r_hb])
        yg, r_yg = aal(4 * N, nreg=4); yg = yg.rearrange("p (k n) -> p k n", n=N)
        ygb, r_ygb = aal(4 * N, BF16, nreg=4); ygb = ygb.rearrange("p (k n) -> p k n", n=N)
        ytmp, r_ytmp = aal(N)
        for kc in range(4):
            by, _, rby = pb_res(7)
            for k in range(kc * 4, kc * 4 + 4):
                mm(by[:, 0:N], CTr[:, k, :], hb[:, 0, k, :], start=(k % 4 == 0), stop=False, reads=[r_CTr, r_hb], writes=[rby])
                mm(by[:, 0:N], CTi[:, k, :], hb[:, 1, k, :], start=False, stop=(k % 4 == 3), reads=[r_CTi, r_hb], writes=[rby])
            stt(ytmp, su32[:, kc, :], c_ssmd[:, l, kc:kc + 1], by[:, 0:N], ALU.mult, ALU.add, reads=[r_su32[kc], r_const, rby], writes=[r_ytmp])
            act(yg[:, kc, :], ytmp, AF.Gelu, reads=[r_ytmp], writes=[r_yg[kc]])
            cp(ygb[:, kc, :], yg[:, kc, :], [r_yg[kc]], [r_ygb[kc]], eng="dve")
        s5_glu(l, N, yg, r_yg, ygb, r_ygb)

    def sample_pass():
        N = NS
        phase()
        mt, r_mt = aal(2 * 96 * 17)
        P.dma(mt, mod_scr.ap(), reads=[r_modscr], writes=[r_mt])
        MT["t"] = mt.rearrange("p (l j n) -> p l j n", l=2, j=96)
        MT["r"] = r_mt
        _ar["base"] = _ar["off"]
        for kc in range(16):
            P.dma(xT[:, kc, 0:N], xsT_d[:, kc, :], writes=[r_x[kc]])
        for kc in range(16):
            modulate(kc, N, (0, 0, 1), False)
        for l in range(2):
            ret_sample(l)
            gdn_sample(l)
            s5_sample(l)
            attn_out_ln(l, N, False)
            ffn(l, N, False, (1, 0, 1) if l == 0 else None)
        for kc in range(16):
            P.dma(ysT_d[:, kc, :], xT[:, kc, 0:N], reads=[r_x[kc]], is_output=True)
        _ar["base"] = 0

    if prompt:
        r_y = P.region("yout")
        for it in range(NT):
            t0 = it * T
            for kc in range(16):
                P.dma(xT[:, kc, :], xT_d[:, kc, t0:t0 + T], writes=[r_x[kc]])
            phase()
            for kc in range(16):
                modulate(kc, T, (0, 0, 1), True)
            if dbg and "h" in dbg and it == 0:
                phase()
                mf, r_mf = aal(16 * T)
                cp(mf.rearrange("p (k n) -> p k n", n=T), hT[:, :, :], r_h, [r_mf], eng="dve")
                P.dma(dbg_d["h"].ap(), mf.rearrange("p (k n) -> p k n", n=T), reads=[r_mf], is_output=True)
            if stop_at == "mod0":
                return P.finish()
            for l in range(2):
                ret_prompt(l, it)
                if stop_at == "ret":
                    return P.finish()
                try:
                    gdn_prompt(l, it)
                except _Stop:
                    return P.finish()
                if stop_at == "gdn":
                    return P.finish()
                s5_prompt(l, it)
                if stop_at == "s5":
                    return P.finish()
                if dbg and "mix" in dbg and it == 0 and l == 0:
                    phase()
                    mf, r_mf = aal(16 * T)
                    cp(mf.rearrange("p (k n) -> p k n", n=T), mixT[:, :, :], r_mix, [r_mf], eng="dve")
                    P.dma(dbg_d["mix"].ap(), mf.rearrange("p (k n) -> p k n", n=T), reads=[r_mf], is_output=True)
                attn_out_ln(l, T, True)
                if dbg and "x1" in dbg and it == 0 and l == 0:
                    P.dma(dbg_d["x1"].ap(), xT[:, :, :], reads=r_x, is_output=True)
                if stop_at == "ln1":
                    return P.finish()
                ffn(l, T, True, (1, 0, 1) if l == 0 else None)
                if dbg and "x2" in dbg and it == 0 and l == 0:
                    P.dma(dbg_d["x2"].ap(), xT[:, :, :], reads=r_x, is_output=True)
                if stop_at == "ffn":
                    return P.finish()
            for kc in range(16):
                P.dma(yT_d[:, kc, t0:t0 + T], xT[:, kc, :], reads=[r_x[kc]], is_output=True)

    if sample:
        sample_pass()
    return P.finish()


def kernel(**inputs):
    inp = {k: np.ascontiguousarray(np.asarray(v)) for k, v in inputs.items()}
    shared = host_shared(inp)
    consts = host_constants()
    in_maps = []
    for core in range(8):
        m = dict(shared)
        m.update(host_core(inp, core))
        for k, v in consts.items():
            if isinstance(v, np.ndarray):
                m["c_" + k] = v
        in_maps.append(m)
    nc = build()
    res = run_bass_kernel_spmd(nc, in_maps, core_ids=list(range(8)))
    r = res.results
    f32 = np.float32
    y_prompt = np.stack([r[b]["yT"].transpose(2, 1, 0).reshape(SEQ, D) for b in range(4)]).astype(f32)
    y_sample = np.concatenate([r[c]["ysT"].transpose(2, 1, 0).reshape(NS, 1, D) for c in range(8)], 0).astype(f32)
    ret_p = np.stack([r[b]["ret_p"] for b in range(4)], 1).astype(f32)
    gdn_p = np.stack([r[b]["gdn_p"] for b in range(4)], 1).astype(f32)
    conv_p = np.stack([r[b]["conv_p"].transpose(0, 3, 2, 1).reshape(2, 3, 3072) for b in range(4)], 1).astype(f32)
    ssm_re_p = np.stack([r[b]["ssm_p"][..., 0].transpose(0, 2, 1).reshape(2, 32, 64) for b in range(4)], 1).astype(f32)
    ssm_im_p = np.stack([r[b]["ssm_p"][..., 1].transpose(0, 2, 1).reshape(2, 32, 64) for b in range(4)], 1).astype(f32)
    ret_s = np.concatenate([r[c]["ret_s"] for c in range(8)], 1).astype(f32)
    gdn_s = np.concatenate([r[c]["gdn_s"] for c in range(8)], 1).astype(f32)
    conv_s = np.concatenate([r[c]["conv_s"] for c in range(8)], 1).astype(f32)
    ssm_re_s = np.concatenate([r[c]["ssm_s_re"].reshape(2, NS, 32, 64) for c in range(8)], 1).astype(f32)
    ssm_im_s = np.concatenate([r[c]["ssm_s_im"].reshape(2, NS, 32, 64) for c in range(8)], 1).astype(f32)
    return (y_prompt, y_sample, ret_p, gdn_p, conv_p, ssm_re_p, ssm_im_p, ret_s, gdn_s, conv_s, ssm_re_s, ssm_im_s)
```
